# Optimizing a Trainium2 kernel written in Bass

```python
import jax, jax.numpy as jnp
from jax import lax
import numpy as np


D_MODEL = 1024
BATCH = 8
SEQ = 2048
DEPTH = 1

D_FF = 2816
M_HEADS = 8
M_HEAD_DIM = 128
M_WIDTH = M_HEADS * M_HEAD_DIM
G_HEADS = 8
G_HEAD_DIM = 128
G_WIDTH = G_HEADS * G_HEAD_DIM
CONV_K = 5
CHUNK = 64
N_ADA = 3
EPS = 1e-6
FFN_RES = 0.5
PROJ_SIZES = (M_WIDTH, M_WIDTH, M_WIDTH, M_WIDTH, 4 * M_HEADS,
              G_WIDTH, G_WIDTH, G_WIDTH, G_WIDTH, 4 * G_HEADS,
              D_MODEL, D_MODEL)
PROJ_DIM = sum(PROJ_SIZES)

kernel_name = 'hybrid_mlstm_gdn_macaron_block'


def rms_norm(x, w):
    xf = x.astype(jnp.float32)
    y = xf * lax.rsqrt(jnp.mean(xf * xf, axis=-1, keepdims=True) + EPS)
    return y.astype(x.dtype) * w


def l2_norm(x):
    return x * lax.rsqrt(jnp.sum(x * x, axis=-1, keepdims=True) + EPS)


def modulate(h, shift, scale):
    return h * (1 + scale[:, None, :]) + shift[:, None, :]


def swiglu(h, w_in, w_out):
    gate, up = jnp.split(h @ w_in, 2, axis=-1)
    return (jax.nn.silu(gate) * up) @ w_out


def _chunk(t):
    b, s = t.shape[:2]
    t = t.reshape((b, s // CHUNK, CHUNK) + t.shape[2:])
    if t.ndim == 5:
        return t.transpose(1, 0, 3, 2, 4)
    return t.transpose(1, 0, 3, 2)


def _unchunk(t):
    nc, b, h, l, d = t.shape
    return t.transpose(1, 0, 3, 2, 4).reshape(b, nc * l, h, d)


def centred_dwconv(x, w):
    return lax.conv_general_dilated(
        x, w[:, None, :], window_strides=(1,),
        padding=[(CONV_K // 2, CONV_K // 2)],
        dimension_numbers=('NWC', 'WIO', 'NWC'),
        feature_group_count=x.shape[-1])


def mlstm_dir(q, k, v, i_pre, f_pre):
    b, s, h, dk = q.shape
    dv = v.shape[-1]
    qc, kc, vc = _chunk(q), _chunk(k), _chunk(v)
    ic = _chunk(i_pre)
    bc = jnp.cumsum(_chunk(jax.nn.log_sigmoid(f_pre)), axis=-1)
    causal = jnp.tril(jnp.ones((CHUNK, CHUNK), bool))

    def step(carry, xs):
        cmat, nvec, m = carry
        qj, kj, vj, ij, bj = xs
        dmat = jnp.where(causal, bj[..., :, None] - bj[..., None, :] + ij[..., None, :], -jnp.inf)
        inter = bj + m[..., None]
        m_t = jnp.maximum(inter, dmat.max(-1))
        p = jnp.exp(dmat - m_t[..., None])
        sc = jnp.einsum('bhtd,bhsd->bhts', qj, kj) * p
        w_inter = jnp.exp(inter - m_t)
        num = (jnp.einsum('bhts,bhsv->bhtv', sc, vj)
               + w_inter[..., None] * jnp.einsum('bhtk,bhkv->bhtv', qj, cmat))
        den = sc.sum(-1) + w_inter * jnp.einsum('bhtk,bhk->bht', qj, nvec)
        hj = num / jnp.maximum(jnp.abs(den), jnp.exp(-m_t))[..., None]
        g = bj[..., -1]
        a = g[..., None] - bj + ij
        m_new = jnp.maximum(g + m, a.max(-1))
        kw = kj * jnp.exp(a - m_new[..., None])[..., None]
        decay = jnp.exp(g + m - m_new)
        cmat = decay[..., None, None] * cmat + jnp.einsum('bhsk,bhsv->bhkv', kw, vj)
        nvec = decay[..., None] * nvec + kw.sum(axis=-2)
        return (cmat, nvec, m_new), hj

    init = (jnp.zeros((b, h, dk, dv), jnp.float32),
            jnp.zeros((b, h, dk), jnp.float32),
            jnp.zeros((b, h), jnp.float32))
    _, hs = lax.scan(step, init, (qc, kc, vc, ic, bc))
    return _unchunk(hs)


def gdn_dir(q, k, v, g, beta):
    b, s, h, dk = q.shape
    dv = v.shape[-1]
    qc, kc, vc = _chunk(q), _chunk(k), _chunk(v)
    bc = _chunk(beta)
    gam = jnp.cumsum(_chunk(g), axis=-1)
    diff = gam[..., :, None] - gam[..., None, :]
    row = jnp.arange(CHUNK)
    strict = row[:, None] > row[None, :]
    incl = row[:, None] >= row[None, :]
    kk = jnp.einsum('nbhtd,nbhsd->nbhts', kc, kc)
    a_mat = bc[..., :, None] * kk * jnp.exp(jnp.where(strict, diff, -jnp.inf))
    m_mat = a_mat + jnp.eye(CHUNK, dtype=a_mat.dtype)
    rhs = jnp.concatenate([bc[..., None] * vc, (bc * jnp.exp(gam))[..., None] * kc], axis=-1)
    sol = lax.linalg.triangular_solve(m_mat, rhs, left_side=True, lower=True, unit_diagonal=True)
    u, w = sol[..., :dv], sol[..., dv:]
    attn = jnp.einsum('nbhtd,nbhsd->nbhts', qc, kc) * jnp.exp(jnp.where(incl, diff, -jnp.inf))
    q_dec = qc * jnp.exp(gam)[..., None]
    g_last = gam[..., -1]
    k_dec = kc * jnp.exp(g_last[..., None] - gam)[..., None]

    def step(state, xs):
        uj, wj, aj, qj, kj, gj = xs
        v_new = uj - jnp.einsum('bhtk,bhkv->bhtv', wj, state)
        o = jnp.einsum('bhtk,bhkv->bhtv', qj, state) + jnp.einsum('bhts,bhsv->bhtv', aj, v_new)
        state = jnp.exp(gj)[..., None, None] * state + jnp.einsum('bhsk,bhsv->bhkv', kj, v_new)
        return state, o

    s0 = jnp.zeros((b, h, dk, dv), jnp.float32)
    _, os = lax.scan(step, s0, (u, w, attn, q_dec, k_dec, g_last))
    return _unchunk(os)


def token_mix(h, w_in, mlstm_gate_bias, gdn_a_log, gdn_dt_bias, gdn_conv_w,
              mlstm_out_norm, gdn_out_norm, w_branch_mlstm, w_branch_gdn, w_out):
    b, s, _ = h.shape
    f32 = jnp.float32
    split_idx = [int(i) for i in np.cumsum(PROJ_SIZES)[:-1]]
    (mq, mk, mv, mo, mg, gq, gk, gv, gz, gg, merge_m, merge_g) = jnp.split(h @ w_in, split_idx, axis=-1)

    def heads(t, nh):
        return t.reshape(b, s, nh, -1)

    def flip(t):
        return jnp.flip(t, axis=1)

    q = heads(mq, M_HEADS).astype(f32) * (M_HEAD_DIM ** -0.5)
    k = heads(mk, M_HEADS).astype(f32)
    v = heads(mv, M_HEADS).astype(f32)
    mgates = mg.astype(f32).reshape(b, s, 4, M_HEADS) + mlstm_gate_bias.astype(f32)
    h_m = (mlstm_dir(q, k, v, mgates[:, :, 0], mgates[:, :, 1])
           + flip(mlstm_dir(flip(q), flip(k), flip(v), flip(mgates[:, :, 2]), flip(mgates[:, :, 3]))))
    h_m = rms_norm(h_m, mlstm_out_norm.astype(f32)) * jax.nn.sigmoid(heads(mo, M_HEADS).astype(f32))
    h_m = h_m.reshape(b, s, M_WIDTH).astype(h.dtype)

    qkv = jax.nn.silu(centred_dwconv(jnp.concatenate([gq, gk, gv], axis=-1), gdn_conv_w))
    cq, ck, cv = jnp.split(qkv, 3, axis=-1)
    q = l2_norm(heads(cq, G_HEADS).astype(f32)) * (G_HEAD_DIM ** -0.5)
    k = l2_norm(heads(ck, G_HEADS).astype(f32))
    v = heads(cv, G_HEADS).astype(f32)
    ggates = gg.astype(f32).reshape(b, s, 4, G_HEADS)
    a_log = gdn_a_log.astype(f32)
    dt_b = gdn_dt_bias.astype(f32)
    g_f = -jnp.exp(a_log[0]) * jax.nn.softplus(ggates[:, :, 0] + dt_b[0])
    beta_f = jax.nn.sigmoid(ggates[:, :, 1])
    g_b = -jnp.exp(a_log[1]) * jax.nn.softplus(ggates[:, :, 2] + dt_b[1])
    beta_b = jax.nn.sigmoid(ggates[:, :, 3])
    o = (gdn_dir(q, k, v, g_f, beta_f)
         + flip(gdn_dir(flip(q), flip(k), flip(v), flip(g_b), flip(beta_b))))
    h_g = rms_norm(o, gdn_out_norm.astype(f32)) * jax.nn.silu(heads(gz, G_HEADS).astype(f32))
    h_g = h_g.reshape(b, s, G_WIDTH).astype(h.dtype)

    y = (jax.nn.sigmoid(merge_m) * (h_m @ w_branch_mlstm)
         + jax.nn.sigmoid(merge_g) * (h_g @ w_branch_gdn))
    return y @ w_out


def setup_inputs(seed: int = 0) -> dict:
    key = jax.random.key(seed)
    ks = jax.random.split(key, 24)
    f32 = jnp.float32

    def nrm(k, shape, scale):
        return jax.random.normal(k, shape, f32) * scale

    def gain(k, shape):
        return 1.0 + 0.02 * jax.random.normal(k, shape, f32)

    gate_offsets = jnp.array([0.0, 3.0, 0.0, 3.0], f32)[None, :, None]
    dt = jnp.exp(jax.random.uniform(ks[11], (DEPTH, 2, G_HEADS), f32,
                                    float(np.log(1e-3)), float(np.log(1e-1))))
    return {
        'x': nrm(ks[0], (BATCH, SEQ, D_MODEL), 1.0),
        'c': nrm(ks[1], (BATCH, D_MODEL), 1.0),
        'w_ada': nrm(ks[2], (DEPTH, D_MODEL, N_ADA * 3 * D_MODEL), 0.5 * D_MODEL ** -0.5),
        'b_ada': nrm(ks[3], (DEPTH, N_ADA * 3 * D_MODEL), 0.02),
        'norm_ffn1': gain(ks[4], (DEPTH, D_MODEL)),
        'w_ffn1_in': nrm(ks[5], (DEPTH, D_MODEL, 2 * D_FF), D_MODEL ** -0.5),
        'w_ffn1_out': nrm(ks[6], (DEPTH, D_FF, D_MODEL), D_FF ** -0.5),
        'norm_mix': gain(ks[7], (DEPTH, D_MODEL)),
        'w_in': nrm(ks[8], (DEPTH, D_MODEL, PROJ_DIM), D_MODEL ** -0.5),
        'mlstm_gate_bias': gate_offsets + nrm(ks[9], (DEPTH, 4, M_HEADS), 0.1),
        'gdn_a_log': jnp.log(jax.random.uniform(ks[10], (DEPTH, 2, G_HEADS), f32, 1.0, 16.0)),
        'gdn_dt_bias': dt + jnp.log(-jnp.expm1(-dt)),
        'gdn_conv_w': nrm(ks[12], (DEPTH, CONV_K, 3 * G_WIDTH), CONV_K ** -0.5),
        'mlstm_out_norm': gain(ks[13], (DEPTH, M_HEAD_DIM)),
        'gdn_out_norm': gain(ks[14], (DEPTH, G_HEAD_DIM)),
        'w_branch_mlstm': nrm(ks[15], (DEPTH, M_WIDTH, D_MODEL), M_WIDTH ** -0.5),
        'w_branch_gdn': nrm(ks[16], (DEPTH, G_WIDTH, D_MODEL), G_WIDTH ** -0.5),
        'w_out': nrm(ks[17], (DEPTH, D_MODEL, D_MODEL), D_MODEL ** -0.5),
        'norm_ffn2': gain(ks[18], (DEPTH, D_MODEL)),
        'w_ffn2_in': nrm(ks[19], (DEPTH, D_MODEL, 2 * D_FF), D_MODEL ** -0.5),
        'w_ffn2_out': nrm(ks[20], (DEPTH, D_FF, D_MODEL), D_FF ** -0.5),
        'norm_final': gain(ks[21], (D_MODEL,)),
    }


def reference(x, c, w_ada, b_ada, norm_ffn1, w_ffn1_in, w_ffn1_out, norm_mix, w_in,
              mlstm_gate_bias, gdn_a_log, gdn_dt_bias, gdn_conv_w, mlstm_out_norm,
              gdn_out_norm, w_branch_mlstm, w_branch_gdn, w_out, norm_ffn2, w_ffn2_in,
              w_ffn2_out, norm_final):
    b = x.shape[0]
    cs = jax.nn.silu(c)
    for l in range(DEPTH):
        mod = (cs @ w_ada[l] + b_ada[l]).reshape(b, N_ADA, 3, D_MODEL)
        h = modulate(rms_norm(x, norm_ffn1[l]), mod[:, 0, 0], mod[:, 0, 1])
        x = x + FFN_RES * mod[:, 0, 2][:, None, :] * swiglu(h, w_ffn1_in[l], w_ffn1_out[l])
        h = modulate(rms_norm(x, norm_mix[l]), mod[:, 1, 0], mod[:, 1, 1])
        x = x + mod[:, 1, 2][:, None, :] * token_mix(
            h, w_in[l], mlstm_gate_bias[l], gdn_a_log[l], gdn_dt_bias[l], gdn_conv_w[l],
            mlstm_out_norm[l], gdn_out_norm[l], w_branch_mlstm[l], w_branch_gdn[l], w_out[l])
        h = modulate(rms_norm(x, norm_ffn2[l]), mod[:, 2, 0], mod[:, 2, 1])
        x = x + FFN_RES * mod[:, 2, 2][:, None, :] * swiglu(h, w_ffn2_in[l], w_ffn2_out[l])
    return rms_norm(x, norm_final)
```

```python
import numpy as np
import ml_dtypes
from contextlib import ExitStack
import concourse.bass as bass
import concourse.mybir as mybir
from concourse.bass_utils import run_bass_kernel_spmd

F32 = mybir.dt.float32
BF16 = mybir.dt.bfloat16
AF = mybir.ActivationFunctionType
ALU = mybir.AluOpType

D = 1024
S = 2048
KC = 8
NTB = 4
DFF = 2816
NJ = 22
GJ = 11
EPS = 1e-6
NCORES = 8

ENGS = ("pe", "act", "dve", "pool", "sp")


class Tok:
    __slots__ = ("w", "r")

    def __init__(self):
        self.w = None
        self.r = {}


class Op:
    __slots__ = ("eng", "fn", "dma", "sig", "sigval", "idx", "deps", "dsem", "dval")


class Prog:
    def __init__(self):
        self.lists = {e: [] for e in ENGS}
        self.count = 0
        self.dma_ops = {e: [] for e in ENGS}

    def add(self, eng, fn, reads=(), writes=(), dma=False):
        o = Op()
        o.eng = eng
        o.fn = fn
        o.dma = dma
        o.sig = False
        o.sigval = 0
        o.idx = self.count
        self.count += 1
        deps = {}

        def dep(p, raw):
            if p is None:
                return
            if not p.dma and not dma and p.eng == eng and eng == "pe":
                return
            deps[p.idx] = p

        for t in reads:
            dep(t.w, True)
        for t in writes:
            dep(t.w, False)
            for p in t.r.values():
                dep(p, False)
        for t in reads:
            key = ("d", o.idx) if dma else eng
            t.r[key] = o
        for t in writes:
            t.w = o
            t.r = {}
        if dma:
            lst = self.dma_ops[eng]
            o.dsem = None
            lst.append(o)
        o.deps = list(deps.values())
        self.lists[eng].append(o)
        return o

    def finalize(self, dma_pool_size):
        for e in ENGS:
            lst = self.dma_ops[e]
            K = dma_pool_size[e]
            for i, o in enumerate(lst):
                o.dsem = (e, i % K)
                o.dval = 16 * (i // K + 1)
                if i >= K:
                    o.deps.append(lst[i - K])
        for e in ENGS:
            for o in self.lists[e]:
                for p in o.deps:
                    if not p.dma:
                        p.sig = True
        for e in ENGS:
            n = 0
            for o in self.lists[e]:
                if o.sig and not o.dma:
                    n += 1
                    o.sigval = n

    def emit(self, ename, eng, engsem, dmasem):
        waited = {}
        for o in self.lists[ename]:
            need = {}
            for p in o.deps:
                if p.dma:
                    key = ("d",) + p.dsem
                    val = p.dval
                else:
                    key = ("e", p.eng)
                    val = p.sigval
                if need.get(key, 0) < val:
                    need[key] = val
            for key, val in need.items():
                if waited.get(key, 0) < val:
                    sem = dmasem[key[1]][key[2]] if key[0] == "d" else engsem[key[1]]
                    eng.wait_ge(sem, val)
                    waited[key] = val
            if o.fn is None:
                continue
            ins = o.fn(eng)
            if o.dma:
                ins.then_inc(dmasem[o.dsem[0]][o.dsem[1]], 16)
            elif o.sig:
                ins.then_inc(engsem[ename], 1)


def _host_consts():
    c = {}
    ident = np.eye(128, dtype=np.float32)
    c["ident"] = ident
    c["ones"] = np.ones((128, 128), np.float32)
    return c


class Builder:
    def __init__(self, debug=None):
        self.debug = debug
        self.nc = bass.Bass("TRN2", target_bir_lowering=False)
        self.P = Prog()
        self.es = ExitStack()
        self.dram = {}
        self.outs = {}

    def din(self, name, shape, dtype=F32):
        t = self.nc.dram_tensor(name, list(shape), dtype, kind="ExternalInput").ap()
        self.dram[name] = t
        return t

    def dout(self, name, shape, dtype=F32):
        t = self.nc.dram_tensor(name, list(shape), dtype, kind="ExternalOutput").ap()
        self.outs[name] = t
        return t

    def sb(self, name, shape, dtype=F32):
        h = self.es.enter_context(self.nc.sbuf_tensor(name, list(shape), dtype))
        return h

    def ps(self, name, shape, dtype=F32):
        h = self.es.enter_context(self.nc.psum_tensor(name, list(shape), dtype))
        return h

    def mm(self, out, lhsT, rhs, start, stop, reads, writes, skip=False):
        if skip:
            return self.P.add("pe", lambda e: e.matmul(out, lhsT, rhs, start=start, stop=stop,
                                                       skip_group_check=True), reads, writes)
        return self.P.add("pe", lambda e: e.matmul(out, lhsT, rhs, start=start, stop=stop),
                          reads, writes)

    def tr(self, out, in_, ident, reads, writes):
        return self.P.add("pe", lambda e: e.transpose(out, in_, ident), reads, writes)

    def act(self, out, in_, func, reads, writes, bias=0.0, scale=1.0, eng="act"):
        return self.P.add(eng, lambda e: e.activation(out, in_, func, bias=bias, scale=scale),
                          reads, writes)

    def tt(self, eng, out, in0, in1, op, reads, writes):
        return self.P.add(eng, lambda e: e.tensor_tensor(out, in0, in1, op), reads, writes)

    def stt(self, out, in0, scalar, in1, op0, op1, reads, writes, eng="dve"):
        return self.P.add(eng, lambda e: e.scalar_tensor_tensor(out, in0, scalar, in1, op0, op1),
                          reads, writes)

    def ts(self, eng, out, in0, s1, s2, op0, op1, reads, writes):
        if s2 is None:
            return self.P.add(eng, lambda e: e.tensor_scalar(out, in0, s1, None, op0), reads, writes)
        return self.P.add(eng, lambda e: e.tensor_scalar(out, in0, s1, s2, op0, op1), reads, writes)

    def cp(self, eng, out, in_, reads, writes):
        if eng == "act":
            return self.P.add(eng, lambda e: e.copy(out, in_), reads, writes)
        return self.P.add(eng, lambda e: e.tensor_copy(out, in_), reads, writes)

    def recip(self, out, in_, reads, writes):
        return self.P.add("dve", lambda e: e.reciprocal(out, in_), reads, writes)

    def memset(self, eng, ap, val, reads, writes):
        return self.P.add(eng, lambda e: e.memset(ap, val), reads, writes)

    def dma(self, out, in_, reads, writes, eng="sp"):
        return self.P.add(eng, lambda e: e.dma_start(out, in_), reads, writes, dma=True)


BIG = 30000.0
NHC = 24
NCB = 12
MSC_XM = 10496


def inherit(new_toks, old_toks):
    ops = {}
    for t in old_toks:
        if t.w is not None:
            ops[t.w.idx] = t.w
        for p in t.r.values():
            ops[p.idx] = p
    for nt in new_toks:
        for i, p in ops.items():
            nt.r[("f", i)] = p


def build(debug=None):
    B = Builder(debug)
    nc = B.nc
    P = B.P

    xT_d = B.din("xT", [128, KC, S])
    c_d = B.din("ccol", [128, KC])
    wada_d = B.din("wada", [18, 128, KC * 512])
    bada_d = B.din("bada", [1, 9216])
    normw_d = B.din("normw", [128, 4 * KC])
    wf1i_d = B.din("wf1i", [NJ, 128, 2048])
    wf1o_d = B.din("wf1o", [16, 128, GJ * 128])
    wf2i_d = B.din("wf2i", [NJ, 128, 2048])
    wf2o_d = B.din("wf2o", [16, 128, GJ * 128])
    ident_d = B.din("ident", [128, 128])
    win_d = B.din("winu", [52, 128, 2048])
    wg_d = B.din("wgates", [128, 512])
    gb_d = B.din("gbias", [128, 8])
    cw_d = B.din("convw", [128, 120])
    on_d = B.din("onorm", [128, 2])
    cm_d = B.din("cmask", [128, 512])
    lm_d = B.din("clm", [128, 256])
    gm_d = B.din("gmask", [128, 14 * 128], BF16)
    out_d = B.dout("outT", [128, KC, S])
    xsp_d = nc.dram_tensor("xspill", [128, KC, S], F32, kind="Internal").ap()
    bc_d = nc.dram_tensor("bcd", [6, 128, 128], F32, kind="Internal").ap()

    xT = B.sb("xT_sb", [128, KC, S])
    hT = B.sb("hT_sb", [128, KC, S], BF16)
    arena = B.sb("arena", [128, 11264])
    NST = 2
    NBF = 3
    wst = B.sb("wst", [128, NST, 2048])
    wbf = B.sb("wbf", [128, NBF, 2048], BF16)
    modrow = B.sb("modrow", [1, 2, 512])
    brow = B.sb("brow", [1, 2, 512])
    ccol = B.sb("ccol_sb", [128, KC])
    cs = B.sb("cs_sb", [128, KC])
    modT = B.sb("modT", [128, 72])
    normw = B.sb("normw_sb", [128, 4 * KC])
    acol = B.sb("acol", [128, 3 * KC])
    gcol = B.sb("gcol", [128, 3 * KC])
    ident = B.sb("ident_sb", [128, 128])
    identb = B.sb("identb_sb", [128, 128], BF16)
    identb4 = B.sb("identb4", [128, 4, 128], BF16)
    onesb = B.sb("onesb", [128, 128], BF16)
    one11 = B.sb("one11", [1, 2])
    epsc = B.sb("epsc", [128, 1])
    tmpA = B.sb("tmpA", [128, 2, 512])
    tmpB = B.sb("tmpB", [128, 2, 512])
    rstd = B.sb("rstd", [128, 2, 512])
    nrmA = B.sb("nrmA", [128, 2, 512])
    sqb = B.sb("sqb", [128, 2, 512], BF16)
    wgb = B.sb("wgb", [128, 512], BF16)
    gb = B.sb("gb_sb", [128, 8])
    nega = B.sb("nega", [128, 2])
    cw = B.sb("cw_sb", [128, 120])
    onorm = B.sb("onorm_sb", [128, 2])
    cmask = B.sb("cmask_sb", [128, 4, 128])
    clm = B.sb("clm_sb", [128, 2, 128])
    eglt = B.sb("eglt", [128, 2, 4])
    gmaskb = B.sb("gmaskb", [128, 14, 128], BF16)

    banks = [B.ps("bank%d" % i, [128, 512]) for i in range(8)]
    bk = [Tok() for _ in range(8)]

    t_x = [[Tok() for _ in range(NTB)] for _ in range(KC)]
    t_h = [Tok() for _ in range(NTB)]
    t_const = Tok()
    t_mod = Tok()
    t_fin = [Tok(), Tok()]
    t_sqs = [Tok(), Tok()]
    t_rs = [Tok(), Tok()]
    t_sil = [Tok(), Tok()]
    out_toks = []

    B.dma(ident[:], ident_d, [], [t_const])
    B.dma(ccol[:], c_d, [], [t_const])
    B.dma(normw[:], normw_d, [], [t_const])
    P.add("pool", lambda e: e.memset(onesb[:], 1.0), [], [t_const])
    P.add("pool", lambda e: e.memset(one11[:], 1.0), [], [t_const])
    P.add("pool", lambda e: e.memset(epsc[:], EPS), [], [t_const])
    B.cp("pool", identb[:], ident[:], [t_const], [t_const])
    t_c2 = Tok()
    B.dma(gb[:], gb_d, [], [t_c2], eng="act")
    B.dma(cw[:], cw_d, [], [t_c2], eng="act")
    B.dma(onorm[:], on_d, [], [t_c2], eng="act")
    B.dma(cmask[:].rearrange("p a b -> p (a b)"), cm_d, [], [t_c2], eng="act")
    B.dma(clm[:].rearrange("p a b -> p (a b)"), lm_d, [], [t_c2], eng="act")
    B.dma(gmaskb[:].rearrange("p a b -> p (a b)"), gm_d, [], [t_c2], eng="act")
    for r in range(4):
        B.cp("pool", identb4[:, r, :], ident[:], [t_const], [t_c2])
    B.act(nega[:], gb[:, 4:6], AF.Exp, [t_c2], [t_c2])
    B.ts("dve", nega[:], nega[:], -1.0, None, ALU.mult, None, [t_c2], [t_c2])

    for kc in range(KC):
        B.dma(xT[:, kc, :], xT_d[:, kc, :], [], t_x[kc], eng="act" if kc % 2 else "sp")

    t_cs = Tok()
    B.act(cs[:], ccol[:], AF.Silu, [t_const], [t_cs])
    t_ada = [Tok(), Tok()]
    t_brow = [Tok(), Tok()]
    t_modrow = [Tok(), Tok()]
    adabuf = arena[:, 0:2 * KC * 512].rearrange("p (s n) -> p s n", s=2)
    for blk in range(18):
        sl = blk % 2
        B.dma(adabuf[:, sl, :], wada_d[blk], [], [t_ada[sl]], eng="sp")
        B.dma(brow[0:1, sl, :], bada_d[0:1, blk * 512:(blk + 1) * 512], [], [t_brow[sl]], eng="act")
        b = banks[sl]
        for kc in range(KC):
            B.mm(b[0:1, :], cs[:, kc:kc + 1], adabuf[:, sl, kc * 512:(kc + 1) * 512],
                 kc == 0, False, [t_cs, t_ada[sl]], [bk[sl]])
        B.mm(b[0:1, :], one11[0:1, 0:1], brow[0:1, sl, :], False, True,
             [t_const, t_brow[sl]], [bk[sl]])
        B.cp("act", modrow[0:1, sl, :], b[0:1, :], [bk[sl]], [t_modrow[sl]])
        for q in range(4):
            j = blk * 4 + q
            B.mm(banks[2][:, j:j + 1], modrow[0:1, sl, q * 128:(q + 1) * 128], one11[0:1, 0:1],
                 True, True, [t_modrow[sl], t_const], [bk[2]])
    B.cp("dve", modT[:], banks[2][:, 0:72], [bk[2]], [t_mod])
    for a in range(3):
        B.stt(acol[:, a * KC:(a + 1) * KC], modT[:, (a * 3 + 1) * KC:(a * 3 + 2) * KC], 1.0,
              normw[:, a * KC:(a + 1) * KC], ALU.add, ALU.mult, [t_mod, t_const], [t_mod])
        B.ts("dve", gcol[:, a * KC:(a + 1) * KC], modT[:, (a * 3 + 2) * KC:(a * 3 + 3) * KC],
             0.5 if a != 1 else 1.0, None, ALU.mult, None, [t_mod], [t_mod])
    arena_toks = list(t_ada)

    st_tok = [Tok() for _ in range(NST)]
    bf_tok = [Tok() for _ in range(NBF)]
    wq = []
    wstate = {"loaded": 0, "used": 0}
    PF = 3

    def w_issue():
        i = wstate["loaded"]
        src, n = wq[i]
        s_ = i % NST
        b_ = i % NBF
        B.dma(wst[:, s_, 0:n], src, [], [st_tok[s_]], eng="sp")
        B.cp("pool", wbf[:, b_, 0:n], wst[:, s_, 0:n], [st_tok[s_]], [bf_tok[b_]])
        wstate["loaded"] += 1

    def w_next():
        i = wstate["used"]
        while wstate["loaded"] < min(len(wq), i + PF):
            w_issue()
        wstate["used"] += 1
        return wbf[:, i % NBF, :], bf_tok[i % NBF]

    def declare_ffn(wi, wo):
        for g in range(2):
            for jj in range(GJ):
                wq.append((wi[g * GJ + jj], 2048))
            for m in range(KC):
                wq.append((wo[g * KC + m], GJ * 128))

    declare_ffn(wf1i_d, wf1o_d)
    if debug not in ("nomix",):
        wq.append((wg_d, 512))
        for h in range(8):
            wq.append((win_d[h], 2048))
            wq.append((win_d[8 + h], 2048))
        if debug != "mlstm_only":
            pass
        for m in range(8):
            wq.append((win_d[16 + m], 2048))
        for mm_ in range(4):
            wq.append((win_d[24 + mm_], 2048))
        for h in range(8):
            wq.append((win_d[28 + h], 2048))
            wq.append((win_d[36 + h], 2048))
        for m in range(8):
            wq.append((win_d[44 + m], 2048))
        for mm_ in range(4):
            wq.append((win_d[24 + mm_], 2048))
    declare_ffn(wf2i_d, wf2o_d)

    nstate = {"i": 0}

    def norm_block(a, tb, final=False):
        i = nstate["i"]
        nstate["i"] += 1
        sl = i % 2
        bi = 6 + sl
        tsl = slice(tb * 512, (tb + 1) * 512)
        for kc in range(KC):
            B.act(sqb[:, kc % 2, :], xT[:, kc, tsl], AF.Square, [t_x[kc][tb]], [t_sqs[kc % 2]])
            B.mm(banks[bi][:], onesb[:], sqb[:, kc % 2, :], kc == 0, kc == KC - 1,
                 [t_sqs[kc % 2], t_const], [bk[bi]])
        t_r = t_rs[sl]
        B.act(nrmA[:, sl, :], banks[bi][:], AF.Ln, [bk[bi], t_const], [t_r], bias=epsc[:, 0:1], scale=1.0 / D)
        B.act(rstd[:, sl, :], nrmA[:, sl, :], AF.Exp, [t_r], [t_r], scale=-0.5)
        for kc in range(KC):
            if final:
                B.stt(tmpB[:, kc % 2, :], xT[:, kc, tsl], normw[:, 3 * KC + kc:3 * KC + kc + 1],
                      rstd[:, sl, :], ALU.mult, ALU.mult, [t_x[kc][tb], t_r, t_const],
                      [t_fin[kc % 2]])
                t_o = Tok()
                out_toks.append(t_o)
                B.dma(out_d[:, kc, tsl], tmpB[:, kc % 2, :], [t_fin[kc % 2]], [t_o],
                      eng="sp")
            else:
                B.stt(tmpB[:, kc % 2, :], xT[:, kc, tsl], acol[:, a * KC + kc:a * KC + kc + 1],
                      rstd[:, sl, :], ALU.mult, ALU.mult, [t_x[kc][tb], t_r, t_mod],
                      [t_fin[kc % 2]])
                B.act(hT[:, kc, tsl], tmpB[:, kc % 2, :], AF.Identity, [t_fin[kc % 2], t_mod],
                      [t_h[tb]], bias=modT[:, a * 3 * KC + kc:a * 3 * KC + kc + 1], scale=1.0)

    def ffn(a):
        nonlocal arena_toks
        for tb in range(NTB):
            norm_block(a, tb)
        actT = arena[:].bitcast(BF16)
        t_act = [[Tok() for _ in range(NTB)] for _ in range(GJ)]
        flat = [t for row in t_act for t in row]
        inherit(flat, arena_toks)
        pi = 0
        for g in range(2):
            for jj in range(GJ):
                w, wt = w_next()
                for tb in range(NTB):
                    tsl = slice(tb * 512, (tb + 1) * 512)
                    bg = (pi % 2) * 2
                    bu = bg + 1
                    pi += 1
                    for kc in range(KC):
                        B.mm(banks[bg][:], w[:, kc * 128:(kc + 1) * 128], hT[:, kc, tsl],
                             kc == 0, kc == KC - 1, [wt, t_h[tb]], [bk[bg]])
                    for kc in range(KC):
                        B.mm(banks[bu][:], w[:, 1024 + kc * 128:1024 + (kc + 1) * 128],
                             hT[:, kc, tsl], kc == 0, kc == KC - 1, [wt, t_h[tb]], [bk[bu]])
                    sl = pi % 2
                    B.act(tmpA[:, sl, :], banks[bg][:], AF.Silu, [bk[bg]], [t_sil[sl]])
                    B.tt("dve", actT[:, jj * S + tb * 512: jj * S + (tb + 1) * 512],
                         banks[bu][:], tmpA[:, sl, :], ALU.mult, [bk[bu], t_sil[sl]],
                         [t_act[jj][tb]])
            for m in range(KC):
                w, wt = w_next()
                for tb in range(NTB):
                    tsl = slice(tb * 512, (tb + 1) * 512)
                    bo = 4 + (pi % 2)
                    pi += 1
                    for jj in range(GJ):
                        B.mm(banks[bo][:], w[:, jj * 128:(jj + 1) * 128],
                             actT[:, jj * S + tb * 512: jj * S + (tb + 1) * 512],
                             jj == 0, jj == GJ - 1, [wt, t_act[jj][tb]], [bk[bo]])
                    B.stt(xT[:, m, tsl], banks[bo][:], gcol[:, a * KC + m:a * KC + m + 1],
                          xT[:, m, tsl], ALU.mult, ALU.add, [bk[bo], t_mod, t_x[m][tb]],
                          [t_x[m][tb]])
        arena_toks = flat

    msc = xT[:].rearrange("p a b -> p (a b)")

    def bfv(lo, n_words):
        return msc[:, lo:lo + n_words].bitcast(BF16)

    qT = bfv(0, 1024)
    kT = bfv(1024, 1024)
    gateT = bfv(2048, 1024)
    v_tok = bfv(3072, 1024)
    k_tok = bfv(4096, 1024)
    hacc = msc[:, 5120:7168]
    cb = [bfv(7168 + i * 256, 256) for i in range(NCB)]
    Sst = msc[:, 10240:10368]
    Sb = bfv(10368, 64)
    vnb = bfv(10432, 64)
    xm = msc[:, MSC_XM:MSC_XM + 2 * 2048].rearrange("p (s n) -> p s n", s=2)
    yT = bfv(0, 8192)
    t_q, t_k, t_gate, t_vtok, t_ktok = Tok(), Tok(), Tok(), Tok(), Tok()
    t_hacc = [Tok() for _ in range(NTB)]
    t_cb = [Tok() for _ in range(NCB)]
    t_S, t_Sb, t_vnb, t_egl = Tok(), Tok(), Tok(), Tok()
    t_xm = [Tok(), Tok()]
    msc_toks = [t_q, t_k, t_gate, t_vtok, t_ktok] + t_hacc + t_cb + [t_S, t_Sb, t_vnb]
    t_y = [[Tok() for _ in range(NTB)] for _ in range(KC)]
    y_flat = [t for row in t_y for t in row]

    bcs = [msc[:, 13312:13824], msc[:, 13824:14336], msc[:, 14848:15360], msc[:, 15360:15872]]
    t_bc = [Tok() for _ in range(4)]
    t_bcd = [Tok() for _ in range(6)]
    msc_toks += t_bc

    def bc_load(slot, q, h, tb):
        r0 = h * 16 + tb * 4
        src = bc_d[q, r0:r0 + 4, :].rearrange("a b -> (a b)").partition_broadcast(128)
        B.dma(bcs[slot], src, [t_bcd[q]], [t_bc[slot]], eng="act")

    cb1 = [bfv(MSC_XM + i * 256, 256) for i in range(11)]
    t_cb1 = [Tok() for _ in range(11)]
    S1 = msc[:, 14592:14720]
    Sb1 = bfv(14720, 64)
    vnb1 = bfv(14784, 64)
    t_S1, t_Sb1, t_vnb1 = Tok(), Tok(), Tok()
    t_egl1 = Tok()
    msc_toks += t_cb1 + [t_S1, t_Sb1, t_vnb1]
    chainR = [
        dict(banks=[0, 1, 2, 3], cb=cb, tcb=t_cb, E=(tmpB[:, 0, :], t_fin[0]),
             XA=(tmpA[:, 0, :], t_sil[0]), XT=(rstd[:, 0, :], t_rs[0]), EG=(nrmA[:, 0, :], t_rs[0]),
             S=(Sst, t_S), Sb=(Sb, t_Sb), vnb=(vnb, t_vnb), egl=(eglt[:, 0, :], t_egl)),
        dict(banks=[4, 5, 6, 7], cb=cb1, tcb=t_cb1, E=(tmpB[:, 1, :], t_fin[1]),
             XA=(tmpA[:, 1, :], t_sil[1]), XT=(rstd[:, 1, :], t_rs[1]), EG=(nrmA[:, 1, :], t_rs[1]),
             S=(S1, t_S1), Sb=(Sb1, t_Sb1), vnb=(vnb1, t_vnb1), egl=(eglt[:, 1, :], t_egl1)),
    ]

    hXT = arena[:, 0:8192].bitcast(BF16)
    t_hx = [[Tok() for _ in range(NTB)] for _ in range(8)]
    hx_flat = [t for row in t_hx for t in row]
    hcap = [arena[:, 8192 + i * 128: 8192 + (i + 1) * 128] for i in range(NHC)]
    t_hc = [Tok() for _ in range(NHC)]
    ZER = 21

    tri = [cmask[:, 0, :], cmask[:, 1, :]]
    bigA = [cmask[:, 2, :], cmask[:, 3, :]]

    rot = {"b": 0, "d": 0}

    def rbank():
        rot["b"] ^= 1
        return rot["b"]

    def hc_transpose_to(dst, src, bslot):
        B.tr(banks[6][:, bslot * 128:(bslot + 1) * 128], hcap[src], ident[:],
             [t_hc[src], t_const], [bk[6]])
        B.cp("dve", hcap[dst], banks[6][:, bslot * 128:(bslot + 1) * 128], [bk[6]], [t_hc[dst]])

    def gate_project(wcol0, dst_tiles, bias_cols):
        for c in range(16):
            for kc in range(KC):
                B.mm(banks[5][:, c * 32:(c + 1) * 32], hT[:, kc, c * 128:(c + 1) * 128],
                     wgb[:, wcol0 + kc * 32: wcol0 + (kc + 1) * 32], kc == 0, kc == KC - 1,
                     [t_h[c // 4], t_c2], [bk[5]])
        src = banks[5][:].rearrange("p (c g h) -> p g h c", c=16, g=4, h=8)
        dst = tmpA[:, 0, :].rearrange("p (g h c) -> p g h c", g=4, h=8, c=16)
        for g in range(4):
            B.cp("dve", dst[:, g], src[:, g], [bk[5]], [t_sil[0]])
        for g in range(4):
            B.tr(banks[6][:, g * 128:(g + 1) * 128], tmpA[:, 0, g * 128:(g + 1) * 128], ident[:],
                 [t_sil[0], t_const], [bk[6]])
        for g in range(4):
            i = dst_tiles[g]
            if bias_cols[g] is None:
                B.cp("act", hcap[i], banks[6][:, g * 128:(g + 1) * 128], [bk[6]], [t_hc[i]])
            else:
                B.act(hcap[i], banks[6][:, g * 128:(g + 1) * 128], AF.Identity, [bk[6], t_c2],
                      [t_hc[i]], bias=gb[:, bias_cols[g]:bias_cols[g] + 1], scale=1.0)

    def scan(dst, src, op0):
        B.P.add("dve", lambda e: e.tensor_tensor_scan(hcap[dst], hcap[src], hcap[ZER], 0.0, op0,
                                                      ALU.add),
                [t_hc[src], t_hc[ZER]], [t_hc[dst]])

    def hts(dst, src, s1, s2, op0, op1, extra=()):
        B.ts("dve", hcap[dst], hcap[src], s1, s2, op0, op1, [t_hc[src]] + list(extra), [t_hc[dst]])

    def htt(dst, a_, b_, op, eng="dve"):
        B.tt(eng, hcap[dst], hcap[a_], hcap[b_], op, [t_hc[a_], t_hc[b_]], [t_hc[dst]])

    def hact(dst, src, func, bias=0.0, scale=1.0, extra=()):
        B.act(hcap[dst], hcap[src], func, [t_hc[src]] + list(extra), [t_hc[dst]], bias=bias,
              scale=scale)

    def cross_chunk_sum(dst_col_tile, src, col, d):
        B.mm(banks[6][:, 0:1], clm[:, d, :], hcap[src][:, col:col + 1], True, True,
             [t_c2, t_hc[src]], [bk[6]])
        B.cp("dve", hcap[dst_col_tile][:, 0:1], banks[6][:, 0:1], [bk[6]], [t_hc[dst_col_tile]])

    def cross_chunk_max(dst_col_tile, dcol, src, col, d, ROW, RA, RB):
        B.tr(banks[6][0:1, 128:256], hcap[src][:, col:col + 1], ident[:], [t_hc[src], t_const],
             [bk[6]])
        B.cp("dve", hcap[ROW][0:1, :], banks[6][0:1, 128:256], [bk[6]], [t_hc[ROW]])
        row3 = hcap[ROW][0:1, :].rearrange("p (h c) -> p h c", h=8)
        cur, oth = RA, RB
        c3 = hcap[cur][0:1, :].rearrange("p (h c) -> p h c", h=8)
        B.memset("dve", hcap[cur][0:1, :], 0.0, [], [t_hc[cur]])
        if d == 0:
            B.cp("dve", c3[:, :, 1:16], row3[:, :, 0:15], [t_hc[ROW]], [t_hc[cur]])
        else:
            B.cp("dve", c3[:, :, 0:15], row3[:, :, 1:16], [t_hc[ROW]], [t_hc[cur]])
        for sh in (1, 2, 4, 8):
            c3 = hcap[cur][0:1, :].rearrange("p (h c) -> p h c", h=8)
            o3 = hcap[oth][0:1, :].rearrange("p (h c) -> p h c", h=8)
            if d == 0:
                B.tt("dve", o3[:, :, sh:16], c3[:, :, sh:16], c3[:, :, 0:16 - sh], ALU.max,
                     [t_hc[cur]], [t_hc[oth]])
                B.cp("dve", o3[:, :, 0:sh], c3[:, :, 0:sh], [t_hc[cur]], [t_hc[oth]])
            else:
                B.tt("dve", o3[:, :, 0:16 - sh], c3[:, :, 0:16 - sh], c3[:, :, sh:16], ALU.max,
                     [t_hc[cur]], [t_hc[oth]])
                B.cp("dve", o3[:, :, 16 - sh:16], c3[:, :, 16 - sh:16], [t_hc[cur]], [t_hc[oth]])
            cur, oth = oth, cur
        B.mm(banks[6][:, 1:2], hcap[cur][0:1, :], one11[0:1, 0:1], True, True,
             [t_hc[cur], t_const], [bk[6]])
        B.cp("dve", hcap[dst_col_tile][:, dcol:dcol + 1], banks[6][:, 1:2], [bk[6]],
             [t_hc[dst_col_tile]])

    NEGM = [9, 14]
    FLOOR = [10, 15]
    UCOL = [11, 16]

    def softplus_acc(dst, src, sign, tA, tB, tC):
        hact(tA, src, AF.Abs)
        hact(tB, tA, AF.Exp, scale=-1.0)
        hts(tA, tB, 2.0, None, ALU.add, None)
        B.recip(hcap[tA], hcap[tA], [t_hc[tA]], [t_hc[tA]])
        htt(tB, tB, tA, ALU.mult)
        htt(tA, tB, tB, ALU.mult)
        hts(tC, tA, 1.0 / 13.0, 1.0 / 11.0, ALU.mult, ALU.add)
        for cst in (1.0 / 9.0, 1.0 / 7.0, 1.0 / 5.0, 1.0 / 3.0, 1.0):
            htt(tC, tC, tA, ALU.mult)
            hts(tC, tC, cst, None, ALU.add, None)
        htt(tC, tC, tB, ALU.mult)
        hts(tA, src, float(sign), 0.0, ALU.mult, ALU.max)
        B.stt(hcap[dst], hcap[tC], 2.0, hcap[tA], ALU.mult, ALU.add, [t_hc[tC], t_hc[tA]],
              [t_hc[dst]])

    def mlstm_gates():
        P.add("pool", lambda e: e.memset(hcap[ZER], 0.0), [], [t_hc[ZER]])
        gate_project(0, [0, 1, 2, 3], [0, 1, 2, 3])
        for d in range(2):
            Gi, Gf = 2 * d, 2 * d + 1
            T1, SPt, LCS, CSP, U, LMX, X, COLS = 4, 5, 6, 7, 8, 12, 13, 20
            softplus_acc(SPt, Gf, -1.0, T1, LCS, CSP)
            scan(LCS, SPt, ALU.add)
            cross_chunk_sum(COLS, LCS, 127, d)
            if d == 1:
                hts(X, LCS, -1.0, hcap[LCS][:, 127:128], ALU.mult, ALU.add)
                htt(LCS, X, SPt, ALU.add)
            hts(CSP, LCS, hcap[COLS][:, 0:1], None, ALU.add, None, extra=[t_hc[COLS]])
            htt(U, Gi, CSP, ALU.add)
            if d == 0:
                scan(LMX, U, ALU.max)
                ccol_ = 127
            else:
                hts(X, U, 0.0, None, ALU.max, None)
                cur, oth = X, LMX
                for sh in (1, 2, 4, 8, 16, 32, 64):
                    B.tt("dve", hcap[oth][:, 0:128 - sh], hcap[cur][:, 0:128 - sh],
                         hcap[cur][:, sh:128], ALU.max, [t_hc[cur]], [t_hc[oth]])
                    B.cp("dve", hcap[oth][:, 128 - sh:128], hcap[cur][:, 128 - sh:128],
                         [t_hc[cur]], [t_hc[oth]])
                    cur, oth = oth, cur
                if cur != LMX:
                    B.cp("dve", hcap[LMX], hcap[cur], [t_hc[cur]], [t_hc[LMX]])
                ccol_ = 0
            cross_chunk_max(COLS, 1, LMX, ccol_, d, 17, 18, 19)
            hts(X, LMX, hcap[COLS][:, 1:2], None, ALU.max, None, extra=[t_hc[COLS]])
            hts(NEGM[d], X, -1.0, None, ALU.mult, None)
            htt(T1, CSP, X, ALU.subtract)
            hact(FLOOR[d], T1, AF.Exp)
            hc_transpose_to(UCOL[d], U, 2)
            B.dma(bc_d[d], hcap[NEGM[d]], [t_hc[NEGM[d]]], [t_bcd[d]], eng="act")
            B.dma(bc_d[2 + d], hcap[FLOOR[d]], [t_hc[FLOOR[d]]], [t_bcd[2 + d]], eng="act")

    def proj_fm(wsl, wt, evac):
        for tb in range(NTB):
            tsl = slice(tb * 512, (tb + 1) * 512)
            bi = rbank()
            for kc in range(KC):
                B.mm(banks[bi][:], wsl[:, kc * 128:(kc + 1) * 128], hT[:, kc, tsl],
                     kc == 0, kc == KC - 1, [wt, t_h[tb]], [bk[bi]])
            evac(tb, tsl, bi)

    def out_norm(h, which):
        for tb in range(NTB):
            tsl = slice(tb * 512, (tb + 1) * 512)
            sl = tb % 2
            bi = 6 + sl
            B.act(sqb[:, sl, :], hacc[:, tsl], AF.Square, [t_hacc[tb]], [t_sqs[sl]])
            B.mm(banks[bi][:], onesb[:], sqb[:, sl, :], True, True, [t_sqs[sl], t_const], [bk[bi]])
            B.act(nrmA[:, sl, :], banks[bi][:], AF.Ln, [bk[bi], t_const], [t_rs[sl]], bias=epsc[:, 0:1],
                  scale=1.0 / 128)
            B.act(rstd[:, sl, :], nrmA[:, sl, :], AF.Exp, [t_rs[sl]], [t_rs[sl]], scale=-0.5)
            B.stt(tmpB[:, sl, :], hacc[:, tsl], onorm[:, which:which + 1], rstd[:, sl, :],
                  ALU.mult, ALU.mult, [t_hacc[tb], t_c2, t_rs[sl]], [t_fin[sl]])
            B.tt("dve", hXT[:, h * S + tb * 512: h * S + (tb + 1) * 512], tmpB[:, sl, :],
                 gateT[:, tsl], ALU.mult, [t_fin[sl], t_gate], [t_hx[h][tb]])

    def selector(bank_i, src_tile, h, tb):
        for r in range(4):
            hc_ = h * 16 + tb * 4 + r
            B.mm(banks[bank_i][:, r * 128:(r + 1) * 128],
                 ident[:, hc_:hc_ + 1].to_broadcast([128, 128]), hcap[src_tile], True, True,
                 [t_const, t_hc[src_tile]], [bk[bank_i]])

    pend_on = {"f": None}

    def mlstm_head(h):
        w, wt = w_next()
        proj_fm(w[:, 0:1024], wt, lambda tb, tsl, bi: B.act(
            qT[:, tsl], banks[bi][:], AF.Identity, [bk[bi]], [t_q], scale=128.0 ** -0.5))
        proj_fm(w[:, 1024:2048], wt, lambda tb, tsl, bi: B.cp(
            "dve", kT[:, tsl], banks[bi][:], [bk[bi]], [t_k]))
        if pend_on["f"] is not None:
            pend_on["f"]()
            pend_on["f"] = None
        w, wt = w_next()
        for tc in range(16):
            q4 = tc % 4
            for kc in range(KC):
                B.mm(banks[5][:, q4 * 128:(q4 + 1) * 128], hT[:, kc, tc * 128:(tc + 1) * 128],
                     w[:, kc * 128:(kc + 1) * 128], kc == 0, kc == KC - 1, [wt, t_h[tc // 4]],
                     [bk[5]])
            if q4 == 3:
                B.cp("act", v_tok[:, (tc - 3) * 128:(tc + 1) * 128], banks[5][:], [bk[5]],
                     [t_vtok])
        proj_fm(w[:, 1024:2048], wt, lambda tb, tsl, bi: B.act(
            gateT[:, tsl], banks[bi][:], AF.Sigmoid, [bk[bi]], [t_gate]))

        iters = [(0, tb) for tb in range(NTB)] + [(1, tb) for tb in range(NTB - 1, -1, -1)]
        bc_load(0, 0, h, 0)
        bc_load(1, 2, h, 0)
        bc_load(3, 2 + iters[1][0], h, iters[1][1])
        pend = {"g": None}

        def step_pending():
            if pend["g"] is not None:
                try:
                    next(pend["g"])
                except StopIteration:
                    pend["g"] = None

        def epilogue_gen(d, tb, tsl, bn_, bd_, FL, tFL):
            B.act(nrmA[:, 0, :], banks[bd_][:], AF.Abs, [bk[bd_]], [t_rs[0]])
            yield
            B.tt("dve", nrmA[:, 0, :], FL, nrmA[:, 0, :], ALU.max, [tFL, t_rs[0]], [t_rs[0]])
            yield
            B.act(nrmA[:, 0, :], nrmA[:, 0, :], AF.Ln, [t_rs[0]], [t_rs[0]])
            yield
            B.act(rstd[:, 0, :], nrmA[:, 0, :], AF.Exp, [t_rs[0]], [t_rs[0]], scale=-1.0)
            yield
            if d == 0:
                B.tt("dve", hacc[:, tsl], banks[bn_][:], rstd[:, 0, :], ALU.mult,
                     [bk[bn_], t_rs[0]], [t_hacc[tb]])
            else:
                B.tt("dve", nrmA[:, 0, :], banks[bn_][:], rstd[:, 0, :], ALU.mult,
                     [bk[bn_], t_rs[0]], [t_rs[0]])
                yield
                B.tt("pool", hacc[:, tsl], hacc[:, tsl], nrmA[:, 0, :], ALU.add,
                     [t_hacc[tb], t_rs[0]], [t_hacc[tb]])
            yield
        for it_, (d, tb) in enumerate(iters):
            if True:
                tsl = slice(tb * 512, (tb + 1) * 512)
                if it_ + 1 < len(iters):
                    nd, ntb = iters[it_ + 1]
                    bc_load(2 * ((it_ + 1) % 2), nd, h, ntb)
                bn_, bd_ = (3, 4) if it_ % 2 == 0 else (5, 6)
                NB, tNB = bcs[2 * (it_ % 2)], t_bc[2 * (it_ % 2)]
                FL, tFL = bcs[2 * (it_ % 2) + 1], t_bc[2 * (it_ % 2) + 1]
                scs = list(range(0, 4 * tb + 4)) if d == 0 else list(range(15, 4 * tb - 1, -1))
                tiles = []
                for sc in scs:
                    r = sc - 4 * tb
                    diag = 0 <= r <= 3
                    if d == 0:
                        c0, c1 = (r * 128 if diag else 0), 512
                    else:
                        c0, c1 = 0, ((r + 1) * 128 if diag else 512)
                    tiles.append((sc, r, diag, c0, c1))

                def emit_st(tile):
                    sc, r, diag, c0, c1 = tile
                    sbk = rbank()
                    B.mm(banks[sbk][:, c0:c1], kT[:, sc * 128:(sc + 1) * 128],
                         qT[:, tb * 512 + c0: tb * 512 + c1], True, True, [t_k, t_q], [bk[sbk]])
                    return sbk

                sbanks = {}

                def stage_ab(n_):
                    sc, r, diag, c0, c1 = tiles[n_]
                    sbanks[n_] = emit_st(tiles[n_])
                    hcs = h * 16 + sc
                    ucv = hcap[UCOL[d]][:, hcs:hcs + 1]
                    ds = n_ % 2
                    Dt = tmpA[:, ds, :]
                    if diag:
                        if d == 0:
                            t0, t1, r0, r1 = c0, c0 + 128, c0 + 128, c1
                        else:
                            t0, t1, r0, r1 = c1 - 128, c1, c0, c1 - 128
                        B.stt(tmpB[:, ds, 0:128], NB[:, t0:t1], ucv, tri[d], ALU.add, ALU.min,
                              [tNB, t_hc[UCOL[d]], t_c2], [t_fin[ds]])
                        B.act(Dt[:, t0:t1], tmpB[:, ds, 0:128], AF.Exp, [t_fin[ds]], [t_sil[ds]])
                        if r1 > r0:
                            B.act(Dt[:, r0:r1], NB[:, r0:r1], AF.Exp,
                                  [tNB, t_hc[UCOL[d]]], [t_sil[ds]], bias=ucv)
                    else:
                        B.act(Dt[:, c0:c1], NB[:, c0:c1], AF.Exp, [tNB, t_hc[UCOL[d]]],
                              [t_sil[ds]], bias=ucv)

                stage_ab(0)
                for n_, (sc, r, diag, c0, c1) in enumerate(tiles):
                    if n_ + 1 < len(tiles):
                        stage_ab(n_ + 1)
                    sbk = sbanks[n_]
                    ds = n_ % 2
                    Dt = tmpA[:, ds, :]
                    Pt = cb[ds]
                    B.tt("dve", Pt[:, c0:c1], banks[sbk][:, c0:c1], Dt[:, c0:c1], ALU.mult,
                         [bk[sbk], t_sil[ds]], [t_cb[ds]])
                    first = n_ == 0
                    last = n_ == len(tiles) - 1
                    B.mm(banks[bn_][:, c0:c1], v_tok[:, sc * 128:(sc + 1) * 128], Pt[:, c0:c1],
                         first, last, [t_vtok, t_cb[ds]], [bk[bn_]])
                    B.mm(banks[bd_][:, c0:c1], onesb[:], Pt[:, c0:c1], first, last,
                         [t_const, t_cb[ds]], [bk[bd_]])
                    if n_ >= 1:
                        step_pending()
                while pend["g"] is not None:
                    step_pending()
                if it_ >= 1 and it_ + 1 < len(iters):
                    nd, ntb = iters[it_ + 1]
                    bc_load(2 * ((it_ + 1) % 2) + 1, 2 + nd, h, ntb)
                pend["g"] = epilogue_gen(d, tb, tsl, bn_, bd_, FL, tFL)
        while pend["g"] is not None:
            step_pending()
        pend_on["f"] = lambda: out_norm(h, 0)

    def gt(d, i):
        return 8 + 6 * d + i

    def gdn_gates():
        P.add("pool", lambda e: e.memset(hcap[ZER], 0.0), [], [t_hc[ZER]])
        gate_project(256, [0, 1, 2, 3], [6, None, 7, None])
        for d in range(2):
            Ga, Gb = 2 * d, 2 * d + 1
            T1, SPt, L, X, BETA = 4, 5, 6, 7, 20
            GAM = gt(d, 0)
            softplus_acc(SPt, Ga, 1.0, T1, L, X)
            hts(X, SPt, nega[:, d:d + 1], None, ALU.mult, None, extra=[t_c2])
            if d == 0:
                scan(GAM, X, ALU.add)
                gl = 127
            else:
                scan(L, X, ALU.add)
                hts(T1, L, -1.0, hcap[L][:, 127:128], ALU.mult, ALU.add)
                htt(GAM, T1, X, ALU.add)
                gl = 0
            hact(BETA, Gb, AF.Sigmoid)
            B.dma(bc_d[4 + d], hcap[GAM], [t_hc[GAM]], [t_bcd[4 + d]], eng="act")
            hc_transpose_to(gt(d, 1), GAM, 0)
            hts(T1, BETA, -1.0, None, ALU.mult, None)
            hc_transpose_to(gt(d, 2), T1, 1)
            hc_transpose_to(gt(d, 3), BETA, 2)
            hact(SPt, GAM, AF.Exp)
            htt(T1, BETA, SPt, ALU.mult)
            hc_transpose_to(gt(d, 4), T1, 3)
            B.act(hcap[X], hcap[GAM], AF.Exp, [t_hc[GAM]], [t_hc[X]],
                  bias=hcap[GAM][:, gl:gl + 1], scale=-1.0)
            hc_transpose_to(gt(d, 5), X, 0)

    def conv_proj(wsl, wt, ch, kind):
        raw = hacc
        acc = msc[:, 7168:7168 + 2048]

        def tacc(tb):
            return [t_cb[2 * tb], t_cb[2 * tb + 1]]

        def proj(tb):
            tsl = slice(tb * 512, (tb + 1) * 512)
            bi = rbank()
            for kc in range(KC):
                B.mm(banks[bi][:], wsl[:, kc * 128:(kc + 1) * 128], hT[:, kc, tsl],
                     kc == 0, kc == KC - 1, [wt, t_h[tb]], [bk[bi]])
            B.cp("act", raw[:, tsl], banks[bi][:], [bk[bi]], [t_hacc[tb]])

        def conv(tb):
            lo, hi = tb * 512, (tb + 1) * 512
            B.ts("dve", acc[:, lo:hi], raw[:, lo:hi], cw[:, ch * 5 + 2: ch * 5 + 3], None,
                 ALU.mult, None, [t_hacc[tb], t_c2], tacc(tb))
            for j in (0, 1, 3, 4):
                dd = j - 2
                o0 = max(lo, -dd)
                o1 = min(hi, S - dd)
                rd = [t_hacc[tb], t_c2] + tacc(tb)
                if dd < 0 and tb > 0:
                    rd.append(t_hacc[tb - 1])
                if dd > 0 and tb < NTB - 1:
                    rd.append(t_hacc[tb + 1])
                B.stt(acc[:, o0:o1], raw[:, o0 + dd:o1 + dd], cw[:, ch * 5 + j: ch * 5 + j + 1],
                      acc[:, o0:o1], ALU.mult, ALU.add, rd, tacc(tb))

        def post(tb):
            tsl = slice(tb * 512, (tb + 1) * 512)
            sl = tb % 2
            if kind == "v":
                B.act(cb[8 + tb][:, :], acc[:, tsl], AF.Silu, tacc(tb), [t_cb[8 + tb]])
                return
            B.act(tmpA[:, sl, :], acc[:, tsl], AF.Silu, tacc(tb), [t_sil[sl]])
            B.act(sqb[:, sl, :], tmpA[:, sl, :], AF.Square, [t_sil[sl]], [t_sqs[sl]])
            bi = 6 + sl
            B.mm(banks[bi][:], onesb[:], sqb[:, sl, :], True, True, [t_sqs[sl], t_const], [bk[bi]])
            B.act(nrmA[:, sl, :], banks[bi][:], AF.Ln, [bk[bi], t_const], [t_rs[sl]],
                  bias=epsc[:, 0:1], scale=1.0)
            B.act(rstd[:, sl, :], nrmA[:, sl, :], AF.Exp, [t_rs[sl]], [t_rs[sl]], scale=-0.5)
            if kind == "q":
                B.stt(qT[:, tsl], tmpA[:, sl, :], 128.0 ** -0.5, rstd[:, sl, :], ALU.mult, ALU.mult,
                      [t_sil[sl], t_rs[sl]], [t_q])
            else:
                B.tt("dve", kT[:, tsl], tmpA[:, sl, :], rstd[:, sl, :], ALU.mult,
                     [t_sil[sl], t_rs[sl]], [t_k])

        proj(0)
        proj(1)
        conv(0)
        post(0)
        proj(2)
        conv(1)
        post(1)
        proj(3)
        conv(2)
        post(2)
        conv(3)
        post(3)

    def to_token_major(dst, t_dst, srcf, src_toks):
        pb = banks[5][:].bitcast(BF16)
        for tc in range(16):
            q4 = tc % 4
            B.tr(pb[:, q4 * 128:(q4 + 1) * 128], srcf(tc), identb[:], src_toks + [t_const], [bk[5]])
            if q4 == 3:
                B.cp("act", dst[:, (tc - 3) * 128:(tc + 1) * 128], pb[:, 0:512], [bk[5]], [t_dst])

    def gdn_head(h):
        w, wt = w_next()
        conv_proj(w[:, 0:1024], wt, h, "q")
        conv_proj(w[:, 1024:2048], wt, 8 + h, "k")
        to_token_major(k_tok, t_ktok, lambda tc: kT[:, tc * 128:(tc + 1) * 128], [t_k])
        w, wt = w_next()
        conv_proj(w[:, 0:1024], wt, 16 + h, "v")
        to_token_major(v_tok, t_vtok,
                       lambda tc: cb[8 + tc // 4][:, (tc % 4) * 128:(tc % 4 + 1) * 128],
                       t_cb[8:12])
        proj_fm(w[:, 1024:2048], wt, lambda tb, tsl, bi: B.act(
            gateT[:, tsl], banks[bi][:], AF.Silu, [bk[bi]], [t_gate]))

        def v4(ap):
            return ap.rearrange("p (r t) -> p r t", r=4)

        for tb in range(NTB):
            B.memset("pool", hacc[:, tb * 512:(tb + 1) * 512], 0.0, [], [t_hacc[tb]])

        def chain(d, R):
            GAM, GAMc, NEGBc, BETAc, BEGc, KDc = [gt(d, i) for i in range(6)]
            P0, P1, P2, P3 = R["banks"]
            cbs, tcs = R["cb"], R["tcb"]
            N0, Noff, Zt, Db, Tb, attnT, Ru, Rw, kd_, nWt, qdT = cbs[0:11]
            (tN0, tNoff, tZt, tDb, tTb, tattn, tRu, tRw, tkd, tnW, tqd) = tcs[0:11]
            E, tE = R["E"]
            XA, tXA = R["XA"]
            XT, tXT = R["XT"]
            EG, tEG = R["EG"]
            S_, tS_ = R["S"]
            Sb_, tSb_ = R["Sb"]
            vn_, tvn_ = R["vnb"]
            eg_, teg_ = R["egl"]
            B.memset("pool", S_, 0.0, [], [tS_])
            B.memset("pool", Sb_, 0.0, [], [tSb_])
            tbs = range(NTB) if d == 0 else range(NTB - 1, -1, -1)
            i4 = identb4[:].rearrange("p a b -> p (a b)")

            def mk(dd, lev):
                return gmaskb[:, dd * 7 + lev, :].unsqueeze(1).to_broadcast([128, 4, 128])

            tbl = list(tbs)
            cix = R["banks"][0] // 4
            bc_load(2 * cix, 4 + d, h, tbl[0])
            for k_, tb in enumerate(tbl):
                tsl = slice(tb * 512, (tb + 1) * 512)
                hc0 = h * 16 + tb * 4
                if k_ + 1 < len(tbl):
                    bc_load(2 * cix + (k_ + 1) % 2, 4 + d, h, tbl[k_ + 1])
                GB, tGB = bcs[2 * cix + k_ % 2], t_bc[2 * cix + k_ % 2]

                def colb(tile):
                    return hcap[tile][:, hc0:hc0 + 4].unsqueeze(2).to_broadcast([128, 4, 128])

                for r in range(4):
                    csl = slice(tb * 512 + r * 128, tb * 512 + (r + 1) * 128)
                    B.mm(banks[P1][:, r * 128:(r + 1) * 128], kT[:, csl], kT[:, csl], True, True,
                         [t_k], [bk[P1]])
                B.tt("dve", v4(E), v4(GB), colb(GAMc), ALU.subtract,
                     [tGB, t_hc[GAMc]], [tE])
                B.tt("dve", v4(XA), v4(E), bigA[d].unsqueeze(1).to_broadcast([128, 4, 128]),
                     ALU.max, [tE, t_c2], [tXA])
                B.act(XA, XA, AF.Exp, [tXA], [tXA], scale=-1.0)
                B.tt("pool", v4(XA), v4(XA), colb(NEGBc), ALU.mult, [tXA, t_hc[NEGBc]], [tXA])
                B.tt("dve", N0, banks[P1][:], XA, ALU.mult, [bk[P1], tXA], [tN0])
                yield
                for r in range(4):
                    csl = slice(tb * 512 + r * 128, tb * 512 + (r + 1) * 128)
                    B.mm(banks[P1][:, r * 128:(r + 1) * 128], kT[:, csl], qT[:, csl], True, True,
                         [t_k, t_q], [bk[P1]])
                B.tt("dve", v4(XT), v4(E), tri[d].unsqueeze(1).to_broadcast([128, 4, 128]),
                     ALU.min, [tE, t_c2], [tXT])
                B.act(XT, XT, AF.Exp, [tXT], [tXT])
                B.tt("dve", attnT, banks[P1][:], XT, ALU.mult, [bk[P1], tXT], [tattn])
                B.act(EG, GB, AF.Exp, [tGB], [tEG])
                B.tt("pool", qdT, qT[:, tsl], EG, ALU.mult, [t_q, tEG], [tqd])
                gl = 127 if d == 0 else 0
                B.act(eg_, v4(GB)[:, :, gl], AF.Exp, [tGB], [teg_])
                yield
                pb1 = banks[P1][:].bitcast(BF16)
                for r in range(4):
                    B.tr(pb1[:, r * 128:(r + 1) * 128], N0[:, r * 128:(r + 1) * 128], identb[:],
                         [tN0, t_const], [bk[P1]])
                B.cp("act", Zt, pb1[:, 0:512], [bk[P1]], [tZt])
                B.tt("pool", v4(Zt), v4(Zt), mk(1 - d, 0), ALU.mult, [tZt, t_c2], [tZt])
                B.tt("dve", Tb, Zt, i4, ALU.add, [tZt, t_c2], [tTb])
                B.mm(banks[P2][:], identb[:], Tb, True, True, [t_const, tTb], [bk[P2]], skip=True)
                B.tt("pool", v4(Noff), v4(N0), mk(d, 0), ALU.mult, [tN0, t_c2], [tNoff])
                B.tt("dve", Db, Noff, i4, ALU.add, [tNoff, t_c2], [tDb])
                B.mm(banks[P1][:], identb[:], Db, True, True, [t_const, tDb], [bk[P1]], skip=True)
                yield
                for lev in range(1, 7):
                    for r in range(4):
                        rs_ = slice(r * 128, (r + 1) * 128)
                        B.mm(banks[P0][:, rs_], N0[:, rs_], Tb[:, rs_], True, True,
                             [tN0, tTb], [bk[P0]])
                    B.tt("dve", v4(Zt), v4(banks[P0][:]), mk(1 - d, lev), ALU.mult,
                         [bk[P0], t_c2], [tZt])
                    yield
                    for r in range(4):
                        rs_ = slice(r * 128, (r + 1) * 128)
                        B.mm(banks[P2][:, rs_], Db[:, rs_], Zt[:, rs_], False, True,
                             [tDb, tZt], [bk[P2]], skip=True)
                    if lev < 6:
                        for r in range(4):
                            rs_ = slice(r * 128, (r + 1) * 128)
                            B.mm(banks[P1][:, rs_], Zt[:, rs_], Db[:, rs_], False, True,
                                 [tDb, tZt], [bk[P1]], skip=True)
                    B.cp("act", Tb, banks[P2][:], [bk[P2]], [tTb])
                    if lev < 6:
                        B.cp("dve", Db, banks[P1][:], [bk[P1]], [tDb])
                    yield
                B.tt("pool", v4(Ru), v4(v_tok[:, tsl]), colb(BETAc), ALU.mult,
                     [t_vtok, t_hc[BETAc]], [tRu])
                B.tt("pool", v4(Rw), v4(k_tok[:, tsl]), colb(BEGc), ALU.mult,
                     [t_ktok, t_hc[BEGc]], [tRw])
                B.tt("pool", v4(kd_), v4(k_tok[:, tsl]), colb(KDc), ALU.mult,
                     [t_ktok, t_hc[KDc]], [tkd])
                for r in range(4):
                    rs_ = slice(r * 128, (r + 1) * 128)
                    B.mm(banks[P0][:, rs_], Rw[:, rs_], Tb[:, rs_], True, True, [tRw, tTb],
                         [bk[P0]])
                B.act(nWt, banks[P0][:], AF.Identity, [bk[P0]], [tnW], scale=-1.0)
                yield
                rr = range(4) if d == 0 else range(3, -1, -1)
                for r in rr:
                    rs_ = slice(r * 128, (r + 1) * 128)
                    B.mm(banks[P1][:, 0:128], Tb[:, rs_], Ru[:, rs_], True, False, [tTb, tRu],
                         [bk[P1]])
                    B.mm(banks[P1][:, 0:128], nWt[:, rs_], Sb_, False, True, [tnW, tSb_], [bk[P1]])
                    B.cp("act", vn_, banks[P1][:, 0:128], [bk[P1]], [tvn_])
                    yield
                    B.mm(banks[P3][:, rs_], Sb_, qdT[:, rs_], True, False, [tSb_, tqd], [bk[P3]])
                    B.mm(banks[P3][:, rs_], vn_, attnT[:, rs_], False, True, [tvn_, tattn],
                         [bk[P3]])
                    B.mm(banks[P1][:, 128:256], kd_[:, rs_], vn_, True, True, [tkd, tvn_],
                         [bk[P1]])
                    B.stt(S_, S_, eg_[:, r:r + 1], banks[P1][:, 128:256], ALU.mult, ALU.add,
                          [tS_, teg_, bk[P1]], [tS_])
                    B.cp("act", Sb_, S_, [tS_], [tSb_])
                    yield
                B.tt("dve", hacc[:, tsl], banks[P3][:], hacc[:, tsl], ALU.add,
                     [bk[P3], t_hacc[tb]], [t_hacc[tb]])
                yield

        gens = [chain(0, chainR[0]), chain(1, chainR[1])]
        while gens:
            for g in list(gens):
                try:
                    next(g)
                except StopIteration:
                    gens.remove(g)
        out_norm(h, 1)

    def branch_phase():
        inherit(y_flat + t_xm, msc_toks)
        for m in range(KC):
            w, wt = w_next()
            for tb in range(NTB):
                tsl = slice(tb * 512, (tb + 1) * 512)
                for kc in range(KC):
                    B.mm(banks[0][:], w[:, kc * 128:(kc + 1) * 128], hT[:, kc, tsl], kc == 0,
                         kc == KC - 1, [wt, t_h[tb]], [bk[0]])
                for kh in range(8):
                    B.mm(banks[1][:], w[:, 1024 + kh * 128:1024 + (kh + 1) * 128],
                         hXT[:, kh * S + tb * 512: kh * S + (tb + 1) * 512], kh == 0, kh == 7,
                         [wt, t_hx[kh][tb]], [bk[1]])
                sl = tb % 2
                B.act(tmpA[:, sl, :], banks[0][:], AF.Sigmoid, [bk[0]], [t_sil[sl]])
                B.tt("dve", yT[:, m * S + tb * 512: m * S + (tb + 1) * 512], banks[1][:],
                     tmpA[:, sl, :], ALU.mult, [bk[1], t_sil[sl]], [t_y[m][tb]])
        for mm_ in range(4):
            w, wt = w_next()
            for half in range(2):
                m = 2 * mm_ + half
                sl = m % 2
                B.dma(xm[:, sl, :], xsp_d[:, m, :], [t_xsp[m]], [t_xm[sl]], eng="act")
                for tb in range(NTB):
                    tsl = slice(tb * 512, (tb + 1) * 512)
                    bi = 2 + (tb % 2)
                    for kc in range(KC):
                        B.mm(banks[bi][:], w[:, half * 1024 + kc * 128: half * 1024 + (kc + 1) * 128],
                             yT[:, kc * S + tb * 512: kc * S + (tb + 1) * 512], kc == 0,
                             kc == KC - 1, [wt, t_y[kc][tb]], [bk[bi]])
                    B.stt(xm[:, sl, tsl], banks[bi][:], gcol[:, KC + m:KC + m + 1], xm[:, sl, tsl],
                          ALU.mult, ALU.add, [bk[bi], t_mod, t_xm[sl]], [t_xm[sl]])
                B.dma(xsp_d[:, m, :], xm[:, sl, :], [t_xm[sl]], [t_xsp[m]], eng="act")
        inherit(msc_toks, y_flat + t_xm)

    t_xsp = [Tok() for _ in range(KC)]

    def mixer(which):
        nonlocal arena_toks
        for tb in range(NTB):
            norm_block(1, tb)
        x_flat = [t for row in t_x for t in row]
        for kc in range(KC):
            B.dma(xsp_d[:, kc, :], xT[:, kc, :], t_x[kc], [t_xsp[kc]], eng="sp")
        inherit(msc_toks + y_flat + t_xm, x_flat)
        inherit(hx_flat + t_hc, arena_toks)
        w, wt = w_next()
        B.cp("pool", wgb[:], w[:, 0:512], [wt], [t_c2])
        if which in ("all", "mlstm"):
            mlstm_gates()
            for h in range(8):
                mlstm_head(h)
            pend_on["f"]()
            pend_on["f"] = None
        else:
            for h in range(8):
                w_next()
                w_next()
            for tk in hx_flat:
                pass
            P.add("pool", lambda e: e.memset(hXT, 0.0), [], hx_flat)
        branch_phase()
        if which in ("all", "gdn"):
            gdn_gates()
            for h in range(8):
                gdn_head(h)
        else:
            for h in range(8):
                w_next()
                w_next()
            P.add("pool", lambda e: e.memset(hXT, 0.0), hx_flat, hx_flat)
        branch_phase()
        inherit(x_flat, msc_toks + y_flat + t_xm)
        for kc in range(KC):
            B.dma(xT[:, kc, :], xsp_d[:, kc, :], [t_xsp[kc]], t_x[kc], eng="sp")
        arena_toks = hx_flat + t_hc

    if debug == "nomix":
        ffn(0)
        ffn(2)
        for tb in range(NTB):
            norm_block(3, tb, final=True)
    elif debug in ("mlstm", "gdn", "mixonly"):
        ffn(0)
        mixer({"mlstm": "mlstm", "gdn": "gdn", "mixonly": "all"}[debug])
        for kc in range(KC):
            t_o = Tok()
            out_toks.append(t_o)
            B.dma(out_d[:, kc, :], xT[:, kc, :], t_x[kc], [t_o])
    else:
        ffn(0)
        mixer("all")
        ffn(2)
        for tb in range(NTB):
            norm_block(3, tb, final=True)
    P.add("sp", None, out_toks, [])

    pool_sz = {"pe": 1, "act": 8, "dve": 1, "pool": 8, "sp": 16}
    P.finalize(pool_sz)
    engsem = {e: B.es.enter_context(nc.semaphore("sem_" + e)) for e in ENGS}
    dmasem = {e: [B.es.enter_context(nc.semaphore("dsem_%s_%d" % (e, i)))
                  for i in range(pool_sz[e])] for e in ("act", "pool", "sp")}
    with nc.Block() as block:
        @block.tensor
        def _(e):
            P.emit("pe", e, engsem, dmasem)

        @block.scalar
        def _(e):
            P.emit("act", e, engsem, dmasem)

        @block.vector
        def _(e):
            P.emit("dve", e, engsem, dmasem)

        @block.gpsimd
        def _(e):
            P.emit("pool", e, engsem, dmasem)

        @block.sync
        def _(e):
            P.emit("sp", e, engsem, dmasem)
    B.es.close()
    return nc


def _col(v):
    return np.ascontiguousarray(v.reshape(-1, 128).T)


def _chunk(w, c0):
    return w[:, c0:c0 + 128].reshape(KC, 128, 128).transpose(1, 0, 2).reshape(128, 1024)


def host_layout(inp):
    shared = {}
    w_ada = inp["w_ada"][0]
    shared["wada"] = np.ascontiguousarray(
        w_ada.reshape(KC, 128, 18, 512).transpose(2, 1, 0, 3).reshape(18, 128, KC * 512))
    shared["bada"] = np.ascontiguousarray(inp["b_ada"][0].reshape(1, 9216))
    shared["normw"] = np.ascontiguousarray(np.concatenate(
        [_col(inp["norm_ffn1"][0]), _col(inp["norm_mix"][0]), _col(inp["norm_ffn2"][0]),
         _col(inp["norm_final"])], axis=1))

    def ffn_in(w):
        return np.ascontiguousarray(
            w.reshape(KC, 128, 2, NJ, 128).transpose(3, 1, 2, 0, 4).reshape(NJ, 128, 2048))

    def ffn_out(w):
        return np.ascontiguousarray(
            w.reshape(2, GJ, 128, KC, 128).transpose(0, 3, 2, 1, 4).reshape(16, 128, GJ * 128))

    shared["wf1i"] = ffn_in(inp["w_ffn1_in"][0])
    shared["wf1o"] = ffn_out(inp["w_ffn1_out"][0])
    shared["wf2i"] = ffn_in(inp["w_ffn2_in"][0])
    shared["wf2o"] = ffn_out(inp["w_ffn2_out"][0])
    shared["ident"] = np.eye(128, dtype=np.float32)

    win = inp["w_in"][0]
    wbm = inp["w_branch_mlstm"][0]
    wbg = inp["w_branch_gdn"][0]
    wo = inp["w_out"][0]
    O_MQ, O_MK, O_MV, O_MO, O_MG = 0, 1024, 2048, 3072, 4096
    O_GQ, O_GK, O_GV, O_GZ, O_GG = 4128, 5152, 6176, 7200, 8224
    O_MM, O_MGG = 8256, 9280
    units = []
    for h in range(8):
        units.append(np.concatenate([_chunk(win, O_MQ + h * 128), _chunk(win, O_MK + h * 128)], 1))
    for h in range(8):
        units.append(np.concatenate([_chunk(win, O_MV + h * 128), _chunk(win, O_MO + h * 128)], 1))
    for m in range(8):
        units.append(np.concatenate([_chunk(win, O_MM + m * 128), _chunk(wbm, m * 128)], 1))
    for mm_ in range(4):
        units.append(np.concatenate([_chunk(wo, 2 * mm_ * 128), _chunk(wo, (2 * mm_ + 1) * 128)], 1))
    for h in range(8):
        units.append(np.concatenate([_chunk(win, O_GQ + h * 128), _chunk(win, O_GK + h * 128)], 1))
    for h in range(8):
        units.append(np.concatenate([_chunk(win, O_GV + h * 128), _chunk(win, O_GZ + h * 128)], 1))
    for m in range(8):
        units.append(np.concatenate([_chunk(win, O_MGG + m * 128), _chunk(wbg, m * 128)], 1))
    shared["winu"] = np.ascontiguousarray(np.stack(units, 0))

    def gchunk(c0):
        return win[:, c0:c0 + 32].reshape(KC, 128, 32).transpose(1, 0, 2).reshape(128, 256)

    shared["wgates"] = np.ascontiguousarray(np.concatenate([gchunk(O_MG), gchunk(O_GG)], 1))
    hidx = np.arange(128) // 16
    gbias = np.zeros((128, 8), np.float32)
    mgb = inp["mlstm_gate_bias"][0]
    for g in range(4):
        gbias[:, g] = mgb[g][hidx]
    gbias[:, 4] = inp["gdn_a_log"][0][0][hidx]
    gbias[:, 5] = inp["gdn_a_log"][0][1][hidx]
    gbias[:, 6] = inp["gdn_dt_bias"][0][0][hidx]
    gbias[:, 7] = inp["gdn_dt_bias"][0][1][hidx]
    shared["gbias"] = gbias
    cwv = inp["gdn_conv_w"][0]
    shared["convw"] = np.ascontiguousarray(
        cwv.reshape(5, 24, 128).transpose(2, 1, 0).reshape(128, 120))
    shared["onorm"] = np.ascontiguousarray(
        np.stack([inp["mlstm_out_norm"][0], inp["gdn_out_norm"][0]], 1))
    p = np.arange(128)[:, None]
    j = np.arange(128)[None, :]
    tri_f = np.where(p <= j, 0.0, -BIG)
    tri_b = np.where(p >= j, 0.0, -BIG)
    big_f = np.where(j < p, 0.0, BIG)
    big_b = np.where(j > p, 0.0, BIG)
    shared["cmask"] = np.ascontiguousarray(
        np.stack([tri_f, tri_b, big_f, big_b], 1).reshape(128, 512).astype(np.float32))
    hk, ck = p // 16, p % 16
    hm, cm_ = j // 16, j % 16
    lf = ((hk == hm) & (ck < cm_)).astype(np.float32)
    lb = ((hk == hm) & (ck > cm_)).astype(np.float32)
    shared["clm"] = np.ascontiguousarray(np.stack([lf, lb], 1).reshape(128, 256))
    gms = []
    for dd in range(2):
        for lev in range(7):
            bsz = 1 << lev
            same = (p // (2 * bsz)) == (j // (2 * bsz))
            if dd == 0:
                mk_ = same & (p % (2 * bsz) >= bsz) & (j % (2 * bsz) < bsz)
            else:
                mk_ = same & (p % (2 * bsz) < bsz) & (j % (2 * bsz) >= bsz)
            gms.append(mk_)
    shared["gmask"] = np.ascontiguousarray(
        np.stack(gms, 1).reshape(128, 14 * 128).astype(ml_dtypes.bfloat16))
    in_maps = []
    for b in range(NCORES):
        m = dict(shared)
        m["xT"] = np.ascontiguousarray(inp["x"][b].T.reshape(KC, 128, S).transpose(1, 0, 2))
        m["ccol"] = _col(inp["c"][b])
        in_maps.append(m)
    return in_maps


_NC_CACHE = {}
DEBUG = None


def kernel(**inputs):
    inp = {k: np.asarray(v, dtype=np.float32) for k, v in inputs.items()}
    in_maps = host_layout(inp)
    if "nc" not in _NC_CACHE:
        _NC_CACHE["nc"] = build(DEBUG)
    nc = _NC_CACHE["nc"]
    res = run_bass_kernel_spmd(nc, in_maps, core_ids=list(range(NCORES)))
    out = np.empty((NCORES, S, D), np.float32)
    for b in range(NCORES):
        oT = np.asarray(res.results[b]["outT"]).reshape(128, KC, S)
        out[b] = oT.transpose(1, 0, 2).reshape(D, S).T
    return out
```

```python
import numpy as np
import ml_dtypes
from contextlib import ExitStack
import concourse.bass as bass
import concourse.mybir as mybir
from concourse.bass_utils import run_bass_kernel_spmd

F32 = mybir.dt.float32
BF16 = mybir.dt.bfloat16
AF = mybir.ActivationFunctionType
ALU = mybir.AluOpType

D = 1024
S = 2048
KC = 8
NTB = 4
DFF = 2816
NJ = 22
GJ = 11
EPS = 1e-6
NCORES = 8

ENGS = ("pe", "act", "dve", "pool", "sp")


class Tok:
    __slots__ = ("w", "r")

    def __init__(self):
        self.w = None
        self.r = {}


class Op:
    __slots__ = ("eng", "fn", "dma", "sig", "sigval", "idx", "deps", "dsem", "dval")


class Prog:
    def __init__(self):
        self.lists = {e: [] for e in ENGS}
        self.count = 0
        self.dma_ops = {e: [] for e in ENGS}

    def add(self, eng, fn, reads=(), writes=(), dma=False):
        o = Op()
        o.eng = eng
        o.fn = fn
        o.dma = dma
        o.sig = False
        o.sigval = 0
        o.idx = self.count
        self.count += 1
        deps = {}

        def dep(p, raw):
            if p is None:
                return
            if not p.dma and not dma and p.eng == eng and eng == "pe":
                return
            deps[p.idx] = p

        for t in reads:
            dep(t.w, True)
        for t in writes:
            dep(t.w, False)
            for p in t.r.values():
                dep(p, False)
        for t in reads:
            key = ("d", o.idx) if dma else eng
            t.r[key] = o
        for t in writes:
            t.w = o
            t.r = {}
        if dma:
            lst = self.dma_ops[eng]
            o.dsem = None
            lst.append(o)
        o.deps = list(deps.values())
        self.lists[eng].append(o)
        return o

    def finalize(self, dma_pool_size):
        for e in ENGS:
            lst = self.dma_ops[e]
            K = dma_pool_size[e]
            for i, o in enumerate(lst):
                o.dsem = (e, i % K)
                o.dval = 16 * (i // K + 1)
                if i >= K:
                    o.deps.append(lst[i - K])
        for e in ENGS:
            for o in self.lists[e]:
                for p in o.deps:
                    if not p.dma:
                        p.sig = True
        for e in ENGS:
            n = 0
            for o in self.lists[e]:
                if o.sig and not o.dma:
                    n += 1
                    o.sigval = n

    def emit(self, ename, eng, engsem, dmasem):
        waited = {}
        for o in self.lists[ename]:
            need = {}
            for p in o.deps:
                if p.dma:
                    key = ("d",) + p.dsem
                    val = p.dval
                else:
                    key = ("e", p.eng)
                    val = p.sigval
                if need.get(key, 0) < val:
                    need[key] = val
            for key, val in need.items():
                if waited.get(key, 0) < val:
                    sem = dmasem[key[1]][key[2]] if key[0] == "d" else engsem[key[1]]
                    eng.wait_ge(sem, val)
                    waited[key] = val
            if o.fn is None:
                continue
            ins = o.fn(eng)
            if o.dma:
                ins.then_inc(dmasem[o.dsem[0]][o.dsem[1]], 16)
            elif o.sig:
                ins.then_inc(engsem[ename], 1)


def _host_consts():
    c = {}
    ident = np.eye(128, dtype=np.float32)
    c["ident"] = ident
    c["ones"] = np.ones((128, 128), np.float32)
    return c


class Builder:
    def __init__(self, debug=None):
        self.debug = debug
        self.nc = bass.Bass("TRN2", target_bir_lowering=False)
        self.P = Prog()
        self.es = ExitStack()
        self.dram = {}
        self.outs = {}

    def din(self, name, shape, dtype=F32):
        t = self.nc.dram_tensor(name, list(shape), dtype, kind="ExternalInput").ap()
        self.dram[name] = t
        return t

    def dout(self, name, shape, dtype=F32):
        t = self.nc.dram_tensor(name, list(shape), dtype, kind="ExternalOutput").ap()
        self.outs[name] = t
        return t

    def sb(self, name, shape, dtype=F32):
        h = self.es.enter_context(self.nc.sbuf_tensor(name, list(shape), dtype))
        return h

    def ps(self, name, shape, dtype=F32):
        h = self.es.enter_context(self.nc.psum_tensor(name, list(shape), dtype))
        return h

    def mm(self, out, lhsT, rhs, start, stop, reads, writes, skip=False):
        if skip:
            return self.P.add("pe", lambda e: e.matmul(out, lhsT, rhs, start=start, stop=stop,
                                                       skip_group_check=True), reads, writes)
        return self.P.add("pe", lambda e: e.matmul(out, lhsT, rhs, start=start, stop=stop),
                          reads, writes)

    def tr(self, out, in_, ident, reads, writes):
        return self.P.add("pe", lambda e: e.transpose(out, in_, ident), reads, writes)

    def act(self, out, in_, func, reads, writes, bias=0.0, scale=1.0, eng="act"):
        return self.P.add(eng, lambda e: e.activation(out, in_, func, bias=bias, scale=scale),
                          reads, writes)

    def tt(self, eng, out, in0, in1, op, reads, writes):
        return self.P.add(eng, lambda e: e.tensor_tensor(out, in0, in1, op), reads, writes)

    def stt(self, out, in0, scalar, in1, op0, op1, reads, writes, eng="dve"):
        return self.P.add(eng, lambda e: e.scalar_tensor_tensor(out, in0, scalar, in1, op0, op1),
                          reads, writes)

    def ts(self, eng, out, in0, s1, s2, op0, op1, reads, writes):
        if s2 is None:
            return self.P.add(eng, lambda e: e.tensor_scalar(out, in0, s1, None, op0), reads, writes)
        return self.P.add(eng, lambda e: e.tensor_scalar(out, in0, s1, s2, op0, op1), reads, writes)

    def cp(self, eng, out, in_, reads, writes):
        if eng == "act":
            return self.P.add(eng, lambda e: e.copy(out, in_), reads, writes)
        return self.P.add(eng, lambda e: e.tensor_copy(out, in_), reads, writes)

    def recip(self, out, in_, reads, writes):
        return self.P.add("dve", lambda e: e.reciprocal(out, in_), reads, writes)

    def memset(self, eng, ap, val, reads, writes):
        return self.P.add(eng, lambda e: e.memset(ap, val), reads, writes)

    def dma(self, out, in_, reads, writes, eng="sp"):
        return self.P.add(eng, lambda e: e.dma_start(out, in_), reads, writes, dma=True)


BIG = 30000.0
NHC = 24
NCB = 12
MSC_XM = 10496


def inherit(new_toks, old_toks):
    ops = {}
    for t in old_toks:
        if t.w is not None:
            ops[t.w.idx] = t.w
        for p in t.r.values():
            ops[p.idx] = p
    for nt in new_toks:
        for i, p in ops.items():
            nt.r[("f", i)] = p


def build(debug=None):
    B = Builder(debug)
    nc = B.nc
    P = B.P

    xT_d = B.din("xT", [128, KC, S])
    c_d = B.din("ccol", [128, KC])
    wada_d = B.din("wada", [18, 128, KC * 512])
    bada_d = B.din("bada", [1, 9216])
    normw_d = B.din("normw", [128, 4 * KC])
    wf1i_d = B.din("wf1i", [NJ, 128, 2048])
    wf1o_d = B.din("wf1o", [16, 128, GJ * 128])
    wf2i_d = B.din("wf2i", [NJ, 128, 2048])
    wf2o_d = B.din("wf2o", [16, 128, GJ * 128])
    ident_d = B.din("ident", [128, 128])
    win_d = B.din("winu", [52, 128, 2048])
    wg_d = B.din("wgates", [128, 512])
    gb_d = B.din("gbias", [128, 8])
    cw_d = B.din("convw", [128, 120])
    on_d = B.din("onorm", [128, 2])
    cm_d = B.din("cmask", [128, 512])
    lm_d = B.din("clm", [128, 256])
    gm_d = B.din("gmask", [128, 14 * 128], BF16)
    out_d = B.dout("outT", [128, KC, S])
    xsp_d = nc.dram_tensor("xspill", [128, KC, S], F32, kind="Internal").ap()
    bc_d = nc.dram_tensor("bcd", [6, 128, 128], F32, kind="Internal").ap()

    xT = B.sb("xT_sb", [128, KC, S])
    hT = B.sb("hT_sb", [128, KC, S], BF16)
    arena = B.sb("arena", [128, 11264])
    NST = 2
    NBF = 3
    wst = B.sb("wst", [128, NST, 2048])
    wbf = B.sb("wbf", [128, NBF, 2048], BF16)
    modrow = B.sb("modrow", [1, 2, 512])
    brow = B.sb("brow", [1, 2, 512])
    ccol = B.sb("ccol_sb", [128, KC])
    cs = B.sb("cs_sb", [128, KC])
    modT = B.sb("modT", [128, 72])
    normw = B.sb("normw_sb", [128, 4 * KC])
    acol = B.sb("acol", [128, 3 * KC])
    gcol = B.sb("gcol", [128, 3 * KC])
    ident = B.sb("ident_sb", [128, 128])
    identb = B.sb("identb_sb", [128, 128], BF16)
    identb4 = B.sb("identb4", [128, 4, 128], BF16)
    onesb = B.sb("onesb", [128, 128], BF16)
    one11 = B.sb("one11", [1, 2])
    epsc = B.sb("epsc", [128, 1])
    tmpA = B.sb("tmpA", [128, 2, 512])
    tmpB = B.sb("tmpB", [128, 2, 512])
    rstd = B.sb("rstd", [128, 2, 512])
    nrmA = B.sb("nrmA", [128, 2, 512])
    sqb = B.sb("sqb", [128, 2, 512], BF16)
    wgb = B.sb("wgb", [128, 512], BF16)
    gb = B.sb("gb_sb", [128, 8])
    nega = B.sb("nega", [128, 2])
    cw = B.sb("cw_sb", [128, 120])
    onorm = B.sb("onorm_sb", [128, 2])
    cmask = B.sb("cmask_sb", [128, 4, 128])
    clm = B.sb("clm_sb", [128, 2, 128])
    eglt = B.sb("eglt", [128, 2, 4])
    gmaskb = B.sb("gmaskb", [128, 14, 128], BF16)

    banks = [B.ps("bank%d" % i, [128, 512]) for i in range(8)]
    bk = [Tok() for _ in range(8)]

    t_x = [[Tok() for _ in range(NTB)] for _ in range(KC)]
    t_h = [Tok() for _ in range(NTB)]
    t_const = Tok()
    t_mod = Tok()
    t_fin = [Tok(), Tok()]
    t_sqs = [Tok(), Tok()]
    t_rs = [Tok(), Tok()]
    t_sil = [Tok(), Tok()]
    out_toks = []

    B.dma(ident[:], ident_d, [], [t_const])
    B.dma(ccol[:], c_d, [], [t_const])
    B.dma(normw[:], normw_d, [], [t_const])
    P.add("pool", lambda e: e.memset(onesb[:], 1.0), [], [t_const])
    P.add("pool", lambda e: e.memset(one11[:], 1.0), [], [t_const])
    P.add("pool", lambda e: e.memset(epsc[:], EPS), [], [t_const])
    B.cp("pool", identb[:], ident[:], [t_const], [t_const])
    t_c2 = Tok()
    B.dma(gb[:], gb_d, [], [t_c2], eng="act")
    B.dma(cw[:], cw_d, [], [t_c2], eng="act")
    B.dma(onorm[:], on_d, [], [t_c2], eng="act")
    B.dma(cmask[:].rearrange("p a b -> p (a b)"), cm_d, [], [t_c2], eng="act")
    B.dma(clm[:].rearrange("p a b -> p (a b)"), lm_d, [], [t_c2], eng="act")
    B.dma(gmaskb[:].rearrange("p a b -> p (a b)"), gm_d, [], [t_c2], eng="act")
    for r in range(4):
        B.cp("pool", identb4[:, r, :], ident[:], [t_const], [t_c2])
    B.act(nega[:], gb[:, 4:6], AF.Exp, [t_c2], [t_c2])
    B.ts("dve", nega[:], nega[:], -1.0, None, ALU.mult, None, [t_c2], [t_c2])

    for kc in range(KC):
        B.dma(xT[:, kc, :], xT_d[:, kc, :], [], t_x[kc], eng="act" if kc % 2 else "sp")

    t_cs = Tok()
    B.act(cs[:], ccol[:], AF.Silu, [t_const], [t_cs])
    t_ada = [Tok(), Tok()]
    t_brow = [Tok(), Tok()]
    t_modrow = [Tok(), Tok()]
    adabuf = arena[:, 0:2 * KC * 512].rearrange("p (s n) -> p s n", s=2)
    for blk in range(18):
        sl = blk % 2
        B.dma(adabuf[:, sl, :], wada_d[blk], [], [t_ada[sl]], eng="sp")
        B.dma(brow[0:1, sl, :], bada_d[0:1, blk * 512:(blk + 1) * 512], [], [t_brow[sl]], eng="act")
        b = banks[sl]
        for kc in range(KC):
            B.mm(b[0:1, :], cs[:, kc:kc + 1], adabuf[:, sl, kc * 512:(kc + 1) * 512],
                 kc == 0, False, [t_cs, t_ada[sl]], [bk[sl]])
        B.mm(b[0:1, :], one11[0:1, 0:1], brow[0:1, sl, :], False, True,
             [t_const, t_brow[sl]], [bk[sl]])
        B.cp("act", modrow[0:1, sl, :], b[0:1, :], [bk[sl]], [t_modrow[sl]])
        for q in range(4):
            j = blk * 4 + q
            B.mm(banks[2][:, j:j + 1], modrow[0:1, sl, q * 128:(q + 1) * 128], one11[0:1, 0:1],
                 True, True, [t_modrow[sl], t_const], [bk[2]])
    B.cp("dve", modT[:], banks[2][:, 0:72], [bk[2]], [t_mod])
    for a in range(3):
        B.stt(acol[:, a * KC:(a + 1) * KC], modT[:, (a * 3 + 1) * KC:(a * 3 + 2) * KC], 1.0,
              normw[:, a * KC:(a + 1) * KC], ALU.add, ALU.mult, [t_mod, t_const], [t_mod])
        B.ts("dve", gcol[:, a * KC:(a + 1) * KC], modT[:, (a * 3 + 2) * KC:(a * 3 + 3) * KC],
             0.5 if a != 1 else 1.0, None, ALU.mult, None, [t_mod], [t_mod])
    arena_toks = list(t_ada)

    st_tok = [Tok() for _ in range(NST)]
    bf_tok = [Tok() for _ in range(NBF)]
    wq = []
    wstate = {"loaded": 0, "used": 0}
    PF = 3

    def w_issue():
        i = wstate["loaded"]
        src, n = wq[i]
        s_ = i % NST
        b_ = i % NBF
        B.dma(wst[:, s_, 0:n], src, [], [st_tok[s_]], eng="sp")
        B.cp("pool", wbf[:, b_, 0:n], wst[:, s_, 0:n], [st_tok[s_]], [bf_tok[b_]])
        wstate["loaded"] += 1

    def w_next():
        i = wstate["used"]
        while wstate["loaded"] < min(len(wq), i + PF):
            w_issue()
        wstate["used"] += 1
        return wbf[:, i % NBF, :], bf_tok[i % NBF]

    def declare_ffn(wi, wo):
        for g in range(2):
            for jj in range(GJ):
                wq.append((wi[g * GJ + jj], 2048))
            for m in range(KC):
                wq.append((wo[g * KC + m], GJ * 128))

    declare_ffn(wf1i_d, wf1o_d)
    if debug not in ("nomix",):
        wq.append((wg_d, 512))
        for h in range(8):
            wq.append((win_d[h], 2048))
            wq.append((win_d[8 + h], 2048))
        if debug != "mlstm_only":
            pass
        for m in range(8):
            wq.append((win_d[16 + m], 2048))
        for mm_ in range(4):
            wq.append((win_d[24 + mm_], 2048))
        for h in range(8):
            wq.append((win_d[28 + h], 2048))
            wq.append((win_d[36 + h], 2048))
        for m in range(8):
            wq.append((win_d[44 + m], 2048))
        for mm_ in range(4):
            wq.append((win_d[24 + mm_], 2048))
    declare_ffn(wf2i_d, wf2o_d)

    nstate = {"i": 0}

    def norm_block(a, tb, final=False):
        i = nstate["i"]
        nstate["i"] += 1
        sl = i % 2
        bi = 6 + sl
        tsl = slice(tb * 512, (tb + 1) * 512)
        for kc in range(KC):
            B.act(sqb[:, kc % 2, :], xT[:, kc, tsl], AF.Square, [t_x[kc][tb]], [t_sqs[kc % 2]])
            B.mm(banks[bi][:], onesb[:], sqb[:, kc % 2, :], kc == 0, kc == KC - 1,
                 [t_sqs[kc % 2], t_const], [bk[bi]])
        t_r = t_rs[sl]
        B.act(nrmA[:, sl, :], banks[bi][:], AF.Ln, [bk[bi], t_const], [t_r], bias=epsc[:, 0:1], scale=1.0 / D)
        B.act(rstd[:, sl, :], nrmA[:, sl, :], AF.Exp, [t_r], [t_r], scale=-0.5)
        for kc in range(KC):
            if final:
                B.stt(tmpB[:, kc % 2, :], xT[:, kc, tsl], normw[:, 3 * KC + kc:3 * KC + kc + 1],
                      rstd[:, sl, :], ALU.mult, ALU.mult, [t_x[kc][tb], t_r, t_const],
                      [t_fin[kc % 2]])
                t_o = Tok()
                out_toks.append(t_o)
                B.dma(out_d[:, kc, tsl], tmpB[:, kc % 2, :], [t_fin[kc % 2]], [t_o],
                      eng="sp")
            else:
                B.stt(tmpB[:, kc % 2, :], xT[:, kc, tsl], acol[:, a * KC + kc:a * KC + kc + 1],
                      rstd[:, sl, :], ALU.mult, ALU.mult, [t_x[kc][tb], t_r, t_mod],
                      [t_fin[kc % 2]])
                B.act(hT[:, kc, tsl], tmpB[:, kc % 2, :], AF.Identity, [t_fin[kc % 2], t_mod],
                      [t_h[tb]], bias=modT[:, a * 3 * KC + kc:a * 3 * KC + kc + 1], scale=1.0)

    def ffn(a):
        nonlocal arena_toks
        for tb in range(NTB):
            norm_block(a, tb)
        actT = arena[:].bitcast(BF16)
        t_act = [[Tok() for _ in range(NTB)] for _ in range(GJ)]
        flat = [t for row in t_act for t in row]
        inherit(flat, arena_toks)
        pi = 0
        for g in range(2):
            for jj in range(GJ):
                w, wt = w_next()
                for tb in range(NTB):
                    tsl = slice(tb * 512, (tb + 1) * 512)
                    bg = (pi % 2) * 2
                    bu = bg + 1
                    pi += 1
                    for kc in range(KC):
                        B.mm(banks[bg][:], w[:, kc * 128:(kc + 1) * 128], hT[:, kc, tsl],
                             kc == 0, kc == KC - 1, [wt, t_h[tb]], [bk[bg]])
                    for kc in range(KC):
                        B.mm(banks[bu][:], w[:, 1024 + kc * 128:1024 + (kc + 1) * 128],
                             hT[:, kc, tsl], kc == 0, kc == KC - 1, [wt, t_h[tb]], [bk[bu]])
                    sl = pi % 2
                    B.act(tmpA[:, sl, :], banks[bg][:], AF.Silu, [bk[bg]], [t_sil[sl]])
                    B.tt("dve", actT[:, jj * S + tb * 512: jj * S + (tb + 1) * 512],
                         banks[bu][:], tmpA[:, sl, :], ALU.mult, [bk[bu], t_sil[sl]],
                         [t_act[jj][tb]])
            for m in range(KC):
                w, wt = w_next()
                for tb in range(NTB):
                    tsl = slice(tb * 512, (tb + 1) * 512)
                    bo = 4 + (pi % 2)
                    pi += 1
                    for jj in range(GJ):
                        B.mm(banks[bo][:], w[:, jj * 128:(jj + 1) * 128],
                             actT[:, jj * S + tb * 512: jj * S + (tb + 1) * 512],
                             jj == 0, jj == GJ - 1, [wt, t_act[jj][tb]], [bk[bo]])
                    B.stt(xT[:, m, tsl], banks[bo][:], gcol[:, a * KC + m:a * KC + m + 1],
                          xT[:, m, tsl], ALU.mult, ALU.add, [bk[bo], t_mod, t_x[m][tb]],
                          [t_x[m][tb]])
        arena_toks = flat

    msc = xT[:].rearrange("p a b -> p (a b)")

    def bfv(lo, n_words):
        return msc[:, lo:lo + n_words].bitcast(BF16)

    qT = bfv(0, 1024)
    kT = bfv(1024, 1024)
    gateT = bfv(2048, 1024)
    v_tok = bfv(3072, 1024)
    k_tok = bfv(4096, 1024)
    hacc = msc[:, 5120:7168]
    cb = [bfv(7168 + i * 256, 256) for i in range(NCB)]
    Sst = msc[:, 10240:10368]
    Sb = bfv(10368, 64)
    vnb = bfv(10432, 64)
    xm = msc[:, MSC_XM:MSC_XM + 2 * 2048].rearrange("p (s n) -> p s n", s=2)
    yT = bfv(0, 8192)
    t_q, t_k, t_gate, t_vtok, t_ktok = Tok(), Tok(), Tok(), Tok(), Tok()
    t_hacc = [Tok() for _ in range(NTB)]
    t_cb = [Tok() for _ in range(NCB)]
    t_S, t_Sb, t_vnb, t_egl = Tok(), Tok(), Tok(), Tok()
    t_xm = [Tok(), Tok()]
    msc_toks = [t_q, t_k, t_gate, t_vtok, t_ktok] + t_hacc + t_cb + [t_S, t_Sb, t_vnb]
    t_y = [[Tok() for _ in range(NTB)] for _ in range(KC)]
    y_flat = [t for row in t_y for t in row]

    bcs = [msc[:, 13312:13824], msc[:, 13824:14336], msc[:, 14848:15360], msc[:, 15360:15872]]
    t_bc = [Tok() for _ in range(4)]
    t_bcd = [Tok() for _ in range(6)]
    msc_toks += t_bc

    def bc_load(slot, q, h, tb):
        r0 = h * 16 + tb * 4
        src = bc_d[q, r0:r0 + 4, :].rearrange("a b -> (a b)").partition_broadcast(128)
        B.dma(bcs[slot], src, [t_bcd[q]], [t_bc[slot]], eng="act")

    cb1 = [bfv(MSC_XM + i * 256, 256) for i in range(11)]
    t_cb1 = [Tok() for _ in range(11)]
    S1 = msc[:, 14592:14720]
    Sb1 = bfv(14720, 64)
    vnb1 = bfv(14784, 64)
    t_S1, t_Sb1, t_vnb1 = Tok(), Tok(), Tok()
    t_egl1 = Tok()
    msc_toks += t_cb1 + [t_S1, t_Sb1, t_vnb1]
    chainR = [
        dict(banks=[0, 1, 2, 3], cb=cb, tcb=t_cb, E=(tmpB[:, 0, :], t_fin[0]),
             XA=(tmpA[:, 0, :], t_sil[0]), XT=(rstd[:, 0, :], t_rs[0]), EG=(nrmA[:, 0, :], t_rs[0]),
             S=(Sst, t_S), Sb=(Sb, t_Sb), vnb=(vnb, t_vnb), egl=(eglt[:, 0, :], t_egl)),
        dict(banks=[4, 5, 6, 7], cb=cb1, tcb=t_cb1, E=(tmpB[:, 1, :], t_fin[1]),
             XA=(tmpA[:, 1, :], t_sil[1]), XT=(rstd[:, 1, :], t_rs[1]), EG=(nrmA[:, 1, :], t_rs[1]),
             S=(S1, t_S1), Sb=(Sb1, t_Sb1), vnb=(vnb1, t_vnb1), egl=(eglt[:, 1, :], t_egl1)),
    ]

    hXT = arena[:, 0:8192].bitcast(BF16)
    t_hx = [[Tok() for _ in range(NTB)] for _ in range(8)]
    hx_flat = [t for row in t_hx for t in row]
    hcap = [arena[:, 8192 + i * 128: 8192 + (i + 1) * 128] for i in range(NHC)]
    t_hc = [Tok() for _ in range(NHC)]
    ZER = 21

    tri = [cmask[:, 0, :], cmask[:, 1, :]]
    bigA = [cmask[:, 2, :], cmask[:, 3, :]]

    rot = {"b": 0, "d": 0}

    def rbank():
        rot["b"] ^= 1
        return rot["b"]

    def hc_transpose_to(dst, src, bslot):
        B.tr(banks[6][:, bslot * 128:(bslot + 1) * 128], hcap[src], ident[:],
             [t_hc[src], t_const], [bk[6]])
        B.cp("dve", hcap[dst], banks[6][:, bslot * 128:(bslot + 1) * 128], [bk[6]], [t_hc[dst]])

    def gate_project(wcol0, dst_tiles, bias_cols):
        for c in range(16):
            for kc in range(KC):
                B.mm(banks[5][:, c * 32:(c + 1) * 32], hT[:, kc, c * 128:(c + 1) * 128],
                     wgb[:, wcol0 + kc * 32: wcol0 + (kc + 1) * 32], kc == 0, kc == KC - 1,
                     [t_h[c // 4], t_c2], [bk[5]])
        src = banks[5][:].rearrange("p (c g h) -> p g h c", c=16, g=4, h=8)
        dst = tmpA[:, 0, :].rearrange("p (g h c) -> p g h c", g=4, h=8, c=16)
        for g in range(4):
            B.cp("dve", dst[:, g], src[:, g], [bk[5]], [t_sil[0]])
        for g in range(4):
            B.tr(banks[6][:, g * 128:(g + 1) * 128], tmpA[:, 0, g * 128:(g + 1) * 128], ident[:],
                 [t_sil[0], t_const], [bk[6]])
        for g in range(4):
            i = dst_tiles[g]
            if bias_cols[g] is None:
                B.cp("act", hcap[i], banks[6][:, g * 128:(g + 1) * 128], [bk[6]], [t_hc[i]])
            else:
                B.act(hcap[i], banks[6][:, g * 128:(g + 1) * 128], AF.Identity, [bk[6], t_c2],
                      [t_hc[i]], bias=gb[:, bias_cols[g]:bias_cols[g] + 1], scale=1.0)

    def scan(dst, src, op0):
        B.P.add("dve", lambda e: e.tensor_tensor_scan(hcap[dst], hcap[src], hcap[ZER], 0.0, op0,
                                                      ALU.add),
                [t_hc[src], t_hc[ZER]], [t_hc[dst]])

    def hts(dst, src, s1, s2, op0, op1, extra=()):
        B.ts("dve", hcap[dst], hcap[src], s1, s2, op0, op1, [t_hc[src]] + list(extra), [t_hc[dst]])

    def htt(dst, a_, b_, op, eng="dve"):
        B.tt(eng, hcap[dst], hcap[a_], hcap[b_], op, [t_hc[a_], t_hc[b_]], [t_hc[dst]])

    def hact(dst, src, func, bias=0.0, scale=1.0, extra=()):
        B.act(hcap[dst], hcap[src], func, [t_hc[src]] + list(extra), [t_hc[dst]], bias=bias,
              scale=scale)

    def cross_chunk_sum(dst_col_tile, src, col, d):
        B.mm(banks[6][:, 0:1], clm[:, d, :], hcap[src][:, col:col + 1], True, True,
             [t_c2, t_hc[src]], [bk[6]])
        B.cp("dve", hcap[dst_col_tile][:, 0:1], banks[6][:, 0:1], [bk[6]], [t_hc[dst_col_tile]])

    def cross_chunk_max(dst_col_tile, dcol, src, col, d, ROW, RA, RB):
        B.tr(banks[6][0:1, 128:256], hcap[src][:, col:col + 1], ident[:], [t_hc[src], t_const],
             [bk[6]])
        B.cp("dve", hcap[ROW][0:1, :], banks[6][0:1, 128:256], [bk[6]], [t_hc[ROW]])
        row3 = hcap[ROW][0:1, :].rearrange("p (h c) -> p h c", h=8)
        cur, oth = RA, RB
        c3 = hcap[cur][0:1, :].rearrange("p (h c) -> p h c", h=8)
        B.memset("dve", hcap[cur][0:1, :], 0.0, [], [t_hc[cur]])
        if d == 0:
            B.cp("dve", c3[:, :, 1:16], row3[:, :, 0:15], [t_hc[ROW]], [t_hc[cur]])
        else:
            B.cp("dve", c3[:, :, 0:15], row3[:, :, 1:16], [t_hc[ROW]], [t_hc[cur]])
        for sh in (1, 2, 4, 8):
            c3 = hcap[cur][0:1, :].rearrange("p (h c) -> p h c", h=8)
            o3 = hcap[oth][0:1, :].rearrange("p (h c) -> p h c", h=8)
            if d == 0:
                B.tt("dve", o3[:, :, sh:16], c3[:, :, sh:16], c3[:, :, 0:16 - sh], ALU.max,
                     [t_hc[cur]], [t_hc[oth]])
                B.cp("dve", o3[:, :, 0:sh], c3[:, :, 0:sh], [t_hc[cur]], [t_hc[oth]])
            else:
                B.tt("dve", o3[:, :, 0:16 - sh], c3[:, :, 0:16 - sh], c3[:, :, sh:16], ALU.max,
                     [t_hc[cur]], [t_hc[oth]])
                B.cp("dve", o3[:, :, 16 - sh:16], c3[:, :, 16 - sh:16], [t_hc[cur]], [t_hc[oth]])
            cur, oth = oth, cur
        B.mm(banks[6][:, 1:2], hcap[cur][0:1, :], one11[0:1, 0:1], True, True,
             [t_hc[cur], t_const], [bk[6]])
        B.cp("dve", hcap[dst_col_tile][:, dcol:dcol + 1], banks[6][:, 1:2], [bk[6]],
             [t_hc[dst_col_tile]])

    NEGM = [9, 14]
    FLOOR = [10, 15]
    UCOL = [11, 16]

    def softplus_acc(dst, src, sign, tA, tB, tC):
        hact(tA, src, AF.Abs)
        hact(tB, tA, AF.Exp, scale=-1.0)
        hts(tA, tB, 2.0, None, ALU.add, None)
        B.recip(hcap[tA], hcap[tA], [t_hc[tA]], [t_hc[tA]])
        htt(tB, tB, tA, ALU.mult)
        htt(tA, tB, tB, ALU.mult)
        hts(tC, tA, 1.0 / 13.0, 1.0 / 11.0, ALU.mult, ALU.add)
        for cst in (1.0 / 9.0, 1.0 / 7.0, 1.0 / 5.0, 1.0 / 3.0, 1.0):
            htt(tC, tC, tA, ALU.mult)
            hts(tC, tC, cst, None, ALU.add, None)
        htt(tC, tC, tB, ALU.mult)
        hts(tA, src, float(sign), 0.0, ALU.mult, ALU.max)
        B.stt(hcap[dst], hcap[tC], 2.0, hcap[tA], ALU.mult, ALU.add, [t_hc[tC], t_hc[tA]],
              [t_hc[dst]])

    def mlstm_gates():
        P.add("pool", lambda e: e.memset(hcap[ZER], 0.0), [], [t_hc[ZER]])
        gate_project(0, [0, 1, 2, 3], [0, 1, 2, 3])
        for d in range(2):
            Gi, Gf = 2 * d, 2 * d + 1
            T1, SPt, LCS, CSP, U, LMX, X, COLS = 4, 5, 6, 7, 8, 12, 13, 20
            softplus_acc(SPt, Gf, -1.0, T1, LCS, CSP)
            scan(LCS, SPt, ALU.add)
            cross_chunk_sum(COLS, LCS, 127, d)
            if d == 1:
                hts(X, LCS, -1.0, hcap[LCS][:, 127:128], ALU.mult, ALU.add)
                htt(LCS, X, SPt, ALU.add)
            hts(CSP, LCS, hcap[COLS][:, 0:1], None, ALU.add, None, extra=[t_hc[COLS]])
            htt(U, Gi, CSP, ALU.add)
            if d == 0:
                scan(LMX, U, ALU.max)
                ccol_ = 127
            else:
                hts(X, U, 0.0, None, ALU.max, None)
                cur, oth = X, LMX
                for sh in (1, 2, 4, 8, 16, 32, 64):
                    B.tt("dve", hcap[oth][:, 0:128 - sh], hcap[cur][:, 0:128 - sh],
                         hcap[cur][:, sh:128], ALU.max, [t_hc[cur]], [t_hc[oth]])
                    B.cp("dve", hcap[oth][:, 128 - sh:128], hcap[cur][:, 128 - sh:128],
                         [t_hc[cur]], [t_hc[oth]])
                    cur, oth = oth, cur
                if cur != LMX:
                    B.cp("dve", hcap[LMX], hcap[cur], [t_hc[cur]], [t_hc[LMX]])
                ccol_ = 0
            cross_chunk_max(COLS, 1, LMX, ccol_, d, 17, 18, 19)
            hts(X, LMX, hcap[COLS][:, 1:2], None, ALU.max, None, extra=[t_hc[COLS]])
            hts(NEGM[d], X, -1.0, None, ALU.mult, None)
            htt(T1, CSP, X, ALU.subtract)
            hact(FLOOR[d], T1, AF.Exp)
            hc_transpose_to(UCOL[d], U, 2)
            B.dma(bc_d[d], hcap[NEGM[d]], [t_hc[NEGM[d]]], [t_bcd[d]], eng="act")
            B.dma(bc_d[2 + d], hcap[FLOOR[d]], [t_hc[FLOOR[d]]], [t_bcd[2 + d]], eng="act")

    def proj_fm(wsl, wt, evac):
        for tb in range(NTB):
            tsl = slice(tb * 512, (tb + 1) * 512)
            bi = rbank()
            for kc in range(KC):
                B.mm(banks[bi][:], wsl[:, kc * 128:(kc + 1) * 128], hT[:, kc, tsl],
                     kc == 0, kc == KC - 1, [wt, t_h[tb]], [bk[bi]])
            evac(tb, tsl, bi)

    def out_norm(h, which):
        for tb in range(NTB):
            tsl = slice(tb * 512, (tb + 1) * 512)
            sl = tb % 2
            bi = 6 + sl
            B.act(sqb[:, sl, :], hacc[:, tsl], AF.Square, [t_hacc[tb]], [t_sqs[sl]])
            B.mm(banks[bi][:], onesb[:], sqb[:, sl, :], True, True, [t_sqs[sl], t_const], [bk[bi]])
            B.act(nrmA[:, sl, :], banks[bi][:], AF.Ln, [bk[bi], t_const], [t_rs[sl]], bias=epsc[:, 0:1],
                  scale=1.0 / 128)
            B.act(rstd[:, sl, :], nrmA[:, sl, :], AF.Exp, [t_rs[sl]], [t_rs[sl]], scale=-0.5)
            B.stt(tmpB[:, sl, :], hacc[:, tsl], onorm[:, which:which + 1], rstd[:, sl, :],
                  ALU.mult, ALU.mult, [t_hacc[tb], t_c2, t_rs[sl]], [t_fin[sl]])
            B.tt("dve", hXT[:, h * S + tb * 512: h * S + (tb + 1) * 512], tmpB[:, sl, :],
                 gateT[:, tsl], ALU.mult, [t_fin[sl], t_gate], [t_hx[h][tb]])

    def selector(bank_i, src_tile, h, tb):
        for r in range(4):
            hc_ = h * 16 + tb * 4 + r
            B.mm(banks[bank_i][:, r * 128:(r + 1) * 128],
                 ident[:, hc_:hc_ + 1].to_broadcast([128, 128]), hcap[src_tile], True, True,
                 [t_const, t_hc[src_tile]], [bk[bank_i]])

    pend_on = {"f": None}

    def mlstm_head(h):
        w, wt = w_next()
        proj_fm(w[:, 0:1024], wt, lambda tb, tsl, bi: B.act(
            qT[:, tsl], banks[bi][:], AF.Identity, [bk[bi]], [t_q], scale=128.0 ** -0.5))
        proj_fm(w[:, 1024:2048], wt, lambda tb, tsl, bi: B.cp(
            "dve", kT[:, tsl], banks[bi][:], [bk[bi]], [t_k]))
        if pend_on["f"] is not None:
            pend_on["f"]()
            pend_on["f"] = None
        w, wt = w_next()
        for tc in range(16):
            q4 = tc % 4
            for kc in range(KC):
                B.mm(banks[5][:, q4 * 128:(q4 + 1) * 128], hT[:, kc, tc * 128:(tc + 1) * 128],
                     w[:, kc * 128:(kc + 1) * 128], kc == 0, kc == KC - 1, [wt, t_h[tc // 4]],
                     [bk[5]])
            if q4 == 3:
                B.cp("act", v_tok[:, (tc - 3) * 128:(tc + 1) * 128], banks[5][:], [bk[5]],
                     [t_vtok])
        proj_fm(w[:, 1024:2048], wt, lambda tb, tsl, bi: B.act(
            gateT[:, tsl], banks[bi][:], AF.Sigmoid, [bk[bi]], [t_gate]))

        iters = [(0, tb) for tb in range(NTB)] + [(1, tb) for tb in range(NTB - 1, -1, -1)]
        bc_load(0, 0, h, 0)
        bc_load(1, 2, h, 0)
        bc_load(3, 2 + iters[1][0], h, iters[1][1])
        pend = {"g": None}

        def step_pending():
            if pend["g"] is not None:
                try:
                    next(pend["g"])
                except StopIteration:
                    pend["g"] = None

        def epilogue_gen(d, tb, tsl, bn_, bd_, FL, tFL):
            B.act(nrmA[:, 0, :], banks[bd_][:], AF.Abs, [bk[bd_]], [t_rs[0]])
            yield
            B.tt("dve", nrmA[:, 0, :], FL, nrmA[:, 0, :], ALU.max, [tFL, t_rs[0]], [t_rs[0]])
            yield
            B.act(nrmA[:, 0, :], nrmA[:, 0, :], AF.Ln, [t_rs[0]], [t_rs[0]])
            yield
            B.act(rstd[:, 0, :], nrmA[:, 0, :], AF.Exp, [t_rs[0]], [t_rs[0]], scale=-1.0)
            yield
            if d == 0:
                B.tt("dve", hacc[:, tsl], banks[bn_][:], rstd[:, 0, :], ALU.mult,
                     [bk[bn_], t_rs[0]], [t_hacc[tb]])
            else:
                B.tt("dve", nrmA[:, 0, :], banks[bn_][:], rstd[:, 0, :], ALU.mult,
                     [bk[bn_], t_rs[0]], [t_rs[0]])
                yield
                B.tt("pool", hacc[:, tsl], hacc[:, tsl], nrmA[:, 0, :], ALU.add,
                     [t_hacc[tb], t_rs[0]], [t_hacc[tb]])
            yield
        for it_, (d, tb) in enumerate(iters):
            if True:
                tsl = slice(tb * 512, (tb + 1) * 512)
                if it_ + 1 < len(iters):
                    nd, ntb = iters[it_ + 1]
                    bc_load(2 * ((it_ + 1) % 2), nd, h, ntb)
                bn_, bd_ = (3, 4) if it_ % 2 == 0 else (5, 6)
                NB, tNB = bcs[2 * (it_ % 2)], t_bc[2 * (it_ % 2)]
                FL, tFL = bcs[2 * (it_ % 2) + 1], t_bc[2 * (it_ % 2) + 1]
                scs = list(range(0, 4 * tb + 4)) if d == 0 else list(range(15, 4 * tb - 1, -1))
                tiles = []
                for sc in scs:
                    r = sc - 4 * tb
                    diag = 0 <= r <= 3
                    if d == 0:
                        c0, c1 = (r * 128 if diag else 0), 512
                    else:
                        c0, c1 = 0, ((r + 1) * 128 if diag else 512)
                    tiles.append((sc, r, diag, c0, c1))

                def emit_st(tile):
                    sc, r, diag, c0, c1 = tile
                    sbk = rbank()
                    B.mm(banks[sbk][:, c0:c1], kT[:, sc * 128:(sc + 1) * 128],
                         qT[:, tb * 512 + c0: tb * 512 + c1], True, True, [t_k, t_q], [bk[sbk]])
                    return sbk

                sbanks = {}

                def stage_ab(n_):
                    sc, r, diag, c0, c1 = tiles[n_]
                    sbanks[n_] = emit_st(tiles[n_])
                    hcs = h * 16 + sc
                    ucv = hcap[UCOL[d]][:, hcs:hcs + 1]
                    ds = n_ % 2
                    Dt = tmpA[:, ds, :]
                    if diag:
                        if d == 0:
                            t0, t1, r0, r1 = c0, c0 + 128, c0 + 128, c1
                        else:
                            t0, t1, r0, r1 = c1 - 128, c1, c0, c1 - 128
                        B.stt(tmpB[:, ds, 0:128], NB[:, t0:t1], ucv, tri[d], ALU.add, ALU.min,
                              [tNB, t_hc[UCOL[d]], t_c2], [t_fin[ds]])
                        B.act(Dt[:, t0:t1], tmpB[:, ds, 0:128], AF.Exp, [t_fin[ds]], [t_sil[ds]])
                        if r1 > r0:
                            B.act(Dt[:, r0:r1], NB[:, r0:r1], AF.Exp,
                                  [tNB, t_hc[UCOL[d]]], [t_sil[ds]], bias=ucv)
                    else:
                        B.act(Dt[:, c0:c1], NB[:, c0:c1], AF.Exp, [tNB, t_hc[UCOL[d]]],
                              [t_sil[ds]], bias=ucv)

                stage_ab(0)
                for n_, (sc, r, diag, c0, c1) in enumerate(tiles):
                    if n_ + 1 < len(tiles):
                        stage_ab(n_ + 1)
                    sbk = sbanks[n_]
                    ds = n_ % 2
                    Dt = tmpA[:, ds, :]
                    Pt = cb[ds]
                    B.tt("dve", Pt[:, c0:c1], banks[sbk][:, c0:c1], Dt[:, c0:c1], ALU.mult,
                         [bk[sbk], t_sil[ds]], [t_cb[ds]])
                    first = n_ == 0
                    last = n_ == len(tiles) - 1
                    B.mm(banks[bn_][:, c0:c1], v_tok[:, sc * 128:(sc + 1) * 128], Pt[:, c0:c1],
                         first, last, [t_vtok, t_cb[ds]], [bk[bn_]])
                    B.mm(banks[bd_][:, c0:c1], onesb[:], Pt[:, c0:c1], first, last,
                         [t_const, t_cb[ds]], [bk[bd_]])
                    if n_ >= 1:
                        step_pending()
                while pend["g"] is not None:
                    step_pending()
                if it_ >= 1 and it_ + 1 < len(iters):
                    nd, ntb = iters[it_ + 1]
                    bc_load(2 * ((it_ + 1) % 2) + 1, 2 + nd, h, ntb)
                pend["g"] = epilogue_gen(d, tb, tsl, bn_, bd_, FL, tFL)
        while pend["g"] is not None:
            step_pending()
        pend_on["f"] = lambda: out_norm(h, 0)

    def gt(d, i):
        return 8 + 6 * d + i

    def gdn_gates():
        P.add("pool", lambda e: e.memset(hcap[ZER], 0.0), [], [t_hc[ZER]])
        gate_project(256, [0, 1, 2, 3], [6, None, 7, None])
        for d in range(2):
            Ga, Gb = 2 * d, 2 * d + 1
            T1, SPt, L, X, BETA = 4, 5, 6, 7, 20
            GAM = gt(d, 0)
            softplus_acc(SPt, Ga, 1.0, T1, L, X)
            hts(X, SPt, nega[:, d:d + 1], None, ALU.mult, None, extra=[t_c2])
            if d == 0:
                scan(GAM, X, ALU.add)
                gl = 127
            else:
                scan(L, X, ALU.add)
                hts(T1, L, -1.0, hcap[L][:, 127:128], ALU.mult, ALU.add)
                htt(GAM, T1, X, ALU.add)
                gl = 0
            hact(BETA, Gb, AF.Sigmoid)
            B.dma(bc_d[4 + d], hcap[GAM], [t_hc[GAM]], [t_bcd[4 + d]], eng="act")
            hc_transpose_to(gt(d, 1), GAM, 0)
            hts(T1, BETA, -1.0, None, ALU.mult, None)
            hc_transpose_to(gt(d, 2), T1, 1)
            hc_transpose_to(gt(d, 3), BETA, 2)
            hact(SPt, GAM, AF.Exp)
            htt(T1, BETA, SPt, ALU.mult)
            hc_transpose_to(gt(d, 4), T1, 3)
            B.act(hcap[X], hcap[GAM], AF.Exp, [t_hc[GAM]], [t_hc[X]],
                  bias=hcap[GAM][:, gl:gl + 1], scale=-1.0)
            hc_transpose_to(gt(d, 5), X, 0)

    def conv_proj(wsl, wt, ch, kind):
        raw = hacc
        acc = msc[:, 7168:7168 + 2048]

        def tacc(tb):
            return [t_cb[2 * tb], t_cb[2 * tb + 1]]

        def proj(tb):
            tsl = slice(tb * 512, (tb + 1) * 512)
            bi = rbank()
            for kc in range(KC):
                B.mm(banks[bi][:], wsl[:, kc * 128:(kc + 1) * 128], hT[:, kc, tsl],
                     kc == 0, kc == KC - 1, [wt, t_h[tb]], [bk[bi]])
            B.cp("act", raw[:, tsl], banks[bi][:], [bk[bi]], [t_hacc[tb]])

        def conv(tb):
            lo, hi = tb * 512, (tb + 1) * 512
            B.ts("dve", acc[:, lo:hi], raw[:, lo:hi], cw[:, ch * 5 + 2: ch * 5 + 3], None,
                 ALU.mult, None, [t_hacc[tb], t_c2], tacc(tb))
            for j in (0, 1, 3, 4):
                dd = j - 2
                o0 = max(lo, -dd)
                o1 = min(hi, S - dd)
                rd = [t_hacc[tb], t_c2] + tacc(tb)
                if dd < 0 and tb > 0:
                    rd.append(t_hacc[tb - 1])
                if dd > 0 and tb < NTB - 1:
                    rd.append(t_hacc[tb + 1])
                B.stt(acc[:, o0:o1], raw[:, o0 + dd:o1 + dd], cw[:, ch * 5 + j: ch * 5 + j + 1],
                      acc[:, o0:o1], ALU.mult, ALU.add, rd, tacc(tb))

        def post(tb):
            tsl = slice(tb * 512, (tb + 1) * 512)
            sl = tb % 2
            if kind == "v":
                B.act(cb[8 + tb][:, :], acc[:, tsl], AF.Silu, tacc(tb), [t_cb[8 + tb]])
                return
            B.act(tmpA[:, sl, :], acc[:, tsl], AF.Silu, tacc(tb), [t_sil[sl]])
            B.act(sqb[:, sl, :], tmpA[:, sl, :], AF.Square, [t_sil[sl]], [t_sqs[sl]])
            bi = 6 + sl
            B.mm(banks[bi][:], onesb[:], sqb[:, sl, :], True, True, [t_sqs[sl], t_const], [bk[bi]])
            B.act(nrmA[:, sl, :], banks[bi][:], AF.Ln, [bk[bi], t_const], [t_rs[sl]],
                  bias=epsc[:, 0:1], scale=1.0)
            B.act(rstd[:, sl, :], nrmA[:, sl, :], AF.Exp, [t_rs[sl]], [t_rs[sl]], scale=-0.5)
            if kind == "q":
                B.stt(qT[:, tsl], tmpA[:, sl, :], 128.0 ** -0.5, rstd[:, sl, :], ALU.mult, ALU.mult,
                      [t_sil[sl], t_rs[sl]], [t_q])
            else:
                B.tt("dve", kT[:, tsl], tmpA[:, sl, :], rstd[:, sl, :], ALU.mult,
                     [t_sil[sl], t_rs[sl]], [t_k])

        proj(0)
        proj(1)
        conv(0)
        post(0)
        proj(2)
        conv(1)
        post(1)
        proj(3)
        conv(2)
        post(2)
        conv(3)
        post(3)

    def to_token_major(dst, t_dst, srcf, src_toks):
        pb = banks[5][:].bitcast(BF16)
        for tc in range(16):
            q4 = tc % 4
            B.tr(pb[:, q4 * 128:(q4 + 1) * 128], srcf(tc), identb[:], src_toks + [t_const], [bk[5]])
            if q4 == 3:
                B.cp("act", dst[:, (tc - 3) * 128:(tc + 1) * 128], pb[:, 0:512], [bk[5]], [t_dst])

    def gdn_head(h):
        w, wt = w_next()
        conv_proj(w[:, 0:1024], wt, h, "q")
        conv_proj(w[:, 1024:2048], wt, 8 + h, "k")
        to_token_major(k_tok, t_ktok, lambda tc: kT[:, tc * 128:(tc + 1) * 128], [t_k])
        w, wt = w_next()
        conv_proj(w[:, 0:1024], wt, 16 + h, "v")
        to_token_major(v_tok, t_vtok,
                       lambda tc: cb[8 + tc // 4][:, (tc % 4) * 128:(tc % 4 + 1) * 128],
                       t_cb[8:12])
        proj_fm(w[:, 1024:2048], wt, lambda tb, tsl, bi: B.act(
            gateT[:, tsl], banks[bi][:], AF.Silu, [bk[bi]], [t_gate]))

        def v4(ap):
            return ap.rearrange("p (r t) -> p r t", r=4)

        for tb in range(NTB):
            B.memset("pool", hacc[:, tb * 512:(tb + 1) * 512], 0.0, [], [t_hacc[tb]])

        def chain(d, R):
            GAM, GAMc, NEGBc, BETAc, BEGc, KDc = [gt(d, i) for i in range(6)]
            P0, P1, P2, P3 = R["banks"]
            cbs, tcs = R["cb"], R["tcb"]
            N0, Noff, Zt, Db, Tb, attnT, Ru, Rw, kd_, nWt, qdT = cbs[0:11]
            (tN0, tNoff, tZt, tDb, tTb, tattn, tRu, tRw, tkd, tnW, tqd) = tcs[0:11]
            E, tE = R["E"]
            XA, tXA = R["XA"]
            XT, tXT = R["XT"]
            EG, tEG = R["EG"]
            S_, tS_ = R["S"]
            Sb_, tSb_ = R["Sb"]
            vn_, tvn_ = R["vnb"]
            eg_, teg_ = R["egl"]
            B.memset("pool", S_, 0.0, [], [tS_])
            B.memset("pool", Sb_, 0.0, [], [tSb_])
            tbs = range(NTB) if d == 0 else range(NTB - 1, -1, -1)
            i4 = identb4[:].rearrange("p a b -> p (a b)")

            def mk(dd, lev):
                return gmaskb[:, dd * 7 + lev, :].unsqueeze(1).to_broadcast([128, 4, 128])

            tbl = list(tbs)
            cix = R["banks"][0] // 4
            bc_load(2 * cix, 4 + d, h, tbl[0])
            for k_, tb in enumerate(tbl):
                tsl = slice(tb * 512, (tb + 1) * 512)
                hc0 = h * 16 + tb * 4
                if k_ + 1 < len(tbl):
                    bc_load(2 * cix + (k_ + 1) % 2, 4 + d, h, tbl[k_ + 1])
                GB, tGB = bcs[2 * cix + k_ % 2], t_bc[2 * cix + k_ % 2]

                def colb(tile):
                    return hcap[tile][:, hc0:hc0 + 4].unsqueeze(2).to_broadcast([128, 4, 128])

                for r in range(4):
                    csl = slice(tb * 512 + r * 128, tb * 512 + (r + 1) * 128)
                    B.mm(banks[P1][:, r * 128:(r + 1) * 128], kT[:, csl], kT[:, csl], True, True,
                         [t_k], [bk[P1]])
                B.tt("dve", v4(E), v4(GB), colb(GAMc), ALU.subtract,
                     [tGB, t_hc[GAMc]], [tE])
                B.tt("dve", v4(XA), v4(E), bigA[d].unsqueeze(1).to_broadcast([128, 4, 128]),
                     ALU.max, [tE, t_c2], [tXA])
                B.act(XA, XA, AF.Exp, [tXA], [tXA], scale=-1.0)
                B.tt("dve", v4(XA), v4(XA), colb(NEGBc), ALU.mult, [tXA, t_hc[NEGBc]], [tXA])
                B.tt("dve", N0, banks[P1][:], XA, ALU.mult, [bk[P1], tXA], [tN0])
                yield
                for r in range(4):
                    csl = slice(tb * 512 + r * 128, tb * 512 + (r + 1) * 128)
                    B.mm(banks[P1][:, r * 128:(r + 1) * 128], kT[:, csl], qT[:, csl], True, True,
                         [t_k, t_q], [bk[P1]])
                B.tt("dve", v4(XT), v4(E), tri[d].unsqueeze(1).to_broadcast([128, 4, 128]),
                     ALU.min, [tE, t_c2], [tXT])
                B.act(XT, XT, AF.Exp, [tXT], [tXT])
                B.tt("dve", attnT, banks[P1][:], XT, ALU.mult, [bk[P1], tXT], [tattn])
                B.act(EG, GB, AF.Exp, [tGB], [tEG])
                B.tt("pool", qdT, qT[:, tsl], EG, ALU.mult, [t_q, tEG], [tqd])
                gl = 127 if d == 0 else 0
                B.act(eg_, v4(GB)[:, :, gl], AF.Exp, [tGB], [teg_])
                yield
                pb1 = banks[P1][:].bitcast(BF16)
                for r in range(4):
                    B.tr(pb1[:, r * 128:(r + 1) * 128], N0[:, r * 128:(r + 1) * 128], identb[:],
                         [tN0, t_const], [bk[P1]])
                B.cp("act", Zt, pb1[:, 0:512], [bk[P1]], [tZt])
                B.tt("dve", v4(Zt), v4(Zt), mk(1 - d, 0), ALU.mult, [tZt, t_c2], [tZt])
                B.tt("dve", Tb, Zt, i4, ALU.add, [tZt, t_c2], [tTb])
                B.mm(banks[P2][:], identb[:], Tb, True, True, [t_const, tTb], [bk[P2]], skip=True)
                B.tt("dve", v4(Noff), v4(N0), mk(d, 0), ALU.mult, [tN0, t_c2], [tNoff])
                B.tt("dve", Db, Noff, i4, ALU.add, [tNoff, t_c2], [tDb])
                B.mm(banks[P1][:], identb[:], Db, True, True, [t_const, tDb], [bk[P1]], skip=True)
                yield
                for lev in range(1, 7):
                    for r in range(4):
                        rs_ = slice(r * 128, (r + 1) * 128)
                        B.mm(banks[P0][:, rs_], N0[:, rs_], Tb[:, rs_], True, True,
                             [tN0, tTb], [bk[P0]])
                    B.tt("dve", v4(Zt), v4(banks[P0][:]), mk(1 - d, lev), ALU.mult,
                         [bk[P0], t_c2], [tZt])
                    yield
                    for r in range(4):
                        rs_ = slice(r * 128, (r + 1) * 128)
                        B.mm(banks[P2][:, rs_], Db[:, rs_], Zt[:, rs_], False, True,
                             [tDb, tZt], [bk[P2]], skip=True)
                    if lev < 6:
                        for r in range(4):
                            rs_ = slice(r * 128, (r + 1) * 128)
                            B.mm(banks[P1][:, rs_], Zt[:, rs_], Db[:, rs_], False, True,
                                 [tDb, tZt], [bk[P1]], skip=True)
                    B.cp("act", Tb, banks[P2][:], [bk[P2]], [tTb])
                    if lev < 6:
                        B.cp("dve", Db, banks[P1][:], [bk[P1]], [tDb])
                    yield
                B.tt("pool", v4(Ru), v4(v_tok[:, tsl]), colb(BETAc), ALU.mult,
                     [t_vtok, t_hc[BETAc]], [tRu])
                B.tt("pool", v4(Rw), v4(k_tok[:, tsl]), colb(BEGc), ALU.mult,
                     [t_ktok, t_hc[BEGc]], [tRw])
                B.tt("pool", v4(kd_), v4(k_tok[:, tsl]), colb(KDc), ALU.mult,
                     [t_ktok, t_hc[KDc]], [tkd])
                for r in range(4):
                    rs_ = slice(r * 128, (r + 1) * 128)
                    B.mm(banks[P0][:, rs_], Rw[:, rs_], Tb[:, rs_], True, True, [tRw, tTb],
                         [bk[P0]])
                B.act(nWt, banks[P0][:], AF.Identity, [bk[P0]], [tnW], scale=-1.0)
                yield
                rr = range(4) if d == 0 else range(3, -1, -1)
                for r in rr:
                    rs_ = slice(r * 128, (r + 1) * 128)
                    B.mm(banks[P1][:, 0:128], Tb[:, rs_], Ru[:, rs_], True, False, [tTb, tRu],
                         [bk[P1]])
                    B.mm(banks[P1][:, 0:128], nWt[:, rs_], Sb_, False, True, [tnW, tSb_], [bk[P1]])
                    B.cp("act", vn_, banks[P1][:, 0:128], [bk[P1]], [tvn_])
                    yield
                    B.mm(banks[P3][:, rs_], Sb_, qdT[:, rs_], True, False, [tSb_, tqd], [bk[P3]])
                    B.mm(banks[P3][:, rs_], vn_, attnT[:, rs_], False, True, [tvn_, tattn],
                         [bk[P3]])
                    B.mm(banks[P1][:, 128:256], kd_[:, rs_], vn_, True, True, [tkd, tvn_],
                         [bk[P1]])
                    B.stt(S_, S_, eg_[:, r:r + 1], banks[P1][:, 128:256], ALU.mult, ALU.add,
                          [tS_, teg_, bk[P1]], [tS_])
                    B.cp("act", Sb_, S_, [tS_], [tSb_])
                    yield
                B.tt("dve", hacc[:, tsl], banks[P3][:], hacc[:, tsl], ALU.add,
                     [bk[P3], t_hacc[tb]], [t_hacc[tb]])
                yield

        gens = [chain(0, chainR[0]), chain(1, chainR[1])]
        while gens:
            for g in list(gens):
                try:
                    next(g)
                except StopIteration:
                    gens.remove(g)
        out_norm(h, 1)

    def branch_phase():
        inherit(y_flat + t_xm, msc_toks)
        for m in range(KC):
            w, wt = w_next()
            for tb in range(NTB):
                tsl = slice(tb * 512, (tb + 1) * 512)
                for kc in range(KC):
                    B.mm(banks[0][:], w[:, kc * 128:(kc + 1) * 128], hT[:, kc, tsl], kc == 0,
                         kc == KC - 1, [wt, t_h[tb]], [bk[0]])
                for kh in range(8):
                    B.mm(banks[1][:], w[:, 1024 + kh * 128:1024 + (kh + 1) * 128],
                         hXT[:, kh * S + tb * 512: kh * S + (tb + 1) * 512], kh == 0, kh == 7,
                         [wt, t_hx[kh][tb]], [bk[1]])
                sl = tb % 2
                B.act(tmpA[:, sl, :], banks[0][:], AF.Sigmoid, [bk[0]], [t_sil[sl]])
                B.tt("dve", yT[:, m * S + tb * 512: m * S + (tb + 1) * 512], banks[1][:],
                     tmpA[:, sl, :], ALU.mult, [bk[1], t_sil[sl]], [t_y[m][tb]])
        for mm_ in range(4):
            w, wt = w_next()
            for half in range(2):
                m = 2 * mm_ + half
                sl = m % 2
                B.dma(xm[:, sl, :], xsp_d[:, m, :], [t_xsp[m]], [t_xm[sl]], eng="act")
                for tb in range(NTB):
                    tsl = slice(tb * 512, (tb + 1) * 512)
                    bi = 2 + (tb % 2)
                    for kc in range(KC):
                        B.mm(banks[bi][:], w[:, half * 1024 + kc * 128: half * 1024 + (kc + 1) * 128],
                             yT[:, kc * S + tb * 512: kc * S + (tb + 1) * 512], kc == 0,
                             kc == KC - 1, [wt, t_y[kc][tb]], [bk[bi]])
                    B.stt(xm[:, sl, tsl], banks[bi][:], gcol[:, KC + m:KC + m + 1], xm[:, sl, tsl],
                          ALU.mult, ALU.add, [bk[bi], t_mod, t_xm[sl]], [t_xm[sl]])
                B.dma(xsp_d[:, m, :], xm[:, sl, :], [t_xm[sl]], [t_xsp[m]], eng="act")
        inherit(msc_toks, y_flat + t_xm)

    t_xsp = [Tok() for _ in range(KC)]

    def mixer(which):
        nonlocal arena_toks
        for tb in range(NTB):
            norm_block(1, tb)
        x_flat = [t for row in t_x for t in row]
        for kc in range(KC):
            B.dma(xsp_d[:, kc, :], xT[:, kc, :], t_x[kc], [t_xsp[kc]], eng="sp")
        inherit(msc_toks + y_flat + t_xm, x_flat)
        inherit(hx_flat + t_hc, arena_toks)
        w, wt = w_next()
        B.cp("pool", wgb[:], w[:, 0:512], [wt], [t_c2])
        if which in ("all", "mlstm"):
            mlstm_gates()
            for h in range(8):
                mlstm_head(h)
            pend_on["f"]()
            pend_on["f"] = None
        else:
            for h in range(8):
                w_next()
                w_next()
            for tk in hx_flat:
                pass
            P.add("pool", lambda e: e.memset(hXT, 0.0), [], hx_flat)
        branch_phase()
        if which in ("all", "gdn"):
            gdn_gates()
            for h in range(8):
                gdn_head(h)
        else:
            for h in range(8):
                w_next()
                w_next()
            P.add("pool", lambda e: e.memset(hXT, 0.0), hx_flat, hx_flat)
        branch_phase()
        inherit(x_flat, msc_toks + y_flat + t_xm)
        for kc in range(KC):
            B.dma(xT[:, kc, :], xsp_d[:, kc, :], [t_xsp[kc]], t_x[kc], eng="sp")
        arena_toks = hx_flat + t_hc

    if debug == "nomix":
        ffn(0)
        ffn(2)
        for tb in range(NTB):
            norm_block(3, tb, final=True)
    elif debug in ("mlstm", "gdn", "mixonly"):
        ffn(0)
        mixer({"mlstm": "mlstm", "gdn": "gdn", "mixonly": "all"}[debug])
        for kc in range(KC):
            t_o = Tok()
            out_toks.append(t_o)
            B.dma(out_d[:, kc, :], xT[:, kc, :], t_x[kc], [t_o])
    else:
        ffn(0)
        mixer("all")
        ffn(2)
        for tb in range(NTB):
            norm_block(3, tb, final=True)
    P.add("sp", None, out_toks, [])

    pool_sz = {"pe": 1, "act": 8, "dve": 1, "pool": 8, "sp": 16}
    P.finalize(pool_sz)
    engsem = {e: B.es.enter_context(nc.semaphore("sem_" + e)) for e in ENGS}
    dmasem = {e: [B.es.enter_context(nc.semaphore("dsem_%s_%d" % (e, i)))
                  for i in range(pool_sz[e])] for e in ("act", "pool", "sp")}
    with nc.Block() as block:
        @block.tensor
        def _(e):
            P.emit("pe", e, engsem, dmasem)

        @block.scalar
        def _(e):
            P.emit("act", e, engsem, dmasem)

        @block.vector
        def _(e):
            P.emit("dve", e, engsem, dmasem)

        @block.gpsimd
        def _(e):
            P.emit("pool", e, engsem, dmasem)

        @block.sync
        def _(e):
            P.emit("sp", e, engsem, dmasem)
    B.es.close()
    return nc


def _col(v):
    return np.ascontiguousarray(v.reshape(-1, 128).T)


def _chunk(w, c0):
    return w[:, c0:c0 + 128].reshape(KC, 128, 128).transpose(1, 0, 2).reshape(128, 1024)


def host_layout(inp):
    shared = {}
    w_ada = inp["w_ada"][0]
    shared["wada"] = np.ascontiguousarray(
        w_ada.reshape(KC, 128, 18, 512).transpose(2, 1, 0, 3).reshape(18, 128, KC * 512))
    shared["bada"] = np.ascontiguousarray(inp["b_ada"][0].reshape(1, 9216))
    shared["normw"] = np.ascontiguousarray(np.concatenate(
        [_col(inp["norm_ffn1"][0]), _col(inp["norm_mix"][0]), _col(inp["norm_ffn2"][0]),
         _col(inp["norm_final"])], axis=1))

    def ffn_in(w):
        return np.ascontiguousarray(
            w.reshape(KC, 128, 2, NJ, 128).transpose(3, 1, 2, 0, 4).reshape(NJ, 128, 2048))

    def ffn_out(w):
        return np.ascontiguousarray(
            w.reshape(2, GJ, 128, KC, 128).transpose(0, 3, 2, 1, 4).reshape(16, 128, GJ * 128))

    shared["wf1i"] = ffn_in(inp["w_ffn1_in"][0])
    shared["wf1o"] = ffn_out(inp["w_ffn1_out"][0])
    shared["wf2i"] = ffn_in(inp["w_ffn2_in"][0])
    shared["wf2o"] = ffn_out(inp["w_ffn2_out"][0])
    shared["ident"] = np.eye(128, dtype=np.float32)

    win = inp["w_in"][0]
    wbm = inp["w_branch_mlstm"][0]
    wbg = inp["w_branch_gdn"][0]
    wo = inp["w_out"][0]
    O_MQ, O_MK, O_MV, O_MO, O_MG = 0, 1024, 2048, 3072, 4096
    O_GQ, O_GK, O_GV, O_GZ, O_GG = 4128, 5152, 6176, 7200, 8224
    O_MM, O_MGG = 8256, 9280
    units = []
    for h in range(8):
        units.append(np.concatenate([_chunk(win, O_MQ + h * 128), _chunk(win, O_MK + h * 128)], 1))
    for h in range(8):
        units.append(np.concatenate([_chunk(win, O_MV + h * 128), _chunk(win, O_MO + h * 128)], 1))
    for m in range(8):
        units.append(np.concatenate([_chunk(win, O_MM + m * 128), _chunk(wbm, m * 128)], 1))
    for mm_ in range(4):
        units.append(np.concatenate([_chunk(wo, 2 * mm_ * 128), _chunk(wo, (2 * mm_ + 1) * 128)], 1))
    for h in range(8):
        units.append(np.concatenate([_chunk(win, O_GQ + h * 128), _chunk(win, O_GK + h * 128)], 1))
    for h in range(8):
        units.append(np.concatenate([_chunk(win, O_GV + h * 128), _chunk(win, O_GZ + h * 128)], 1))
    for m in range(8):
        units.append(np.concatenate([_chunk(win, O_MGG + m * 128), _chunk(wbg, m * 128)], 1))
    shared["winu"] = np.ascontiguousarray(np.stack(units, 0))

    def gchunk(c0):
        return win[:, c0:c0 + 32].reshape(KC, 128, 32).transpose(1, 0, 2).reshape(128, 256)

    shared["wgates"] = np.ascontiguousarray(np.concatenate([gchunk(O_MG), gchunk(O_GG)], 1))
    hidx = np.arange(128) // 16
    gbias = np.zeros((128, 8), np.float32)
    mgb = inp["mlstm_gate_bias"][0]
    for g in range(4):
        gbias[:, g] = mgb[g][hidx]
    gbias[:, 4] = inp["gdn_a_log"][0][0][hidx]
    gbias[:, 5] = inp["gdn_a_log"][0][1][hidx]
    gbias[:, 6] = inp["gdn_dt_bias"][0][0][hidx]
    gbias[:, 7] = inp["gdn_dt_bias"][0][1][hidx]
    shared["gbias"] = gbias
    cwv = inp["gdn_conv_w"][0]
    shared["convw"] = np.ascontiguousarray(
        cwv.reshape(5, 24, 128).transpose(2, 1, 0).reshape(128, 120))
    shared["onorm"] = np.ascontiguousarray(
        np.stack([inp["mlstm_out_norm"][0], inp["gdn_out_norm"][0]], 1))
    p = np.arange(128)[:, None]
    j = np.arange(128)[None, :]
    tri_f = np.where(p <= j, 0.0, -BIG)
    tri_b = np.where(p >= j, 0.0, -BIG)
    big_f = np.where(j < p, 0.0, BIG)
    big_b = np.where(j > p, 0.0, BIG)
    shared["cmask"] = np.ascontiguousarray(
        np.stack([tri_f, tri_b, big_f, big_b], 1).reshape(128, 512).astype(np.float32))
    hk, ck = p // 16, p % 16
    hm, cm_ = j // 16, j % 16
    lf = ((hk == hm) & (ck < cm_)).astype(np.float32)
    lb = ((hk == hm) & (ck > cm_)).astype(np.float32)
    shared["clm"] = np.ascontiguousarray(np.stack([lf, lb], 1).reshape(128, 256))
    gms = []
    for dd in range(2):
        for lev in range(7):
            bsz = 1 << lev
            same = (p // (2 * bsz)) == (j // (2 * bsz))
            if dd == 0:
                mk_ = same & (p % (2 * bsz) >= bsz) & (j % (2 * bsz) < bsz)
            else:
                mk_ = same & (p % (2 * bsz) < bsz) & (j % (2 * bsz) >= bsz)
            gms.append(mk_)
    shared["gmask"] = np.ascontiguousarray(
        np.stack(gms, 1).reshape(128, 14 * 128).astype(ml_dtypes.bfloat16))
    in_maps = []
    for b in range(NCORES):
        m = dict(shared)
        m["xT"] = np.ascontiguousarray(inp["x"][b].T.reshape(KC, 128, S).transpose(1, 0, 2))
        m["ccol"] = _col(inp["c"][b])
        in_maps.append(m)
    return in_maps


_NC_CACHE = {}
DEBUG = None


def kernel(**inputs):
    inp = {k: np.asarray(v, dtype=np.float32) for k, v in inputs.items()}
    in_maps = host_layout(inp)
    if "nc" not in _NC_CACHE:
        _NC_CACHE["nc"] = build(DEBUG)
    nc = _NC_CACHE["nc"]
    res = run_bass_kernel_spmd(nc, in_maps, core_ids=list(range(NCORES)))
    out = np.empty((NCORES, S, D), np.float32)
    for b in range(NCORES):
        oT = np.asarray(res.results[b]["outT"]).reshape(128, KC, S)
        out[b] = oT.transpose(1, 0, 2).reshape(D, S).T
    return out
```

```python
import numpy as np
import ml_dtypes
from contextlib import ExitStack
import concourse.bass as bass
import concourse.mybir as mybir
from concourse.bass_utils import run_bass_kernel_spmd

F32 = mybir.dt.float32
BF16 = mybir.dt.bfloat16
AF = mybir.ActivationFunctionType
ALU = mybir.AluOpType

D = 1024
S = 2048
KC = 8
NTB = 4
DFF = 2816
NJ = 22
GJ = 11
EPS = 1e-6
NCORES = 8

ENGS = ("pe", "act", "dve", "pool", "sp")


class Tok:
    __slots__ = ("w", "r")

    def __init__(self):
        self.w = None
        self.r = {}


class Op:
    __slots__ = ("eng", "fn", "dma", "sig", "sigval", "idx", "deps", "dsem", "dval")


class Prog:
    def __init__(self):
        self.lists = {e: [] for e in ENGS}
        self.count = 0
        self.dma_ops = {e: [] for e in ENGS}

    def add(self, eng, fn, reads=(), writes=(), dma=False):
        o = Op()
        o.eng = eng
        o.fn = fn
        o.dma = dma
        o.sig = False
        o.sigval = 0
        o.idx = self.count
        self.count += 1
        deps = {}

        def dep(p, raw):
            if p is None:
                return
            if not p.dma and not dma and p.eng == eng and eng == "pe":
                return
            deps[p.idx] = p

        for t in reads:
            dep(t.w, True)
        for t in writes:
            dep(t.w, False)
            for p in t.r.values():
                dep(p, False)
        for t in reads:
            key = ("d", o.idx) if dma else eng
            t.r[key] = o
        for t in writes:
            t.w = o
            t.r = {}
        if dma:
            lst = self.dma_ops[eng]
            o.dsem = None
            lst.append(o)
        o.deps = list(deps.values())
        self.lists[eng].append(o)
        return o

    def finalize(self, dma_pool_size):
        for e in ENGS:
            lst = self.dma_ops[e]
            K = dma_pool_size[e]
            for i, o in enumerate(lst):
                o.dsem = (e, i % K)
                o.dval = 16 * (i // K + 1)
                if i >= K:
                    o.deps.append(lst[i - K])
        for e in ENGS:
            for o in self.lists[e]:
                for p in o.deps:
                    if not p.dma:
                        p.sig = True
        for e in ENGS:
            n = 0
            for o in self.lists[e]:
                if o.sig and not o.dma:
                    n += 1
                    o.sigval = n

    def emit(self, ename, eng, engsem, dmasem):
        waited = {}
        for o in self.lists[ename]:
            need = {}
            for p in o.deps:
                if p.dma:
                    key = ("d",) + p.dsem
                    val = p.dval
                else:
                    key = ("e", p.eng)
                    val = p.sigval
                if need.get(key, 0) < val:
                    need[key] = val
            for key, val in need.items():
                if waited.get(key, 0) < val:
                    sem = dmasem[key[1]][key[2]] if key[0] == "d" else engsem[key[1]]
                    eng.wait_ge(sem, val)
                    waited[key] = val
            if o.fn is None:
                continue
            ins = o.fn(eng)
            if o.dma:
                ins.then_inc(dmasem[o.dsem[0]][o.dsem[1]], 16)
            elif o.sig:
                ins.then_inc(engsem[ename], 1)


def _host_consts():
    c = {}
    ident = np.eye(128, dtype=np.float32)
    c["ident"] = ident
    c["ones"] = np.ones((128, 128), np.float32)
    return c


class Builder:
    def __init__(self, debug=None):
        self.debug = debug
        self.nc = bass.Bass("TRN2", target_bir_lowering=False)
        self.P = Prog()
        self.es = ExitStack()
        self.dram = {}
        self.outs = {}

    def din(self, name, shape, dtype=F32):
        t = self.nc.dram_tensor(name, list(shape), dtype, kind="ExternalInput").ap()
        self.dram[name] = t
        return t

    def dout(self, name, shape, dtype=F32):
        t = self.nc.dram_tensor(name, list(shape), dtype, kind="ExternalOutput").ap()
        self.outs[name] = t
        return t

    def sb(self, name, shape, dtype=F32):
        h = self.es.enter_context(self.nc.sbuf_tensor(name, list(shape), dtype))
        return h

    def ps(self, name, shape, dtype=F32):
        h = self.es.enter_context(self.nc.psum_tensor(name, list(shape), dtype))
        return h

    def mm(self, out, lhsT, rhs, start, stop, reads, writes, skip=False):
        if skip:
            return self.P.add("pe", lambda e: e.matmul(out, lhsT, rhs, start=start, stop=stop,
                                                       skip_group_check=True), reads, writes)
        return self.P.add("pe", lambda e: e.matmul(out, lhsT, rhs, start=start, stop=stop),
                          reads, writes)

    def tr(self, out, in_, ident, reads, writes):
        return self.P.add("pe", lambda e: e.transpose(out, in_, ident), reads, writes)

    def act(self, out, in_, func, reads, writes, bias=0.0, scale=1.0, eng="act"):
        return self.P.add(eng, lambda e: e.activation(out, in_, func, bias=bias, scale=scale),
                          reads, writes)

    def tt(self, eng, out, in0, in1, op, reads, writes):
        return self.P.add(eng, lambda e: e.tensor_tensor(out, in0, in1, op), reads, writes)

    def stt(self, out, in0, scalar, in1, op0, op1, reads, writes, eng="dve"):
        return self.P.add(eng, lambda e: e.scalar_tensor_tensor(out, in0, scalar, in1, op0, op1),
                          reads, writes)

    def ts(self, eng, out, in0, s1, s2, op0, op1, reads, writes):
        if s2 is None:
            return self.P.add(eng, lambda e: e.tensor_scalar(out, in0, s1, None, op0), reads, writes)
        return self.P.add(eng, lambda e: e.tensor_scalar(out, in0, s1, s2, op0, op1), reads, writes)

    def cp(self, eng, out, in_, reads, writes):
        if eng == "act":
            return self.P.add(eng, lambda e: e.copy(out, in_), reads, writes)
        return self.P.add(eng, lambda e: e.tensor_copy(out, in_), reads, writes)

    def recip(self, out, in_, reads, writes):
        return self.P.add("dve", lambda e: e.reciprocal(out, in_), reads, writes)

    def memset(self, eng, ap, val, reads, writes):
        return self.P.add(eng, lambda e: e.memset(ap, val), reads, writes)

    def dma(self, out, in_, reads, writes, eng="sp"):
        return self.P.add(eng, lambda e: e.dma_start(out, in_), reads, writes, dma=True)


BIG = 30000.0
NHC = 24
NCB = 12
MSC_XM = 10496


def inherit(new_toks, old_toks):
    ops = {}
    for t in old_toks:
        if t.w is not None:
            ops[t.w.idx] = t.w
        for p in t.r.values():
            ops[p.idx] = p
    for nt in new_toks:
        for i, p in ops.items():
            nt.r[("f", i)] = p


def build(debug=None):
    B = Builder(debug)
    nc = B.nc
    P = B.P

    xT_d = B.din("xT", [128, KC, S])
    c_d = B.din("ccol", [128, KC])
    wada_d = B.din("wada", [18, 128, KC * 512])
    bada_d = B.din("bada", [1, 9216])
    normw_d = B.din("normw", [128, 4 * KC])
    wf1i_d = B.din("wf1i", [NJ, 128, 2048])
    wf1o_d = B.din("wf1o", [16, 128, GJ * 128])
    wf2i_d = B.din("wf2i", [NJ, 128, 2048])
    wf2o_d = B.din("wf2o", [16, 128, GJ * 128])
    ident_d = B.din("ident", [128, 128])
    win_d = B.din("winu", [52, 128, 2048])
    wg_d = B.din("wgates", [128, 512])
    gb_d = B.din("gbias", [128, 8])
    cw_d = B.din("convw", [128, 120])
    on_d = B.din("onorm", [128, 2])
    cm_d = B.din("cmask", [128, 512])
    lm_d = B.din("clm", [128, 256])
    gm_d = B.din("gmask", [128, 14 * 128], BF16)
    out_d = B.dout("outT", [128, KC, S])
    xsp_d = nc.dram_tensor("xspill", [128, KC, S], F32, kind="Internal").ap()
    bc_d = nc.dram_tensor("bcd", [6, 128, 128], F32, kind="Internal").ap()

    xT = B.sb("xT_sb", [128, KC, S])
    hT = B.sb("hT_sb", [128, KC, S], BF16)
    arena = B.sb("arena", [128, 11264])
    NST = 2
    NBF = 3
    wst = B.sb("wst", [128, NST, 2048])
    wbf = B.sb("wbf", [128, NBF, 2048], BF16)
    modrow = B.sb("modrow", [1, 2, 512])
    brow = B.sb("brow", [1, 2, 512])
    ccol = B.sb("ccol_sb", [128, KC])
    cs = B.sb("cs_sb", [128, KC])
    modT = B.sb("modT", [128, 72])
    normw = B.sb("normw_sb", [128, 4 * KC])
    acol = B.sb("acol", [128, 3 * KC])
    gcol = B.sb("gcol", [128, 3 * KC])
    ident = B.sb("ident_sb", [128, 128])
    identb = B.sb("identb_sb", [128, 128], BF16)
    identb4 = B.sb("identb4", [128, 4, 128], BF16)
    onesb = B.sb("onesb", [128, 128], BF16)
    one11 = B.sb("one11", [1, 2])
    epsc = B.sb("epsc", [128, 1])
    tmpA = B.sb("tmpA", [128, 2, 512])
    tmpB = B.sb("tmpB", [128, 2, 512])
    rstd = B.sb("rstd", [128, 2, 512])
    nrmA = B.sb("nrmA", [128, 2, 512])
    sqb = B.sb("sqb", [128, 2, 512], BF16)
    wgb = B.sb("wgb", [128, 512], BF16)
    gb = B.sb("gb_sb", [128, 8])
    nega = B.sb("nega", [128, 2])
    cw = B.sb("cw_sb", [128, 120])
    onorm = B.sb("onorm_sb", [128, 2])
    cmask = B.sb("cmask_sb", [128, 4, 128])
    clm = B.sb("clm_sb", [128, 2, 128])
    eglt = B.sb("eglt", [128, 2, 4])
    gmaskb = B.sb("gmaskb", [128, 14, 128], BF16)

    banks = [B.ps("bank%d" % i, [128, 512]) for i in range(8)]
    bk = [Tok() for _ in range(8)]

    t_x = [[Tok() for _ in range(NTB)] for _ in range(KC)]
    t_h = [Tok() for _ in range(NTB)]
    t_const = Tok()
    t_mod = Tok()
    t_fin = [Tok(), Tok()]
    t_sqs = [Tok(), Tok()]
    t_rs = [Tok(), Tok()]
    t_sil = [Tok(), Tok()]
    out_toks = []

    B.dma(ident[:], ident_d, [], [t_const])
    B.dma(ccol[:], c_d, [], [t_const])
    B.dma(normw[:], normw_d, [], [t_const])
    P.add("pool", lambda e: e.memset(onesb[:], 1.0), [], [t_const])
    P.add("pool", lambda e: e.memset(one11[:], 1.0), [], [t_const])
    P.add("pool", lambda e: e.memset(epsc[:], EPS), [], [t_const])
    B.cp("pool", identb[:], ident[:], [t_const], [t_const])
    t_c2 = Tok()
    B.dma(gb[:], gb_d, [], [t_c2], eng="act")
    B.dma(cw[:], cw_d, [], [t_c2], eng="act")
    B.dma(onorm[:], on_d, [], [t_c2], eng="act")
    B.dma(cmask[:].rearrange("p a b -> p (a b)"), cm_d, [], [t_c2], eng="act")
    B.dma(clm[:].rearrange("p a b -> p (a b)"), lm_d, [], [t_c2], eng="act")
    B.dma(gmaskb[:].rearrange("p a b -> p (a b)"), gm_d, [], [t_c2], eng="act")
    for r in range(4):
        B.cp("pool", identb4[:, r, :], ident[:], [t_const], [t_c2])
    B.act(nega[:], gb[:, 4:6], AF.Exp, [t_c2], [t_c2])
    B.ts("dve", nega[:], nega[:], -1.0, None, ALU.mult, None, [t_c2], [t_c2])

    for kc in range(KC):
        B.dma(xT[:, kc, :], xT_d[:, kc, :], [], t_x[kc], eng="act" if kc % 2 else "sp")

    t_cs = Tok()
    B.act(cs[:], ccol[:], AF.Silu, [t_const], [t_cs])
    t_ada = [Tok(), Tok()]
    t_brow = [Tok(), Tok()]
    t_modrow = [Tok(), Tok()]
    adabuf = arena[:, 0:2 * KC * 512].rearrange("p (s n) -> p s n", s=2)
    for blk in range(18):
        sl = blk % 2
        B.dma(adabuf[:, sl, :], wada_d[blk], [], [t_ada[sl]], eng="sp")
        B.dma(brow[0:1, sl, :], bada_d[0:1, blk * 512:(blk + 1) * 512], [], [t_brow[sl]], eng="act")
        b = banks[sl]
        for kc in range(KC):
            B.mm(b[0:1, :], cs[:, kc:kc + 1], adabuf[:, sl, kc * 512:(kc + 1) * 512],
                 kc == 0, False, [t_cs, t_ada[sl]], [bk[sl]])
        B.mm(b[0:1, :], one11[0:1, 0:1], brow[0:1, sl, :], False, True,
             [t_const, t_brow[sl]], [bk[sl]])
        B.cp("act", modrow[0:1, sl, :], b[0:1, :], [bk[sl]], [t_modrow[sl]])
        for q in range(4):
            j = blk * 4 + q
            B.mm(banks[2][:, j:j + 1], modrow[0:1, sl, q * 128:(q + 1) * 128], one11[0:1, 0:1],
                 True, True, [t_modrow[sl], t_const], [bk[2]])
    B.cp("dve", modT[:], banks[2][:, 0:72], [bk[2]], [t_mod])
    for a in range(3):
        B.stt(acol[:, a * KC:(a + 1) * KC], modT[:, (a * 3 + 1) * KC:(a * 3 + 2) * KC], 1.0,
              normw[:, a * KC:(a + 1) * KC], ALU.add, ALU.mult, [t_mod, t_const], [t_mod])
        B.ts("dve", gcol[:, a * KC:(a + 1) * KC], modT[:, (a * 3 + 2) * KC:(a * 3 + 3) * KC],
             0.5 if a != 1 else 1.0, None, ALU.mult, None, [t_mod], [t_mod])
    arena_toks = list(t_ada)

    st_tok = [Tok() for _ in range(NST)]
    bf_tok = [Tok() for _ in range(NBF)]
    wq = []
    wstate = {"loaded": 0, "used": 0}
    PF = 3

    def w_issue():
        i = wstate["loaded"]
        src, n = wq[i]
        s_ = i % NST
        b_ = i % NBF
        B.dma(wst[:, s_, 0:n], src, [], [st_tok[s_]], eng="sp")
        B.cp("pool", wbf[:, b_, 0:n], wst[:, s_, 0:n], [st_tok[s_]], [bf_tok[b_]])
        wstate["loaded"] += 1

    def w_next():
        i = wstate["used"]
        while wstate["loaded"] < min(len(wq), i + PF):
            w_issue()
        wstate["used"] += 1
        return wbf[:, i % NBF, :], bf_tok[i % NBF]

    def declare_ffn(wi, wo):
        for g in range(2):
            for jj in range(GJ):
                wq.append((wi[g * GJ + jj], 2048))
            for m in range(KC):
                wq.append((wo[g * KC + m], GJ * 128))

    declare_ffn(wf1i_d, wf1o_d)
    if debug not in ("nomix",):
        wq.append((wg_d, 512))
        for h in range(8):
            wq.append((win_d[h], 2048))
            wq.append((win_d[8 + h], 2048))
        if debug != "mlstm_only":
            pass
        for m in range(8):
            wq.append((win_d[16 + m], 2048))
        for mm_ in range(4):
            wq.append((win_d[24 + mm_], 2048))
        for h in range(8):
            wq.append((win_d[28 + h], 2048))
            wq.append((win_d[36 + h], 2048))
        for m in range(8):
            wq.append((win_d[44 + m], 2048))
        for mm_ in range(4):
            wq.append((win_d[24 + mm_], 2048))
    declare_ffn(wf2i_d, wf2o_d)

    nstate = {"i": 0}

    def norm_block(a, tb, final=False):
        i = nstate["i"]
        nstate["i"] += 1
        sl = i % 2
        bi = 6 + sl
        tsl = slice(tb * 512, (tb + 1) * 512)
        for kc in range(KC):
            B.act(sqb[:, kc % 2, :], xT[:, kc, tsl], AF.Square, [t_x[kc][tb]], [t_sqs[kc % 2]])
            B.mm(banks[bi][:], onesb[:], sqb[:, kc % 2, :], kc == 0, kc == KC - 1,
                 [t_sqs[kc % 2], t_const], [bk[bi]])
        t_r = t_rs[sl]
        B.act(nrmA[:, sl, :], banks[bi][:], AF.Ln, [bk[bi], t_const], [t_r], bias=epsc[:, 0:1], scale=1.0 / D)
        B.act(rstd[:, sl, :], nrmA[:, sl, :], AF.Exp, [t_r], [t_r], scale=-0.5)
        for kc in range(KC):
            if final:
                B.stt(tmpB[:, kc % 2, :], xT[:, kc, tsl], normw[:, 3 * KC + kc:3 * KC + kc + 1],
                      rstd[:, sl, :], ALU.mult, ALU.mult, [t_x[kc][tb], t_r, t_const],
                      [t_fin[kc % 2]])
                t_o = Tok()
                out_toks.append(t_o)
                B.dma(out_d[:, kc, tsl], tmpB[:, kc % 2, :], [t_fin[kc % 2]], [t_o],
                      eng="sp")
            else:
                B.stt(tmpB[:, kc % 2, :], xT[:, kc, tsl], acol[:, a * KC + kc:a * KC + kc + 1],
                      rstd[:, sl, :], ALU.mult, ALU.mult, [t_x[kc][tb], t_r, t_mod],
                      [t_fin[kc % 2]])
                B.act(hT[:, kc, tsl], tmpB[:, kc % 2, :], AF.Identity, [t_fin[kc % 2], t_mod],
                      [t_h[tb]], bias=modT[:, a * 3 * KC + kc:a * 3 * KC + kc + 1], scale=1.0)

    def ffn(a):
        nonlocal arena_toks
        for tb in range(NTB):
            norm_block(a, tb)
        actT = arena[:].bitcast(BF16)
        t_act = [[Tok() for _ in range(NTB)] for _ in range(GJ)]
        flat = [t for row in t_act for t in row]
        inherit(flat, arena_toks)
        pi = 0
        for g in range(2):
            for jj in range(GJ):
                w, wt = w_next()
                for tb in range(NTB):
                    tsl = slice(tb * 512, (tb + 1) * 512)
                    bg = (pi % 2) * 2
                    bu = bg + 1
                    pi += 1
                    for kc in range(KC):
                        B.mm(banks[bg][:], w[:, kc * 128:(kc + 1) * 128], hT[:, kc, tsl],
                             kc == 0, kc == KC - 1, [wt, t_h[tb]], [bk[bg]])
                    for kc in range(KC):
                        B.mm(banks[bu][:], w[:, 1024 + kc * 128:1024 + (kc + 1) * 128],
                             hT[:, kc, tsl], kc == 0, kc == KC - 1, [wt, t_h[tb]], [bk[bu]])
                    sl = pi % 2
                    B.act(tmpA[:, sl, :], banks[bg][:], AF.Silu, [bk[bg]], [t_sil[sl]])
                    B.tt("dve", actT[:, jj * S + tb * 512: jj * S + (tb + 1) * 512],
                         banks[bu][:], tmpA[:, sl, :], ALU.mult, [bk[bu], t_sil[sl]],
                         [t_act[jj][tb]])
            for m in range(KC):
                w, wt = w_next()
                for tb in range(NTB):
                    tsl = slice(tb * 512, (tb + 1) * 512)
                    bo = 4 + (pi % 2)
                    pi += 1
                    for jj in range(GJ):
                        B.mm(banks[bo][:], w[:, jj * 128:(jj + 1) * 128],
                             actT[:, jj * S + tb * 512: jj * S + (tb + 1) * 512],
                             jj == 0, jj == GJ - 1, [wt, t_act[jj][tb]], [bk[bo]])
                    B.stt(xT[:, m, tsl], banks[bo][:], gcol[:, a * KC + m:a * KC + m + 1],
                          xT[:, m, tsl], ALU.mult, ALU.add, [bk[bo], t_mod, t_x[m][tb]],
                          [t_x[m][tb]])
        arena_toks = flat

    msc = xT[:].rearrange("p a b -> p (a b)")

    def bfv(lo, n_words):
        return msc[:, lo:lo + n_words].bitcast(BF16)

    qT = bfv(0, 1024)
    kT = bfv(1024, 1024)
    gateT = bfv(2048, 1024)
    v_tok = bfv(3072, 1024)
    k_tok = bfv(4096, 1024)
    hacc = msc[:, 5120:7168]
    cb = [bfv(7168 + i * 256, 256) for i in range(NCB)]
    Sst = msc[:, 10240:10368]
    Sb = bfv(10368, 64)
    vnb = bfv(10432, 64)
    xm = msc[:, MSC_XM:MSC_XM + 2 * 2048].rearrange("p (s n) -> p s n", s=2)
    yT = bfv(0, 8192)
    t_q, t_k, t_gate, t_vtok, t_ktok = Tok(), Tok(), Tok(), Tok(), Tok()
    t_hacc = [Tok() for _ in range(NTB)]
    t_cb = [Tok() for _ in range(NCB)]
    t_S, t_Sb, t_vnb, t_egl = Tok(), Tok(), Tok(), Tok()
    t_xm = [Tok(), Tok()]
    msc_toks = [t_q, t_k, t_gate, t_vtok, t_ktok] + t_hacc + t_cb + [t_S, t_Sb, t_vnb]
    t_y = [[Tok() for _ in range(NTB)] for _ in range(KC)]
    y_flat = [t for row in t_y for t in row]

    bcs = [msc[:, 13312:13824], msc[:, 13824:14336], msc[:, 14848:15360], msc[:, 15360:15872]]
    t_bc = [Tok() for _ in range(4)]
    t_bcd = [Tok() for _ in range(6)]
    msc_toks += t_bc

    def bc_load(slot, q, h, tb):
        r0 = h * 16 + tb * 4
        src = bc_d[q, r0:r0 + 4, :].rearrange("a b -> (a b)").partition_broadcast(128)
        B.dma(bcs[slot], src, [t_bcd[q]], [t_bc[slot]], eng="act")

    cb1 = [bfv(MSC_XM + i * 256, 256) for i in range(11)]
    t_cb1 = [Tok() for _ in range(11)]
    S1 = msc[:, 14592:14720]
    Sb1 = bfv(14720, 64)
    vnb1 = bfv(14784, 64)
    t_S1, t_Sb1, t_vnb1 = Tok(), Tok(), Tok()
    t_egl1 = Tok()
    msc_toks += t_cb1 + [t_S1, t_Sb1, t_vnb1]
    chainR = [
        dict(banks=[0, 1, 2, 3], cb=cb, tcb=t_cb, E=(tmpB[:, 0, :], t_fin[0]),
             XA=(tmpA[:, 0, :], t_sil[0]), XT=(rstd[:, 0, :], t_rs[0]), EG=(nrmA[:, 0, :], t_rs[0]),
             S=(Sst, t_S), Sb=(Sb, t_Sb), vnb=(vnb, t_vnb), egl=(eglt[:, 0, :], t_egl)),
        dict(banks=[4, 5, 6, 7], cb=cb1, tcb=t_cb1, E=(tmpB[:, 1, :], t_fin[1]),
             XA=(tmpA[:, 1, :], t_sil[1]), XT=(rstd[:, 1, :], t_rs[1]), EG=(nrmA[:, 1, :], t_rs[1]),
             S=(S1, t_S1), Sb=(Sb1, t_Sb1), vnb=(vnb1, t_vnb1), egl=(eglt[:, 1, :], t_egl1)),
    ]

    hXT = arena[:, 0:8192].bitcast(BF16)
    t_hx = [[Tok() for _ in range(NTB)] for _ in range(8)]
    hx_flat = [t for row in t_hx for t in row]
    hcap = [arena[:, 8192 + i * 128: 8192 + (i + 1) * 128] for i in range(NHC)]
    t_hc = [Tok() for _ in range(NHC)]
    ZER = 21

    tri = [cmask[:, 0, :], cmask[:, 1, :]]
    bigA = [cmask[:, 2, :], cmask[:, 3, :]]

    rot = {"b": 0, "d": 0}

    def rbank():
        rot["b"] ^= 1
        return rot["b"]

    def hc_transpose_to(dst, src, bslot):
        B.tr(banks[6][:, bslot * 128:(bslot + 1) * 128], hcap[src], ident[:],
             [t_hc[src], t_const], [bk[6]])
        B.cp("dve", hcap[dst], banks[6][:, bslot * 128:(bslot + 1) * 128], [bk[6]], [t_hc[dst]])

    def gate_project(wcol0, dst_tiles, bias_cols):
        for c in range(16):
            for kc in range(KC):
                B.mm(banks[5][:, c * 32:(c + 1) * 32], hT[:, kc, c * 128:(c + 1) * 128],
                     wgb[:, wcol0 + kc * 32: wcol0 + (kc + 1) * 32], kc == 0, kc == KC - 1,
                     [t_h[c // 4], t_c2], [bk[5]])
        src = banks[5][:].rearrange("p (c g h) -> p g h c", c=16, g=4, h=8)
        dst = tmpA[:, 0, :].rearrange("p (g h c) -> p g h c", g=4, h=8, c=16)
        for g in range(4):
            B.cp("dve", dst[:, g], src[:, g], [bk[5]], [t_sil[0]])
        for g in range(4):
            B.tr(banks[6][:, g * 128:(g + 1) * 128], tmpA[:, 0, g * 128:(g + 1) * 128], ident[:],
                 [t_sil[0], t_const], [bk[6]])
        for g in range(4):
            i = dst_tiles[g]
            if bias_cols[g] is None:
                B.cp("act", hcap[i], banks[6][:, g * 128:(g + 1) * 128], [bk[6]], [t_hc[i]])
            else:
                B.act(hcap[i], banks[6][:, g * 128:(g + 1) * 128], AF.Identity, [bk[6], t_c2],
                      [t_hc[i]], bias=gb[:, bias_cols[g]:bias_cols[g] + 1], scale=1.0)

    def scan(dst, src, op0):
        B.P.add("dve", lambda e: e.tensor_tensor_scan(hcap[dst], hcap[src], hcap[ZER], 0.0, op0,
                                                      ALU.add),
                [t_hc[src], t_hc[ZER]], [t_hc[dst]])

    def hts(dst, src, s1, s2, op0, op1, extra=()):
        B.ts("dve", hcap[dst], hcap[src], s1, s2, op0, op1, [t_hc[src]] + list(extra), [t_hc[dst]])

    def htt(dst, a_, b_, op, eng="dve"):
        B.tt(eng, hcap[dst], hcap[a_], hcap[b_], op, [t_hc[a_], t_hc[b_]], [t_hc[dst]])

    def hact(dst, src, func, bias=0.0, scale=1.0, extra=()):
        B.act(hcap[dst], hcap[src], func, [t_hc[src]] + list(extra), [t_hc[dst]], bias=bias,
              scale=scale)

    def cross_chunk_sum(dst_col_tile, src, col, d):
        B.mm(banks[6][:, 0:1], clm[:, d, :], hcap[src][:, col:col + 1], True, True,
             [t_c2, t_hc[src]], [bk[6]])
        B.cp("dve", hcap[dst_col_tile][:, 0:1], banks[6][:, 0:1], [bk[6]], [t_hc[dst_col_tile]])

    def cross_chunk_max(dst_col_tile, dcol, src, col, d, ROW, RA, RB):
        B.tr(banks[6][0:1, 128:256], hcap[src][:, col:col + 1], ident[:], [t_hc[src], t_const],
             [bk[6]])
        B.cp("dve", hcap[ROW][0:1, :], banks[6][0:1, 128:256], [bk[6]], [t_hc[ROW]])
        row3 = hcap[ROW][0:1, :].rearrange("p (h c) -> p h c", h=8)
        cur, oth = RA, RB
        c3 = hcap[cur][0:1, :].rearrange("p (h c) -> p h c", h=8)
        B.memset("dve", hcap[cur][0:1, :], 0.0, [], [t_hc[cur]])
        if d == 0:
            B.cp("dve", c3[:, :, 1:16], row3[:, :, 0:15], [t_hc[ROW]], [t_hc[cur]])
        else:
            B.cp("dve", c3[:, :, 0:15], row3[:, :, 1:16], [t_hc[ROW]], [t_hc[cur]])
        for sh in (1, 2, 4, 8):
            c3 = hcap[cur][0:1, :].rearrange("p (h c) -> p h c", h=8)
            o3 = hcap[oth][0:1, :].rearrange("p (h c) -> p h c", h=8)
            if d == 0:
                B.tt("dve", o3[:, :, sh:16], c3[:, :, sh:16], c3[:, :, 0:16 - sh], ALU.max,
                     [t_hc[cur]], [t_hc[oth]])
                B.cp("dve", o3[:, :, 0:sh], c3[:, :, 0:sh], [t_hc[cur]], [t_hc[oth]])
            else:
                B.tt("dve", o3[:, :, 0:16 - sh], c3[:, :, 0:16 - sh], c3[:, :, sh:16], ALU.max,
                     [t_hc[cur]], [t_hc[oth]])
                B.cp("dve", o3[:, :, 16 - sh:16], c3[:, :, 16 - sh:16], [t_hc[cur]], [t_hc[oth]])
            cur, oth = oth, cur
        B.mm(banks[6][:, 1:2], hcap[cur][0:1, :], one11[0:1, 0:1], True, True,
             [t_hc[cur], t_const], [bk[6]])
        B.cp("dve", hcap[dst_col_tile][:, dcol:dcol + 1], banks[6][:, 1:2], [bk[6]],
             [t_hc[dst_col_tile]])

    NEGM = [9, 14]
    FLOOR = [10, 15]
    UCOL = [11, 16]

    def softplus_acc(dst, src, sign, tA, tB, tC):
        hact(tA, src, AF.Abs)
        hact(tB, tA, AF.Exp, scale=-1.0)
        hts(tA, tB, 2.0, None, ALU.add, None)
        B.recip(hcap[tA], hcap[tA], [t_hc[tA]], [t_hc[tA]])
        htt(tB, tB, tA, ALU.mult)
        htt(tA, tB, tB, ALU.mult)
        hts(tC, tA, 1.0 / 13.0, 1.0 / 11.0, ALU.mult, ALU.add)
        for cst in (1.0 / 9.0, 1.0 / 7.0, 1.0 / 5.0, 1.0 / 3.0, 1.0):
            htt(tC, tC, tA, ALU.mult)
            hts(tC, tC, cst, None, ALU.add, None)
        htt(tC, tC, tB, ALU.mult)
        hts(tA, src, float(sign), 0.0, ALU.mult, ALU.max)
        B.stt(hcap[dst], hcap[tC], 2.0, hcap[tA], ALU.mult, ALU.add, [t_hc[tC], t_hc[tA]],
              [t_hc[dst]])

    def mlstm_gates():
        P.add("pool", lambda e: e.memset(hcap[ZER], 0.0), [], [t_hc[ZER]])
        gate_project(0, [0, 1, 2, 3], [0, 1, 2, 3])
        for d in range(2):
            Gi, Gf = 2 * d, 2 * d + 1
            T1, SPt, LCS, CSP, U, LMX, X, COLS = 4, 5, 6, 7, 8, 12, 13, 20
            softplus_acc(SPt, Gf, -1.0, T1, LCS, CSP)
            scan(LCS, SPt, ALU.add)
            cross_chunk_sum(COLS, LCS, 127, d)
            if d == 1:
                hts(X, LCS, -1.0, hcap[LCS][:, 127:128], ALU.mult, ALU.add)
                htt(LCS, X, SPt, ALU.add)
            hts(CSP, LCS, hcap[COLS][:, 0:1], None, ALU.add, None, extra=[t_hc[COLS]])
            htt(U, Gi, CSP, ALU.add)
            if d == 0:
                scan(LMX, U, ALU.max)
                ccol_ = 127
            else:
                hts(X, U, 0.0, None, ALU.max, None)
                cur, oth = X, LMX
                for sh in (1, 2, 4, 8, 16, 32, 64):
                    B.tt("dve", hcap[oth][:, 0:128 - sh], hcap[cur][:, 0:128 - sh],
                         hcap[cur][:, sh:128], ALU.max, [t_hc[cur]], [t_hc[oth]])
                    B.cp("dve", hcap[oth][:, 128 - sh:128], hcap[cur][:, 128 - sh:128],
                         [t_hc[cur]], [t_hc[oth]])
                    cur, oth = oth, cur
                if cur != LMX:
                    B.cp("dve", hcap[LMX], hcap[cur], [t_hc[cur]], [t_hc[LMX]])
                ccol_ = 0
            cross_chunk_max(COLS, 1, LMX, ccol_, d, 17, 18, 19)
            hts(X, LMX, hcap[COLS][:, 1:2], None, ALU.max, None, extra=[t_hc[COLS]])
            hts(NEGM[d], X, -1.0, None, ALU.mult, None)
            htt(T1, CSP, X, ALU.subtract)
            hact(FLOOR[d], T1, AF.Exp)
            hc_transpose_to(UCOL[d], U, 2)
            B.dma(bc_d[d], hcap[NEGM[d]], [t_hc[NEGM[d]]], [t_bcd[d]], eng="act")
            B.dma(bc_d[2 + d], hcap[FLOOR[d]], [t_hc[FLOOR[d]]], [t_bcd[2 + d]], eng="act")

    def proj_fm(wsl, wt, evac):
        for tb in range(NTB):
            tsl = slice(tb * 512, (tb + 1) * 512)
            bi = rbank()
            for kc in range(KC):
                B.mm(banks[bi][:], wsl[:, kc * 128:(kc + 1) * 128], hT[:, kc, tsl],
                     kc == 0, kc == KC - 1, [wt, t_h[tb]], [bk[bi]])
            evac(tb, tsl, bi)

    def out_norm(h, which):
        for tb in range(NTB):
            tsl = slice(tb * 512, (tb + 1) * 512)
            sl = tb % 2
            bi = 6 + sl
            B.act(sqb[:, sl, :], hacc[:, tsl], AF.Square, [t_hacc[tb]], [t_sqs[sl]])
            B.mm(banks[bi][:], onesb[:], sqb[:, sl, :], True, True, [t_sqs[sl], t_const], [bk[bi]])
            B.act(nrmA[:, sl, :], banks[bi][:], AF.Ln, [bk[bi], t_const], [t_rs[sl]], bias=epsc[:, 0:1],
                  scale=1.0 / 128)
            B.act(rstd[:, sl, :], nrmA[:, sl, :], AF.Exp, [t_rs[sl]], [t_rs[sl]], scale=-0.5)
            B.stt(tmpB[:, sl, :], hacc[:, tsl], onorm[:, which:which + 1], rstd[:, sl, :],
                  ALU.mult, ALU.mult, [t_hacc[tb], t_c2, t_rs[sl]], [t_fin[sl]])
            B.tt("dve", hXT[:, h * S + tb * 512: h * S + (tb + 1) * 512], tmpB[:, sl, :],
                 gateT[:, tsl], ALU.mult, [t_fin[sl], t_gate], [t_hx[h][tb]])

    def selector(bank_i, src_tile, h, tb):
        for r in range(4):
            hc_ = h * 16 + tb * 4 + r
            B.mm(banks[bank_i][:, r * 128:(r + 1) * 128],
                 ident[:, hc_:hc_ + 1].to_broadcast([128, 128]), hcap[src_tile], True, True,
                 [t_const, t_hc[src_tile]], [bk[bank_i]])

    pend_on = {"f": None}

    def mlstm_head(h):
        w, wt = w_next()
        proj_fm(w[:, 0:1024], wt, lambda tb, tsl, bi: B.act(
            qT[:, tsl], banks[bi][:], AF.Identity, [bk[bi]], [t_q], scale=128.0 ** -0.5))
        proj_fm(w[:, 1024:2048], wt, lambda tb, tsl, bi: B.cp(
            "dve", kT[:, tsl], banks[bi][:], [bk[bi]], [t_k]))
        if pend_on["f"] is not None:
            pend_on["f"]()
            pend_on["f"] = None
        w, wt = w_next()
        for tc in range(16):
            q4 = tc % 4
            vb = 5 if (tc // 4) % 2 == 0 else 7
            for kc in range(KC):
                B.mm(banks[vb][:, q4 * 128:(q4 + 1) * 128], hT[:, kc, tc * 128:(tc + 1) * 128],
                     w[:, kc * 128:(kc + 1) * 128], kc == 0, kc == KC - 1, [wt, t_h[tc // 4]],
                     [bk[vb]])
            if q4 == 3:
                B.cp("act", v_tok[:, (tc - 3) * 128:(tc + 1) * 128], banks[vb][:], [bk[vb]],
                     [t_vtok])
        proj_fm(w[:, 1024:2048], wt, lambda tb, tsl, bi: B.act(
            gateT[:, tsl], banks[bi][:], AF.Sigmoid, [bk[bi]], [t_gate]))

        iters = [(0, tb) for tb in range(NTB)] + [(1, tb) for tb in range(NTB - 1, -1, -1)]
        bc_load(0, 0, h, 0)
        bc_load(1, 2, h, 0)
        bc_load(3, 2 + iters[1][0], h, iters[1][1])
        pend = {"g": None}

        def step_pending():
            if pend["g"] is not None:
                try:
                    next(pend["g"])
                except StopIteration:
                    pend["g"] = None

        def epilogue_gen(d, tb, tsl, bn_, bd_, FL, tFL):
            B.act(nrmA[:, 0, :], banks[bd_][:], AF.Abs, [bk[bd_]], [t_rs[0]])
            yield
            B.tt("dve", nrmA[:, 0, :], FL, nrmA[:, 0, :], ALU.max, [tFL, t_rs[0]], [t_rs[0]])
            yield
            B.act(nrmA[:, 0, :], nrmA[:, 0, :], AF.Ln, [t_rs[0]], [t_rs[0]])
            yield
            B.act(rstd[:, 0, :], nrmA[:, 0, :], AF.Exp, [t_rs[0]], [t_rs[0]], scale=-1.0)
            yield
            if d == 0:
                B.tt("dve", hacc[:, tsl], banks[bn_][:], rstd[:, 0, :], ALU.mult,
                     [bk[bn_], t_rs[0]], [t_hacc[tb]])
            else:
                B.tt("dve", nrmA[:, 0, :], banks[bn_][:], rstd[:, 0, :], ALU.mult,
                     [bk[bn_], t_rs[0]], [t_rs[0]])
                yield
                B.tt("pool", hacc[:, tsl], hacc[:, tsl], nrmA[:, 0, :], ALU.add,
                     [t_hacc[tb], t_rs[0]], [t_hacc[tb]])
            yield
        for it_, (d, tb) in enumerate(iters):
            if True:
                tsl = slice(tb * 512, (tb + 1) * 512)
                if it_ + 1 < len(iters):
                    nd, ntb = iters[it_ + 1]
                    bc_load(2 * ((it_ + 1) % 2), nd, h, ntb)
                bn_, bd_ = (3, 4) if it_ % 2 == 0 else (5, 6)
                NB, tNB = bcs[2 * (it_ % 2)], t_bc[2 * (it_ % 2)]
                FL, tFL = bcs[2 * (it_ % 2) + 1], t_bc[2 * (it_ % 2) + 1]
                scs = list(range(0, 4 * tb + 4)) if d == 0 else list(range(15, 4 * tb - 1, -1))
                tiles = []
                for sc in scs:
                    r = sc - 4 * tb
                    diag = 0 <= r <= 3
                    if d == 0:
                        c0, c1 = (r * 128 if diag else 0), 512
                    else:
                        c0, c1 = 0, ((r + 1) * 128 if diag else 512)
                    tiles.append((sc, r, diag, c0, c1))

                def emit_st(tile):
                    sc, r, diag, c0, c1 = tile
                    sbk = rbank()
                    B.mm(banks[sbk][:, c0:c1], kT[:, sc * 128:(sc + 1) * 128],
                         qT[:, tb * 512 + c0: tb * 512 + c1], True, True, [t_k, t_q], [bk[sbk]])
                    return sbk

                sbanks = {}

                def stage_ab(n_):
                    sc, r, diag, c0, c1 = tiles[n_]
                    sbanks[n_] = emit_st(tiles[n_])
                    hcs = h * 16 + sc
                    ucv = hcap[UCOL[d]][:, hcs:hcs + 1]
                    ds = n_ % 2
                    Dt = tmpA[:, ds, :]
                    if diag:
                        if d == 0:
                            t0, t1, r0, r1 = c0, c0 + 128, c0 + 128, c1
                        else:
                            t0, t1, r0, r1 = c1 - 128, c1, c0, c1 - 128
                        B.stt(tmpB[:, ds, 0:128], NB[:, t0:t1], ucv, tri[d], ALU.add, ALU.min,
                              [tNB, t_hc[UCOL[d]], t_c2], [t_fin[ds]])
                        B.act(Dt[:, t0:t1], tmpB[:, ds, 0:128], AF.Exp, [t_fin[ds]], [t_sil[ds]])
                        if r1 > r0:
                            B.act(Dt[:, r0:r1], NB[:, r0:r1], AF.Exp,
                                  [tNB, t_hc[UCOL[d]]], [t_sil[ds]], bias=ucv)
                    else:
                        B.act(Dt[:, c0:c1], NB[:, c0:c1], AF.Exp, [tNB, t_hc[UCOL[d]]],
                              [t_sil[ds]], bias=ucv)

                stage_ab(0)
                for n_, (sc, r, diag, c0, c1) in enumerate(tiles):
                    if n_ + 1 < len(tiles):
                        stage_ab(n_ + 1)
                    sbk = sbanks[n_]
                    ds = n_ % 2
                    Dt = tmpA[:, ds, :]
                    Pt = cb[ds]
                    B.tt("dve", Pt[:, c0:c1], banks[sbk][:, c0:c1], Dt[:, c0:c1], ALU.mult,
                         [bk[sbk], t_sil[ds]], [t_cb[ds]])
                    first = n_ == 0
                    last = n_ == len(tiles) - 1
                    B.mm(banks[bn_][:, c0:c1], v_tok[:, sc * 128:(sc + 1) * 128], Pt[:, c0:c1],
                         first, last, [t_vtok, t_cb[ds]], [bk[bn_]])
                    B.mm(banks[bd_][:, c0:c1], onesb[:], Pt[:, c0:c1], first, last,
                         [t_const, t_cb[ds]], [bk[bd_]])
                    if n_ >= 1:
                        step_pending()
                while pend["g"] is not None:
                    step_pending()
                if it_ >= 1 and it_ + 1 < len(iters):
                    nd, ntb = iters[it_ + 1]
                    bc_load(2 * ((it_ + 1) % 2) + 1, 2 + nd, h, ntb)
                pend["g"] = epilogue_gen(d, tb, tsl, bn_, bd_, FL, tFL)
        while pend["g"] is not None:
            step_pending()
        pend_on["f"] = lambda: out_norm(h, 0)

    def gt(d, i):
        return 8 + 6 * d + i

    def gdn_gates():
        P.add("pool", lambda e: e.memset(hcap[ZER], 0.0), [], [t_hc[ZER]])
        gate_project(256, [0, 1, 2, 3], [6, None, 7, None])
        for d in range(2):
            Ga, Gb = 2 * d, 2 * d + 1
            T1, SPt, L, X, BETA = 4, 5, 6, 7, 20
            GAM = gt(d, 0)
            softplus_acc(SPt, Ga, 1.0, T1, L, X)
            hts(X, SPt, nega[:, d:d + 1], None, ALU.mult, None, extra=[t_c2])
            if d == 0:
                scan(GAM, X, ALU.add)
                gl = 127
            else:
                scan(L, X, ALU.add)
                hts(T1, L, -1.0, hcap[L][:, 127:128], ALU.mult, ALU.add)
                htt(GAM, T1, X, ALU.add)
                gl = 0
            hact(BETA, Gb, AF.Sigmoid)
            B.dma(bc_d[4 + d], hcap[GAM], [t_hc[GAM]], [t_bcd[4 + d]], eng="act")
            hc_transpose_to(gt(d, 1), GAM, 0)
            hts(T1, BETA, -1.0, None, ALU.mult, None)
            hc_transpose_to(gt(d, 2), T1, 1)
            hc_transpose_to(gt(d, 3), BETA, 2)
            hact(SPt, GAM, AF.Exp)
            htt(T1, BETA, SPt, ALU.mult)
            hc_transpose_to(gt(d, 4), T1, 3)
            B.act(hcap[X], hcap[GAM], AF.Exp, [t_hc[GAM]], [t_hc[X]],
                  bias=hcap[GAM][:, gl:gl + 1], scale=-1.0)
            hc_transpose_to(gt(d, 5), X, 0)

    def conv_proj(wsl, wt, ch, kind):
        raw = hacc
        acc = msc[:, 7168:7168 + 2048]

        def tacc(tb):
            return [t_cb[2 * tb], t_cb[2 * tb + 1]]

        def proj(tb):
            tsl = slice(tb * 512, (tb + 1) * 512)
            bi = rbank()
            for kc in range(KC):
                B.mm(banks[bi][:], wsl[:, kc * 128:(kc + 1) * 128], hT[:, kc, tsl],
                     kc == 0, kc == KC - 1, [wt, t_h[tb]], [bk[bi]])
            B.cp("act", raw[:, tsl], banks[bi][:], [bk[bi]], [t_hacc[tb]])

        def conv(tb):
            lo, hi = tb * 512, (tb + 1) * 512
            B.ts("dve", acc[:, lo:hi], raw[:, lo:hi], cw[:, ch * 5 + 2: ch * 5 + 3], None,
                 ALU.mult, None, [t_hacc[tb], t_c2], tacc(tb))
            for j in (0, 1, 3, 4):
                dd = j - 2
                o0 = max(lo, -dd)
                o1 = min(hi, S - dd)
                rd = [t_hacc[tb], t_c2] + tacc(tb)
                if dd < 0 and tb > 0:
                    rd.append(t_hacc[tb - 1])
                if dd > 0 and tb < NTB - 1:
                    rd.append(t_hacc[tb + 1])
                B.stt(acc[:, o0:o1], raw[:, o0 + dd:o1 + dd], cw[:, ch * 5 + j: ch * 5 + j + 1],
                      acc[:, o0:o1], ALU.mult, ALU.add, rd, tacc(tb))

        def post(tb):
            tsl = slice(tb * 512, (tb + 1) * 512)
            sl = tb % 2
            if kind == "v":
                B.act(cb[8 + tb][:, :], acc[:, tsl], AF.Silu, tacc(tb), [t_cb[8 + tb]])
                return
            B.act(tmpA[:, sl, :], acc[:, tsl], AF.Silu, tacc(tb), [t_sil[sl]])
            B.act(sqb[:, sl, :], tmpA[:, sl, :], AF.Square, [t_sil[sl]], [t_sqs[sl]])
            bi = 6 + sl
            B.mm(banks[bi][:], onesb[:], sqb[:, sl, :], True, True, [t_sqs[sl], t_const], [bk[bi]])
            B.act(nrmA[:, sl, :], banks[bi][:], AF.Ln, [bk[bi], t_const], [t_rs[sl]],
                  bias=epsc[:, 0:1], scale=1.0)
            B.act(rstd[:, sl, :], nrmA[:, sl, :], AF.Exp, [t_rs[sl]], [t_rs[sl]], scale=-0.5)
            if kind == "q":
                B.stt(qT[:, tsl], tmpA[:, sl, :], 128.0 ** -0.5, rstd[:, sl, :], ALU.mult, ALU.mult,
                      [t_sil[sl], t_rs[sl]], [t_q])
            else:
                B.tt("dve", kT[:, tsl], tmpA[:, sl, :], rstd[:, sl, :], ALU.mult,
                     [t_sil[sl], t_rs[sl]], [t_k])

        proj(0)
        proj(1)
        conv(0)
        post(0)
        proj(2)
        conv(1)
        post(1)
        proj(3)
        conv(2)
        post(2)
        conv(3)
        post(3)

    def to_token_major(dst, t_dst, srcf, src_toks):
        pb = banks[5][:].bitcast(BF16)
        for tc in range(16):
            q4 = tc % 4
            B.tr(pb[:, q4 * 128:(q4 + 1) * 128], srcf(tc), identb[:], src_toks + [t_const], [bk[5]])
            if q4 == 3:
                B.cp("act", dst[:, (tc - 3) * 128:(tc + 1) * 128], pb[:, 0:512], [bk[5]], [t_dst])

    def gdn_head(h):
        w, wt = w_next()
        conv_proj(w[:, 0:1024], wt, h, "q")
        conv_proj(w[:, 1024:2048], wt, 8 + h, "k")
        to_token_major(k_tok, t_ktok, lambda tc: kT[:, tc * 128:(tc + 1) * 128], [t_k])
        w, wt = w_next()
        conv_proj(w[:, 0:1024], wt, 16 + h, "v")
        to_token_major(v_tok, t_vtok,
                       lambda tc: cb[8 + tc // 4][:, (tc % 4) * 128:(tc % 4 + 1) * 128],
                       t_cb[8:12])
        proj_fm(w[:, 1024:2048], wt, lambda tb, tsl, bi: B.act(
            gateT[:, tsl], banks[bi][:], AF.Silu, [bk[bi]], [t_gate]))

        def v4(ap):
            return ap.rearrange("p (r t) -> p r t", r=4)

        for tb in range(NTB):
            B.memset("pool", hacc[:, tb * 512:(tb + 1) * 512], 0.0, [], [t_hacc[tb]])

        def chain(d, R):
            GAM, GAMc, NEGBc, BETAc, BEGc, KDc = [gt(d, i) for i in range(6)]
            P0, P1, P2, P3 = R["banks"]
            cbs, tcs = R["cb"], R["tcb"]
            N0, Noff, Zt, Db, Tb, attnT, Ru, Rw, kd_, nWt, qdT = cbs[0:11]
            (tN0, tNoff, tZt, tDb, tTb, tattn, tRu, tRw, tkd, tnW, tqd) = tcs[0:11]
            E, tE = R["E"]
            XA, tXA = R["XA"]
            XT, tXT = R["XT"]
            EG, tEG = R["EG"]
            S_, tS_ = R["S"]
            Sb_, tSb_ = R["Sb"]
            vn_, tvn_ = R["vnb"]
            eg_, teg_ = R["egl"]
            B.memset("pool", S_, 0.0, [], [tS_])
            B.memset("pool", Sb_, 0.0, [], [tSb_])
            tbs = range(NTB) if d == 0 else range(NTB - 1, -1, -1)
            i4 = identb4[:].rearrange("p a b -> p (a b)")

            def mk(dd, lev):
                return gmaskb[:, dd * 7 + lev, :].unsqueeze(1).to_broadcast([128, 4, 128])

            tbl = list(tbs)
            cix = R["banks"][0] // 4
            bc_load(2 * cix, 4 + d, h, tbl[0])
            for k_, tb in enumerate(tbl):
                tsl = slice(tb * 512, (tb + 1) * 512)
                hc0 = h * 16 + tb * 4
                if k_ + 1 < len(tbl):
                    bc_load(2 * cix + (k_ + 1) % 2, 4 + d, h, tbl[k_ + 1])
                GB, tGB = bcs[2 * cix + k_ % 2], t_bc[2 * cix + k_ % 2]

                def colb(tile):
                    return hcap[tile][:, hc0:hc0 + 4].unsqueeze(2).to_broadcast([128, 4, 128])

                for r in range(4):
                    csl = slice(tb * 512 + r * 128, tb * 512 + (r + 1) * 128)
                    B.mm(banks[P1][:, r * 128:(r + 1) * 128], kT[:, csl], kT[:, csl], True, True,
                         [t_k], [bk[P1]])
                B.tt("dve", v4(E), v4(GB), colb(GAMc), ALU.subtract,
                     [tGB, t_hc[GAMc]], [tE])
                B.tt("dve", v4(XA), v4(E), bigA[d].unsqueeze(1).to_broadcast([128, 4, 128]),
                     ALU.max, [tE, t_c2], [tXA])
                B.act(XA, XA, AF.Exp, [tXA], [tXA], scale=-1.0)
                B.tt("pool", v4(XA), v4(XA), colb(NEGBc), ALU.mult, [tXA, t_hc[NEGBc]], [tXA])
                B.tt("dve", N0, banks[P1][:], XA, ALU.mult, [bk[P1], tXA], [tN0])
                yield
                for r in range(4):
                    csl = slice(tb * 512 + r * 128, tb * 512 + (r + 1) * 128)
                    B.mm(banks[P1][:, r * 128:(r + 1) * 128], kT[:, csl], qT[:, csl], True, True,
                         [t_k, t_q], [bk[P1]])
                B.tt("dve", v4(XT), v4(E), tri[d].unsqueeze(1).to_broadcast([128, 4, 128]),
                     ALU.min, [tE, t_c2], [tXT])
                B.act(XT, XT, AF.Exp, [tXT], [tXT])
                B.tt("dve", attnT, banks[P1][:], XT, ALU.mult, [bk[P1], tXT], [tattn])
                B.act(EG, GB, AF.Exp, [tGB], [tEG])
                B.tt("pool", qdT, qT[:, tsl], EG, ALU.mult, [t_q, tEG], [tqd])
                gl = 127 if d == 0 else 0
                B.act(eg_, v4(GB)[:, :, gl], AF.Exp, [tGB], [teg_])
                yield
                pb1 = banks[P1][:].bitcast(BF16)
                for r in range(4):
                    B.tr(pb1[:, r * 128:(r + 1) * 128], N0[:, r * 128:(r + 1) * 128], identb[:],
                         [tN0, t_const], [bk[P1]])
                B.cp("act", Zt, pb1[:, 0:512], [bk[P1]], [tZt])
                B.tt("pool", v4(Zt), v4(Zt), mk(1 - d, 0), ALU.mult, [tZt, t_c2], [tZt])
                B.tt("dve", Tb, Zt, i4, ALU.add, [tZt, t_c2], [tTb])
                B.mm(banks[P2][:], identb[:], Tb, True, True, [t_const, tTb], [bk[P2]], skip=True)
                B.tt("pool", v4(Noff), v4(N0), mk(d, 0), ALU.mult, [tN0, t_c2], [tNoff])
                B.tt("dve", Db, Noff, i4, ALU.add, [tNoff, t_c2], [tDb])
                B.mm(banks[P1][:], identb[:], Db, True, True, [t_const, tDb], [bk[P1]], skip=True)
                yield
                for lev in range(1, 7):
                    for r in range(4):
                        rs_ = slice(r * 128, (r + 1) * 128)
                        B.mm(banks[P0][:, rs_], N0[:, rs_], Tb[:, rs_], True, True,
                             [tN0, tTb], [bk[P0]])
                    B.tt("dve", v4(Zt), v4(banks[P0][:]), mk(1 - d, lev), ALU.mult,
                         [bk[P0], t_c2], [tZt])
                    yield
                    for r in range(4):
                        rs_ = slice(r * 128, (r + 1) * 128)
                        B.mm(banks[P2][:, rs_], Db[:, rs_], Zt[:, rs_], False, True,
                             [tDb, tZt], [bk[P2]], skip=True)
                    if lev < 6:
                        for r in range(4):
                            rs_ = slice(r * 128, (r + 1) * 128)
                            B.mm(banks[P1][:, rs_], Zt[:, rs_], Db[:, rs_], False, True,
                                 [tDb, tZt], [bk[P1]], skip=True)
                    B.cp("act", Tb, banks[P2][:], [bk[P2]], [tTb])
                    if lev < 6:
                        B.cp("dve", Db, banks[P1][:], [bk[P1]], [tDb])
                    yield
                B.tt("pool", v4(Ru), v4(v_tok[:, tsl]), colb(BETAc), ALU.mult,
                     [t_vtok, t_hc[BETAc]], [tRu])
                B.tt("pool", v4(Rw), v4(k_tok[:, tsl]), colb(BEGc), ALU.mult,
                     [t_ktok, t_hc[BEGc]], [tRw])
                B.tt("pool", v4(kd_), v4(k_tok[:, tsl]), colb(KDc), ALU.mult,
                     [t_ktok, t_hc[KDc]], [tkd])
                for r in range(4):
                    rs_ = slice(r * 128, (r + 1) * 128)
                    B.mm(banks[P0][:, rs_], Rw[:, rs_], Tb[:, rs_], True, True, [tRw, tTb],
                         [bk[P0]])
                B.act(nWt, banks[P0][:], AF.Identity, [bk[P0]], [tnW], scale=-1.0)
                yield
                rr = range(4) if d == 0 else range(3, -1, -1)
                for r in rr:
                    rs_ = slice(r * 128, (r + 1) * 128)
                    B.mm(banks[P1][:, 0:128], Tb[:, rs_], Ru[:, rs_], True, False, [tTb, tRu],
                         [bk[P1]])
                    B.mm(banks[P1][:, 0:128], nWt[:, rs_], Sb_, False, True, [tnW, tSb_], [bk[P1]])
                    B.cp("act", vn_, banks[P1][:, 0:128], [bk[P1]], [tvn_])
                    yield
                    B.mm(banks[P3][:, rs_], Sb_, qdT[:, rs_], True, False, [tSb_, tqd], [bk[P3]])
                    B.mm(banks[P3][:, rs_], vn_, attnT[:, rs_], False, True, [tvn_, tattn],
                         [bk[P3]])
                    B.mm(banks[P1][:, 128:256], kd_[:, rs_], vn_, True, True, [tkd, tvn_],
                         [bk[P1]])
                    B.stt(S_, S_, eg_[:, r:r + 1], banks[P1][:, 128:256], ALU.mult, ALU.add,
                          [tS_, teg_, bk[P1]], [tS_])
                    B.cp("act", Sb_, S_, [tS_], [tSb_])
                    yield
                B.tt("dve", hacc[:, tsl], banks[P3][:], hacc[:, tsl], ALU.add,
                     [bk[P3], t_hacc[tb]], [t_hacc[tb]])
                yield

        gens = [chain(0, chainR[0]), chain(1, chainR[1])]
        while gens:
            for g in list(gens):
                try:
                    next(g)
                except StopIteration:
                    gens.remove(g)
        out_norm(h, 1)

    def branch_phase():
        inherit(y_flat + t_xm, msc_toks)
        for m in range(KC):
            w, wt = w_next()
            for tb in range(NTB):
                tsl = slice(tb * 512, (tb + 1) * 512)
                for kc in range(KC):
                    B.mm(banks[0][:], w[:, kc * 128:(kc + 1) * 128], hT[:, kc, tsl], kc == 0,
                         kc == KC - 1, [wt, t_h[tb]], [bk[0]])
                for kh in range(8):
                    B.mm(banks[1][:], w[:, 1024 + kh * 128:1024 + (kh + 1) * 128],
                         hXT[:, kh * S + tb * 512: kh * S + (tb + 1) * 512], kh == 0, kh == 7,
                         [wt, t_hx[kh][tb]], [bk[1]])
                sl = tb % 2
                B.act(tmpA[:, sl, :], banks[0][:], AF.Sigmoid, [bk[0]], [t_sil[sl]])
                B.tt("dve", yT[:, m * S + tb * 512: m * S + (tb + 1) * 512], banks[1][:],
                     tmpA[:, sl, :], ALU.mult, [bk[1], t_sil[sl]], [t_y[m][tb]])
        for mm_ in range(4):
            w, wt = w_next()
            for half in range(2):
                m = 2 * mm_ + half
                sl = m % 2
                B.dma(xm[:, sl, :], xsp_d[:, m, :], [t_xsp[m]], [t_xm[sl]], eng="act")
                for tb in range(NTB):
                    tsl = slice(tb * 512, (tb + 1) * 512)
                    bi = 2 + (tb % 2)
                    for kc in range(KC):
                        B.mm(banks[bi][:], w[:, half * 1024 + kc * 128: half * 1024 + (kc + 1) * 128],
                             yT[:, kc * S + tb * 512: kc * S + (tb + 1) * 512], kc == 0,
                             kc == KC - 1, [wt, t_y[kc][tb]], [bk[bi]])
                    B.stt(xm[:, sl, tsl], banks[bi][:], gcol[:, KC + m:KC + m + 1], xm[:, sl, tsl],
                          ALU.mult, ALU.add, [bk[bi], t_mod, t_xm[sl]], [t_xm[sl]])
                B.dma(xsp_d[:, m, :], xm[:, sl, :], [t_xm[sl]], [t_xsp[m]], eng="act")
        inherit(msc_toks, y_flat + t_xm)

    t_xsp = [Tok() for _ in range(KC)]

    def mixer(which):
        nonlocal arena_toks
        for tb in range(NTB):
            norm_block(1, tb)
        x_flat = [t for row in t_x for t in row]
        for kc in range(KC):
            B.dma(xsp_d[:, kc, :], xT[:, kc, :], t_x[kc], [t_xsp[kc]], eng="sp")
        inherit(msc_toks + y_flat + t_xm, x_flat)
        inherit(hx_flat + t_hc, arena_toks)
        w, wt = w_next()
        B.cp("pool", wgb[:], w[:, 0:512], [wt], [t_c2])
        if which in ("all", "mlstm"):
            mlstm_gates()
            for h in range(8):
                mlstm_head(h)
            pend_on["f"]()
            pend_on["f"] = None
        else:
            for h in range(8):
                w_next()
                w_next()
            for tk in hx_flat:
                pass
            P.add("pool", lambda e: e.memset(hXT, 0.0), [], hx_flat)
        branch_phase()
        if which in ("all", "gdn"):
            gdn_gates()
            for h in range(8):
                gdn_head(h)
        else:
            for h in range(8):
                w_next()
                w_next()
            P.add("pool", lambda e: e.memset(hXT, 0.0), hx_flat, hx_flat)
        branch_phase()
        inherit(x_flat, msc_toks + y_flat + t_xm)
        for kc in range(KC):
            B.dma(xT[:, kc, :], xsp_d[:, kc, :], [t_xsp[kc]], t_x[kc], eng="sp")
        arena_toks = hx_flat + t_hc

    if debug == "nomix":
        ffn(0)
        ffn(2)
        for tb in range(NTB):
            norm_block(3, tb, final=True)
    elif debug in ("mlstm", "gdn", "mixonly"):
        ffn(0)
        mixer({"mlstm": "mlstm", "gdn": "gdn", "mixonly": "all"}[debug])
        for kc in range(KC):
            t_o = Tok()
            out_toks.append(t_o)
            B.dma(out_d[:, kc, :], xT[:, kc, :], t_x[kc], [t_o])
    else:
        ffn(0)
        mixer("all")
        ffn(2)
        for tb in range(NTB):
            norm_block(3, tb, final=True)
    P.add("sp", None, out_toks, [])

    pool_sz = {"pe": 1, "act": 8, "dve": 1, "pool": 8, "sp": 16}
    P.finalize(pool_sz)
    engsem = {e: B.es.enter_context(nc.semaphore("sem_" + e)) for e in ENGS}
    dmasem = {e: [B.es.enter_context(nc.semaphore("dsem_%s_%d" % (e, i)))
                  for i in range(pool_sz[e])] for e in ("act", "pool", "sp")}
    with nc.Block() as block:
        @block.tensor
        def _(e):
            P.emit("pe", e, engsem, dmasem)

        @block.scalar
        def _(e):
            P.emit("act", e, engsem, dmasem)

        @block.vector
        def _(e):
            P.emit("dve", e, engsem, dmasem)

        @block.gpsimd
        def _(e):
            P.emit("pool", e, engsem, dmasem)

        @block.sync
        def _(e):
            P.emit("sp", e, engsem, dmasem)
    B.es.close()
    return nc


def _col(v):
    return np.ascontiguousarray(v.reshape(-1, 128).T)


def _chunk(w, c0):
    return w[:, c0:c0 + 128].reshape(KC, 128, 128).transpose(1, 0, 2).reshape(128, 1024)


def host_layout(inp):
    shared = {}
    w_ada = inp["w_ada"][0]
    shared["wada"] = np.ascontiguousarray(
        w_ada.reshape(KC, 128, 18, 512).transpose(2, 1, 0, 3).reshape(18, 128, KC * 512))
    shared["bada"] = np.ascontiguousarray(inp["b_ada"][0].reshape(1, 9216))
    shared["normw"] = np.ascontiguousarray(np.concatenate(
        [_col(inp["norm_ffn1"][0]), _col(inp["norm_mix"][0]), _col(inp["norm_ffn2"][0]),
         _col(inp["norm_final"])], axis=1))

    def ffn_in(w):
        return np.ascontiguousarray(
            w.reshape(KC, 128, 2, NJ, 128).transpose(3, 1, 2, 0, 4).reshape(NJ, 128, 2048))

    def ffn_out(w):
        return np.ascontiguousarray(
            w.reshape(2, GJ, 128, KC, 128).transpose(0, 3, 2, 1, 4).reshape(16, 128, GJ * 128))

    shared["wf1i"] = ffn_in(inp["w_ffn1_in"][0])
    shared["wf1o"] = ffn_out(inp["w_ffn1_out"][0])
    shared["wf2i"] = ffn_in(inp["w_ffn2_in"][0])
    shared["wf2o"] = ffn_out(inp["w_ffn2_out"][0])
    shared["ident"] = np.eye(128, dtype=np.float32)

    win = inp["w_in"][0]
    wbm = inp["w_branch_mlstm"][0]
    wbg = inp["w_branch_gdn"][0]
    wo = inp["w_out"][0]
    O_MQ, O_MK, O_MV, O_MO, O_MG = 0, 1024, 2048, 3072, 4096
    O_GQ, O_GK, O_GV, O_GZ, O_GG = 4128, 5152, 6176, 7200, 8224
    O_MM, O_MGG = 8256, 9280
    units = []
    for h in range(8):
        units.append(np.concatenate([_chunk(win, O_MQ + h * 128), _chunk(win, O_MK + h * 128)], 1))
    for h in range(8):
        units.append(np.concatenate([_chunk(win, O_MV + h * 128), _chunk(win, O_MO + h * 128)], 1))
    for m in range(8):
        units.append(np.concatenate([_chunk(win, O_MM + m * 128), _chunk(wbm, m * 128)], 1))
    for mm_ in range(4):
        units.append(np.concatenate([_chunk(wo, 2 * mm_ * 128), _chunk(wo, (2 * mm_ + 1) * 128)], 1))
    for h in range(8):
        units.append(np.concatenate([_chunk(win, O_GQ + h * 128), _chunk(win, O_GK + h * 128)], 1))
    for h in range(8):
        units.append(np.concatenate([_chunk(win, O_GV + h * 128), _chunk(win, O_GZ + h * 128)], 1))
    for m in range(8):
        units.append(np.concatenate([_chunk(win, O_MGG + m * 128), _chunk(wbg, m * 128)], 1))
    shared["winu"] = np.ascontiguousarray(np.stack(units, 0))

    def gchunk(c0):
        return win[:, c0:c0 + 32].reshape(KC, 128, 32).transpose(1, 0, 2).reshape(128, 256)

    shared["wgates"] = np.ascontiguousarray(np.concatenate([gchunk(O_MG), gchunk(O_GG)], 1))
    hidx = np.arange(128) // 16
    gbias = np.zeros((128, 8), np.float32)
    mgb = inp["mlstm_gate_bias"][0]
    for g in range(4):
        gbias[:, g] = mgb[g][hidx]
    gbias[:, 4] = inp["gdn_a_log"][0][0][hidx]
    gbias[:, 5] = inp["gdn_a_log"][0][1][hidx]
    gbias[:, 6] = inp["gdn_dt_bias"][0][0][hidx]
    gbias[:, 7] = inp["gdn_dt_bias"][0][1][hidx]
    shared["gbias"] = gbias
    cwv = inp["gdn_conv_w"][0]
    shared["convw"] = np.ascontiguousarray(
        cwv.reshape(5, 24, 128).transpose(2, 1, 0).reshape(128, 120))
    shared["onorm"] = np.ascontiguousarray(
        np.stack([inp["mlstm_out_norm"][0], inp["gdn_out_norm"][0]], 1))
    p = np.arange(128)[:, None]
    j = np.arange(128)[None, :]
    tri_f = np.where(p <= j, 0.0, -BIG)
    tri_b = np.where(p >= j, 0.0, -BIG)
    big_f = np.where(j < p, 0.0, BIG)
    big_b = np.where(j > p, 0.0, BIG)
    shared["cmask"] = np.ascontiguousarray(
        np.stack([tri_f, tri_b, big_f, big_b], 1).reshape(128, 512).astype(np.float32))
    hk, ck = p // 16, p % 16
    hm, cm_ = j // 16, j % 16
    lf = ((hk == hm) & (ck < cm_)).astype(np.float32)
    lb = ((hk == hm) & (ck > cm_)).astype(np.float32)
    shared["clm"] = np.ascontiguousarray(np.stack([lf, lb], 1).reshape(128, 256))
    gms = []
    for dd in range(2):
        for lev in range(7):
            bsz = 1 << lev
            same = (p // (2 * bsz)) == (j // (2 * bsz))
            if dd == 0:
                mk_ = same & (p % (2 * bsz) >= bsz) & (j % (2 * bsz) < bsz)
            else:
                mk_ = same & (p % (2 * bsz) < bsz) & (j % (2 * bsz) >= bsz)
            gms.append(mk_)
    shared["gmask"] = np.ascontiguousarray(
        np.stack(gms, 1).reshape(128, 14 * 128).astype(ml_dtypes.bfloat16))
    in_maps = []
    for b in range(NCORES):
        m = dict(shared)
        m["xT"] = np.ascontiguousarray(inp["x"][b].T.reshape(KC, 128, S).transpose(1, 0, 2))
        m["ccol"] = _col(inp["c"][b])
        in_maps.append(m)
    return in_maps


_NC_CACHE = {}
DEBUG = None


def kernel(**inputs):
    inp = {k: np.asarray(v, dtype=np.float32) for k, v in inputs.items()}
    in_maps = host_layout(inp)
    if "nc" not in _NC_CACHE:
        _NC_CACHE["nc"] = build(DEBUG)
    nc = _NC_CACHE["nc"]
    res = run_bass_kernel_spmd(nc, in_maps, core_ids=list(range(NCORES)))
    out = np.empty((NCORES, S, D), np.float32)
    for b in range(NCORES):
        oT = np.asarray(res.results[b]["outT"]).reshape(128, KC, S)
        out[b] = oT.transpose(1, 0, 2).reshape(D, S).T
    return out
```

```python
import numpy as np
import ml_dtypes
from contextlib import ExitStack
import concourse.bass as bass
import concourse.mybir as mybir
from concourse.bass_utils import run_bass_kernel_spmd

F32 = mybir.dt.float32
BF16 = mybir.dt.bfloat16
AF = mybir.ActivationFunctionType
ALU = mybir.AluOpType

D = 1024
S = 2048
KC = 8
NTB = 4
DFF = 2816
NJ = 22
GJ = 11
EPS = 1e-6
NCORES = 8

ENGS = ("pe", "act", "dve", "pool", "sp")


class Tok:
    __slots__ = ("w", "r")

    def __init__(self):
        self.w = None
        self.r = {}


class Op:
    __slots__ = ("eng", "fn", "dma", "sig", "sigval", "idx", "deps", "dsem", "dval")


class Prog:
    def __init__(self):
        self.lists = {e: [] for e in ENGS}
        self.count = 0
        self.dma_ops = {e: [] for e in ENGS}

    def add(self, eng, fn, reads=(), writes=(), dma=False):
        o = Op()
        o.eng = eng
        o.fn = fn
        o.dma = dma
        o.sig = False
        o.sigval = 0
        o.idx = self.count
        self.count += 1
        deps = {}

        def dep(p, raw):
            if p is None:
                return
            if not p.dma and not dma and p.eng == eng and eng == "pe":
                return
            deps[p.idx] = p

        for t in reads:
            dep(t.w, True)
        for t in writes:
            dep(t.w, False)
            for p in t.r.values():
                dep(p, False)
        for t in reads:
            key = ("d", o.idx) if dma else eng
            t.r[key] = o
        for t in writes:
            t.w = o
            t.r = {}
        if dma:
            lst = self.dma_ops[eng]
            o.dsem = None
            lst.append(o)
        o.deps = list(deps.values())
        self.lists[eng].append(o)
        return o

    def finalize(self, dma_pool_size):
        for e in ENGS:
            lst = self.dma_ops[e]
            K = dma_pool_size[e]
            for i, o in enumerate(lst):
                o.dsem = (e, i % K)
                o.dval = 16 * (i // K + 1)
                if i >= K:
                    o.deps.append(lst[i - K])
        for e in ENGS:
            for o in self.lists[e]:
                for p in o.deps:
                    if not p.dma:
                        p.sig = True
        for e in ENGS:
            n = 0
            for o in self.lists[e]:
                if o.sig and not o.dma:
                    n += 1
                    o.sigval = n

    def emit(self, ename, eng, engsem, dmasem):
        waited = {}
        for o in self.lists[ename]:
            need = {}
            for p in o.deps:
                if p.dma:
                    key = ("d",) + p.dsem
                    val = p.dval
                else:
                    key = ("e", p.eng)
                    val = p.sigval
                if need.get(key, 0) < val:
                    need[key] = val
            for key, val in need.items():
                if waited.get(key, 0) < val:
                    sem = dmasem[key[1]][key[2]] if key[0] == "d" else engsem[key[1]]
                    eng.wait_ge(sem, val)
                    waited[key] = val
            if o.fn is None:
                continue
            ins = o.fn(eng)
            if o.dma:
                ins.then_inc(dmasem[o.dsem[0]][o.dsem[1]], 16)
            elif o.sig:
                ins.then_inc(engsem[ename], 1)


def _host_consts():
    c = {}
    ident = np.eye(128, dtype=np.float32)
    c["ident"] = ident
    c["ones"] = np.ones((128, 128), np.float32)
    return c


class Builder:
    def __init__(self, debug=None):
        self.debug = debug
        self.nc = bass.Bass("TRN2", target_bir_lowering=False)
        self.P = Prog()
        self.es = ExitStack()
        self.dram = {}
        self.outs = {}

    def din(self, name, shape, dtype=F32):
        t = self.nc.dram_tensor(name, list(shape), dtype, kind="ExternalInput").ap()
        self.dram[name] = t
        return t

    def dout(self, name, shape, dtype=F32):
        t = self.nc.dram_tensor(name, list(shape), dtype, kind="ExternalOutput").ap()
        self.outs[name] = t
        return t

    def sb(self, name, shape, dtype=F32):
        h = self.es.enter_context(self.nc.sbuf_tensor(name, list(shape), dtype))
        return h

    def ps(self, name, shape, dtype=F32):
        h = self.es.enter_context(self.nc.psum_tensor(name, list(shape), dtype))
        return h

    def mm(self, out, lhsT, rhs, start, stop, reads, writes, skip=False):
        if skip:
            return self.P.add("pe", lambda e: e.matmul(out, lhsT, rhs, start=start, stop=stop,
                                                       skip_group_check=True), reads, writes)
        return self.P.add("pe", lambda e: e.matmul(out, lhsT, rhs, start=start, stop=stop),
                          reads, writes)

    def tr(self, out, in_, ident, reads, writes):
        return self.P.add("pe", lambda e: e.transpose(out, in_, ident), reads, writes)

    def act(self, out, in_, func, reads, writes, bias=0.0, scale=1.0, eng="act"):
        return self.P.add(eng, lambda e: e.activation(out, in_, func, bias=bias, scale=scale),
                          reads, writes)

    def tt(self, eng, out, in0, in1, op, reads, writes):
        return self.P.add(eng, lambda e: e.tensor_tensor(out, in0, in1, op), reads, writes)

    def stt(self, out, in0, scalar, in1, op0, op1, reads, writes, eng="dve"):
        return self.P.add(eng, lambda e: e.scalar_tensor_tensor(out, in0, scalar, in1, op0, op1),
                          reads, writes)

    def ts(self, eng, out, in0, s1, s2, op0, op1, reads, writes):
        if s2 is None:
            return self.P.add(eng, lambda e: e.tensor_scalar(out, in0, s1, None, op0), reads, writes)
        return self.P.add(eng, lambda e: e.tensor_scalar(out, in0, s1, s2, op0, op1), reads, writes)

    def cp(self, eng, out, in_, reads, writes):
        if eng == "act":
            return self.P.add(eng, lambda e: e.copy(out, in_), reads, writes)
        return self.P.add(eng, lambda e: e.tensor_copy(out, in_), reads, writes)

    def recip(self, out, in_, reads, writes):
        return self.P.add("dve", lambda e: e.reciprocal(out, in_), reads, writes)

    def memset(self, eng, ap, val, reads, writes):
        return self.P.add(eng, lambda e: e.memset(ap, val), reads, writes)

    def dma(self, out, in_, reads, writes, eng="sp"):
        return self.P.add(eng, lambda e: e.dma_start(out, in_), reads, writes, dma=True)


BIG = 30000.0
NHC = 24
NCB = 12
MSC_XM = 10496


def inherit(new_toks, old_toks):
    ops = {}
    for t in old_toks:
        if t.w is not None:
            ops[t.w.idx] = t.w
        for p in t.r.values():
            ops[p.idx] = p
    for nt in new_toks:
        for i, p in ops.items():
            nt.r[("f", i)] = p


def build(debug=None):
    B = Builder(debug)
    nc = B.nc
    P = B.P

    xT_d = B.din("xT", [128, KC, S])
    c_d = B.din("ccol", [128, KC])
    wada_d = B.din("wada", [18, 128, KC * 512])
    bada_d = B.din("bada", [1, 9216])
    normw_d = B.din("normw", [128, 4 * KC])
    wf1i_d = B.din("wf1i", [NJ, 128, 2048])
    wf1o_d = B.din("wf1o", [16, 128, GJ * 128])
    wf2i_d = B.din("wf2i", [NJ, 128, 2048])
    wf2o_d = B.din("wf2o", [16, 128, GJ * 128])
    ident_d = B.din("ident", [128, 128])
    win_d = B.din("winu", [52, 128, 2048])
    wg_d = B.din("wgates", [128, 512])
    gb_d = B.din("gbias", [128, 8])
    cw_d = B.din("convw", [128, 120])
    on_d = B.din("onorm", [128, 2])
    cm_d = B.din("cmask", [128, 512])
    lm_d = B.din("clm", [128, 256])
    gm_d = B.din("gmask", [128, 14 * 128], BF16)
    out_d = B.dout("outT", [128, KC, S])
    xsp_d = nc.dram_tensor("xspill", [128, KC, S], F32, kind="Internal").ap()
    bc_d = nc.dram_tensor("bcd", [6, 128, 128], F32, kind="Internal").ap()

    xT = B.sb("xT_sb", [128, KC, S])
    hT = B.sb("hT_sb", [128, KC, S], BF16)
    arena = B.sb("arena", [128, 11264])
    NST = 2
    NBF = 3
    wst = B.sb("wst", [128, NST, 2048])
    wbf = B.sb("wbf", [128, NBF, 2048], BF16)
    modrow = B.sb("modrow", [1, 2, 512])
    brow = B.sb("brow", [1, 2, 512])
    ccol = B.sb("ccol_sb", [128, KC])
    cs = B.sb("cs_sb", [128, KC])
    modT = B.sb("modT", [128, 72])
    normw = B.sb("normw_sb", [128, 4 * KC])
    acol = B.sb("acol", [128, 3 * KC])
    gcol = B.sb("gcol", [128, 3 * KC])
    ident = B.sb("ident_sb", [128, 128])
    identb = B.sb("identb_sb", [128, 128], BF16)
    identb4 = B.sb("identb4", [128, 4, 128], BF16)
    onesb = B.sb("onesb", [128, 128], BF16)
    one11 = B.sb("one11", [1, 2])
    epsc = B.sb("epsc", [128, 1])
    tmpA = B.sb("tmpA", [128, 2, 512])
    tmpB = B.sb("tmpB", [128, 2, 512])
    rstd = B.sb("rstd", [128, 2, 512])
    nrmA = B.sb("nrmA", [128, 2, 512])
    sqb = B.sb("sqb", [128, 2, 512], BF16)
    wgb = B.sb("wgb", [128, 512], BF16)
    gb = B.sb("gb_sb", [128, 8])
    nega = B.sb("nega", [128, 2])
    cw = B.sb("cw_sb", [128, 120])
    onorm = B.sb("onorm_sb", [128, 2])
    cmask = B.sb("cmask_sb", [128, 4, 128])
    clm = B.sb("clm_sb", [128, 2, 128])
    eglt = B.sb("eglt", [128, 2, 4])
    gmaskb = B.sb("gmaskb", [128, 14, 128], BF16)

    banks = [B.ps("bank%d" % i, [128, 512]) for i in range(8)]
    bk = [Tok() for _ in range(8)]

    t_x = [[Tok() for _ in range(NTB)] for _ in range(KC)]
    t_h = [Tok() for _ in range(NTB)]
    t_const = Tok()
    t_mod = Tok()
    t_fin = [Tok(), Tok()]
    t_sqs = [Tok(), Tok()]
    t_rs = [Tok(), Tok()]
    t_sil = [Tok(), Tok()]
    out_toks = []

    B.dma(ident[:], ident_d, [], [t_const])
    B.dma(ccol[:], c_d, [], [t_const])
    B.dma(normw[:], normw_d, [], [t_const])
    P.add("pool", lambda e: e.memset(onesb[:], 1.0), [], [t_const])
    P.add("pool", lambda e: e.memset(one11[:], 1.0), [], [t_const])
    P.add("pool", lambda e: e.memset(epsc[:], EPS), [], [t_const])
    B.cp("pool", identb[:], ident[:], [t_const], [t_const])
    t_c2 = Tok()
    B.dma(gb[:], gb_d, [], [t_c2], eng="act")
    B.dma(cw[:], cw_d, [], [t_c2], eng="act")
    B.dma(onorm[:], on_d, [], [t_c2], eng="act")
    B.dma(cmask[:].rearrange("p a b -> p (a b)"), cm_d, [], [t_c2], eng="act")
    B.dma(clm[:].rearrange("p a b -> p (a b)"), lm_d, [], [t_c2], eng="act")
    B.dma(gmaskb[:].rearrange("p a b -> p (a b)"), gm_d, [], [t_c2], eng="act")
    for r in range(4):
        B.cp("pool", identb4[:, r, :], ident[:], [t_const], [t_c2])
    B.act(nega[:], gb[:, 4:6], AF.Exp, [t_c2], [t_c2])
    B.ts("dve", nega[:], nega[:], -1.0, None, ALU.mult, None, [t_c2], [t_c2])

    for kc in range(KC):
        B.dma(xT[:, kc, :], xT_d[:, kc, :], [], t_x[kc], eng="act" if kc % 2 else "sp")

    t_cs = Tok()
    B.act(cs[:], ccol[:], AF.Silu, [t_const], [t_cs])
    t_ada = [Tok(), Tok()]
    t_brow = [Tok(), Tok()]
    t_modrow = [Tok(), Tok()]
    adabuf = arena[:, 0:2 * KC * 512].rearrange("p (s n) -> p s n", s=2)
    for blk in range(18):
        sl = blk % 2
        B.dma(adabuf[:, sl, :], wada_d[blk], [], [t_ada[sl]], eng="sp")
        B.dma(brow[0:1, sl, :], bada_d[0:1, blk * 512:(blk + 1) * 512], [], [t_brow[sl]], eng="act")
        b = banks[sl]
        for kc in range(KC):
            B.mm(b[0:1, :], cs[:, kc:kc + 1], adabuf[:, sl, kc * 512:(kc + 1) * 512],
                 kc == 0, False, [t_cs, t_ada[sl]], [bk[sl]])
        B.mm(b[0:1, :], one11[0:1, 0:1], brow[0:1, sl, :], False, True,
             [t_const, t_brow[sl]], [bk[sl]])
        B.cp("act", modrow[0:1, sl, :], b[0:1, :], [bk[sl]], [t_modrow[sl]])
        for q in range(4):
            j = blk * 4 + q
            B.mm(banks[2][:, j:j + 1], modrow[0:1, sl, q * 128:(q + 1) * 128], one11[0:1, 0:1],
                 True, True, [t_modrow[sl], t_const], [bk[2]])
    B.cp("dve", modT[:], banks[2][:, 0:72], [bk[2]], [t_mod])
    for a in range(3):
        B.stt(acol[:, a * KC:(a + 1) * KC], modT[:, (a * 3 + 1) * KC:(a * 3 + 2) * KC], 1.0,
              normw[:, a * KC:(a + 1) * KC], ALU.add, ALU.mult, [t_mod, t_const], [t_mod])
        B.ts("dve", gcol[:, a * KC:(a + 1) * KC], modT[:, (a * 3 + 2) * KC:(a * 3 + 3) * KC],
             0.5 if a != 1 else 1.0, None, ALU.mult, None, [t_mod], [t_mod])
    arena_toks = list(t_ada)

    st_tok = [Tok() for _ in range(NST)]
    bf_tok = [Tok() for _ in range(NBF)]
    wq = []
    wstate = {"loaded": 0, "used": 0}
    PF = 3

    wstate["dma"] = 0
    wstate["cast"] = 0

    def w_dma():
        i = wstate["dma"]
        src, n = wq[i]
        B.dma(wst[:, i % NST, 0:n], src, [], [st_tok[i % NST]], eng="sp")
        wstate["dma"] += 1

    def w_cast():
        i = wstate["cast"]
        src, n = wq[i]
        B.cp("pool", wbf[:, i % NBF, 0:n], wst[:, i % NST, 0:n], [st_tok[i % NST]],
             [bf_tok[i % NBF]])
        wstate["cast"] += 1

    def w_next():
        i = wstate["used"]
        while wstate["cast"] < min(len(wq), i + 2):
            while wstate["dma"] <= wstate["cast"]:
                w_dma()
            w_cast()
        while wstate["dma"] < min(len(wq), i + 3) and wstate["dma"] < wstate["cast"] + NST:
            w_dma()
        wstate["used"] += 1
        return wbf[:, i % NBF, :], bf_tok[i % NBF]

    def declare_ffn(wi, wo):
        for g in range(2):
            for jj in range(GJ):
                wq.append((wi[g * GJ + jj], 2048))
            for m in range(KC):
                wq.append((wo[g * KC + m], GJ * 128))

    declare_ffn(wf1i_d, wf1o_d)
    if debug not in ("nomix",):
        wq.append((wg_d, 512))
        for h in range(8):
            wq.append((win_d[h], 2048))
            wq.append((win_d[8 + h], 2048))
        if debug != "mlstm_only":
            pass
        for m in range(8):
            wq.append((win_d[16 + m], 2048))
        for mm_ in range(4):
            wq.append((win_d[24 + mm_], 2048))
        for h in range(8):
            wq.append((win_d[28 + h], 2048))
            wq.append((win_d[36 + h], 2048))
        for m in range(8):
            wq.append((win_d[44 + m], 2048))
        for mm_ in range(4):
            wq.append((win_d[24 + mm_], 2048))
    declare_ffn(wf2i_d, wf2o_d)

    nstate = {"i": 0}

    def norm_block(a, tb, final=False):
        i = nstate["i"]
        nstate["i"] += 1
        sl = i % 2
        bi = 6 + sl
        tsl = slice(tb * 512, (tb + 1) * 512)
        for kc in range(KC):
            B.act(sqb[:, kc % 2, :], xT[:, kc, tsl], AF.Square, [t_x[kc][tb]], [t_sqs[kc % 2]])
            B.mm(banks[bi][:], onesb[:], sqb[:, kc % 2, :], kc == 0, kc == KC - 1,
                 [t_sqs[kc % 2], t_const], [bk[bi]])
        t_r = t_rs[sl]
        B.act(nrmA[:, sl, :], banks[bi][:], AF.Ln, [bk[bi], t_const], [t_r], bias=epsc[:, 0:1], scale=1.0 / D)
        B.act(rstd[:, sl, :], nrmA[:, sl, :], AF.Exp, [t_r], [t_r], scale=-0.5)
        for kc in range(KC):
            if final:
                B.stt(tmpB[:, kc % 2, :], xT[:, kc, tsl], normw[:, 3 * KC + kc:3 * KC + kc + 1],
                      rstd[:, sl, :], ALU.mult, ALU.mult, [t_x[kc][tb], t_r, t_const],
                      [t_fin[kc % 2]])
                t_o = Tok()
                out_toks.append(t_o)
                B.dma(out_d[:, kc, tsl], tmpB[:, kc % 2, :], [t_fin[kc % 2]], [t_o],
                      eng="sp")
            else:
                B.stt(tmpB[:, kc % 2, :], xT[:, kc, tsl], acol[:, a * KC + kc:a * KC + kc + 1],
                      rstd[:, sl, :], ALU.mult, ALU.mult, [t_x[kc][tb], t_r, t_mod],
                      [t_fin[kc % 2]])
                B.act(hT[:, kc, tsl], tmpB[:, kc % 2, :], AF.Identity, [t_fin[kc % 2], t_mod],
                      [t_h[tb]], bias=modT[:, a * 3 * KC + kc:a * 3 * KC + kc + 1], scale=1.0)

    def ffn(a):
        nonlocal arena_toks
        for tb in range(NTB):
            norm_block(a, tb)
        actT = arena[:].bitcast(BF16)
        t_act = [[Tok() for _ in range(NTB)] for _ in range(GJ)]
        flat = [t for row in t_act for t in row]
        inherit(flat, arena_toks)
        pi = 0
        for g in range(2):
            for jj in range(GJ):
                w, wt = w_next()
                for tb in range(NTB):
                    tsl = slice(tb * 512, (tb + 1) * 512)
                    bg = (pi % 2) * 2
                    bu = bg + 1
                    pi += 1
                    for kc in range(KC):
                        B.mm(banks[bg][:], w[:, kc * 128:(kc + 1) * 128], hT[:, kc, tsl],
                             kc == 0, kc == KC - 1, [wt, t_h[tb]], [bk[bg]])
                    for kc in range(KC):
                        B.mm(banks[bu][:], w[:, 1024 + kc * 128:1024 + (kc + 1) * 128],
                             hT[:, kc, tsl], kc == 0, kc == KC - 1, [wt, t_h[tb]], [bk[bu]])
                    sl = pi % 2
                    B.act(tmpA[:, sl, :], banks[bg][:], AF.Silu, [bk[bg]], [t_sil[sl]])
                    B.tt("dve", actT[:, jj * S + tb * 512: jj * S + (tb + 1) * 512],
                         banks[bu][:], tmpA[:, sl, :], ALU.mult, [bk[bu], t_sil[sl]],
                         [t_act[jj][tb]])
            for m in range(KC):
                w, wt = w_next()
                for tb in range(NTB):
                    tsl = slice(tb * 512, (tb + 1) * 512)
                    bo = 4 + (pi % 2)
                    pi += 1
                    for jj in range(GJ):
                        B.mm(banks[bo][:], w[:, jj * 128:(jj + 1) * 128],
                             actT[:, jj * S + tb * 512: jj * S + (tb + 1) * 512],
                             jj == 0, jj == GJ - 1, [wt, t_act[jj][tb]], [bk[bo]])
                    B.stt(xT[:, m, tsl], banks[bo][:], gcol[:, a * KC + m:a * KC + m + 1],
                          xT[:, m, tsl], ALU.mult, ALU.add, [bk[bo], t_mod, t_x[m][tb]],
                          [t_x[m][tb]])
        arena_toks = flat

    msc = xT[:].rearrange("p a b -> p (a b)")

    def bfv(lo, n_words):
        return msc[:, lo:lo + n_words].bitcast(BF16)

    qT = bfv(0, 1024)
    kT = bfv(1024, 1024)
    gateT = bfv(2048, 1024)
    v_tok = bfv(3072, 1024)
    k_tok = bfv(4096, 1024)
    hacc = msc[:, 5120:7168]
    cb = [bfv(7168 + i * 256, 256) for i in range(NCB)]
    Sst = msc[:, 10240:10368]
    Sb = bfv(10368, 64)
    vnb = bfv(10432, 64)
    xm = msc[:, MSC_XM:MSC_XM + 2 * 2048].rearrange("p (s n) -> p s n", s=2)
    yT = bfv(0, 8192)
    t_q, t_k, t_gate, t_vtok, t_ktok = Tok(), Tok(), Tok(), Tok(), Tok()
    t_hacc = [Tok() for _ in range(NTB)]
    t_cb = [Tok() for _ in range(NCB)]
    t_S, t_Sb, t_vnb, t_egl = Tok(), Tok(), Tok(), Tok()
    t_xm = [Tok(), Tok()]
    msc_toks = [t_q, t_k, t_gate, t_vtok, t_ktok] + t_hacc + t_cb + [t_S, t_Sb, t_vnb]
    t_y = [[Tok() for _ in range(NTB)] for _ in range(KC)]
    y_flat = [t for row in t_y for t in row]

    bcs = [msc[:, 13312:13824], msc[:, 13824:14336], msc[:, 14848:15360], msc[:, 15360:15872]]
    t_bc = [Tok() for _ in range(4)]
    t_bcd = [Tok() for _ in range(6)]
    msc_toks += t_bc

    def bc_load(slot, q, h, tb):
        r0 = h * 16 + tb * 4
        src = bc_d[q, r0:r0 + 4, :].rearrange("a b -> (a b)").partition_broadcast(128)
        B.dma(bcs[slot], src, [t_bcd[q]], [t_bc[slot]], eng="act")

    cb1 = [bfv(MSC_XM + i * 256, 256) for i in range(11)]
    t_cb1 = [Tok() for _ in range(11)]
    S1 = msc[:, 14592:14720]
    Sb1 = bfv(14720, 64)
    vnb1 = bfv(14784, 64)
    t_S1, t_Sb1, t_vnb1 = Tok(), Tok(), Tok()
    t_egl1 = Tok()
    msc_toks += t_cb1 + [t_S1, t_Sb1, t_vnb1]
    chainR = [
        dict(banks=[0, 1, 2, 3], cb=cb, tcb=t_cb, E=(tmpB[:, 0, :], t_fin[0]),
             XA=(tmpA[:, 0, :], t_sil[0]), XT=(rstd[:, 0, :], t_rs[0]), EG=(nrmA[:, 0, :], t_rs[0]),
             S=(Sst, t_S), Sb=(Sb, t_Sb), vnb=(vnb, t_vnb), egl=(eglt[:, 0, :], t_egl)),
        dict(banks=[4, 5, 6, 7], cb=cb1, tcb=t_cb1, E=(tmpB[:, 1, :], t_fin[1]),
             XA=(tmpA[:, 1, :], t_sil[1]), XT=(rstd[:, 1, :], t_rs[1]), EG=(nrmA[:, 1, :], t_rs[1]),
             S=(S1, t_S1), Sb=(Sb1, t_Sb1), vnb=(vnb1, t_vnb1), egl=(eglt[:, 1, :], t_egl1)),
    ]

    hXT = arena[:, 0:8192].bitcast(BF16)
    t_hx = [[Tok() for _ in range(NTB)] for _ in range(8)]
    hx_flat = [t for row in t_hx for t in row]
    hcap = [arena[:, 8192 + i * 128: 8192 + (i + 1) * 128] for i in range(NHC)]
    t_hc = [Tok() for _ in range(NHC)]
    ZER = 21

    tri = [cmask[:, 0, :], cmask[:, 1, :]]
    bigA = [cmask[:, 2, :], cmask[:, 3, :]]

    rot = {"b": 0, "d": 0}

    def rbank():
        rot["b"] ^= 1
        return rot["b"]

    def hc_transpose_to(dst, src, bslot):
        B.tr(banks[6][:, bslot * 128:(bslot + 1) * 128], hcap[src], ident[:],
             [t_hc[src], t_const], [bk[6]])
        B.cp("dve", hcap[dst], banks[6][:, bslot * 128:(bslot + 1) * 128], [bk[6]], [t_hc[dst]])

    def gate_project(wcol0, dst_tiles, bias_cols):
        for c in range(16):
            for kc in range(KC):
                B.mm(banks[5][:, c * 32:(c + 1) * 32], hT[:, kc, c * 128:(c + 1) * 128],
                     wgb[:, wcol0 + kc * 32: wcol0 + (kc + 1) * 32], kc == 0, kc == KC - 1,
                     [t_h[c // 4], t_c2], [bk[5]])
        src = banks[5][:].rearrange("p (c g h) -> p g h c", c=16, g=4, h=8)
        dst = tmpA[:, 0, :].rearrange("p (g h c) -> p g h c", g=4, h=8, c=16)
        for g in range(4):
            B.cp("dve", dst[:, g], src[:, g], [bk[5]], [t_sil[0]])
        for g in range(4):
            B.tr(banks[6][:, g * 128:(g + 1) * 128], tmpA[:, 0, g * 128:(g + 1) * 128], ident[:],
                 [t_sil[0], t_const], [bk[6]])
        for g in range(4):
            i = dst_tiles[g]
            if bias_cols[g] is None:
                B.cp("act", hcap[i], banks[6][:, g * 128:(g + 1) * 128], [bk[6]], [t_hc[i]])
            else:
                B.act(hcap[i], banks[6][:, g * 128:(g + 1) * 128], AF.Identity, [bk[6], t_c2],
                      [t_hc[i]], bias=gb[:, bias_cols[g]:bias_cols[g] + 1], scale=1.0)

    def scan(dst, src, op0):
        B.P.add("dve", lambda e: e.tensor_tensor_scan(hcap[dst], hcap[src], hcap[ZER], 0.0, op0,
                                                      ALU.add),
                [t_hc[src], t_hc[ZER]], [t_hc[dst]])

    def hts(dst, src, s1, s2, op0, op1, extra=()):
        B.ts("dve", hcap[dst], hcap[src], s1, s2, op0, op1, [t_hc[src]] + list(extra), [t_hc[dst]])

    def htt(dst, a_, b_, op, eng="dve"):
        B.tt(eng, hcap[dst], hcap[a_], hcap[b_], op, [t_hc[a_], t_hc[b_]], [t_hc[dst]])

    def hact(dst, src, func, bias=0.0, scale=1.0, extra=()):
        B.act(hcap[dst], hcap[src], func, [t_hc[src]] + list(extra), [t_hc[dst]], bias=bias,
              scale=scale)

    def cross_chunk_sum(dst_col_tile, src, col, d):
        B.mm(banks[6][:, 0:1], clm[:, d, :], hcap[src][:, col:col + 1], True, True,
             [t_c2, t_hc[src]], [bk[6]])
        B.cp("dve", hcap[dst_col_tile][:, 0:1], banks[6][:, 0:1], [bk[6]], [t_hc[dst_col_tile]])

    def cross_chunk_max(dst_col_tile, dcol, src, col, d, ROW, RA, RB):
        B.tr(banks[6][0:1, 128:256], hcap[src][:, col:col + 1], ident[:], [t_hc[src], t_const],
             [bk[6]])
        B.cp("dve", hcap[ROW][0:1, :], banks[6][0:1, 128:256], [bk[6]], [t_hc[ROW]])
        row3 = hcap[ROW][0:1, :].rearrange("p (h c) -> p h c", h=8)
        cur, oth = RA, RB
        c3 = hcap[cur][0:1, :].rearrange("p (h c) -> p h c", h=8)
        B.memset("dve", hcap[cur][0:1, :], 0.0, [], [t_hc[cur]])
        if d == 0:
            B.cp("dve", c3[:, :, 1:16], row3[:, :, 0:15], [t_hc[ROW]], [t_hc[cur]])
        else:
            B.cp("dve", c3[:, :, 0:15], row3[:, :, 1:16], [t_hc[ROW]], [t_hc[cur]])
        for sh in (1, 2, 4, 8):
            c3 = hcap[cur][0:1, :].rearrange("p (h c) -> p h c", h=8)
            o3 = hcap[oth][0:1, :].rearrange("p (h c) -> p h c", h=8)
            if d == 0:
                B.tt("dve", o3[:, :, sh:16], c3[:, :, sh:16], c3[:, :, 0:16 - sh], ALU.max,
                     [t_hc[cur]], [t_hc[oth]])
                B.cp("dve", o3[:, :, 0:sh], c3[:, :, 0:sh], [t_hc[cur]], [t_hc[oth]])
            else:
                B.tt("dve", o3[:, :, 0:16 - sh], c3[:, :, 0:16 - sh], c3[:, :, sh:16], ALU.max,
                     [t_hc[cur]], [t_hc[oth]])
                B.cp("dve", o3[:, :, 16 - sh:16], c3[:, :, 16 - sh:16], [t_hc[cur]], [t_hc[oth]])
            cur, oth = oth, cur
        B.mm(banks[6][:, 1:2], hcap[cur][0:1, :], one11[0:1, 0:1], True, True,
             [t_hc[cur], t_const], [bk[6]])
        B.cp("dve", hcap[dst_col_tile][:, dcol:dcol + 1], banks[6][:, 1:2], [bk[6]],
             [t_hc[dst_col_tile]])

    NEGM = [9, 14]
    FLOOR = [10, 15]
    UCOL = [11, 16]

    def softplus_acc(dst, src, sign, tA, tB, tC):
        hact(tA, src, AF.Abs)
        hact(tB, tA, AF.Exp, scale=-1.0)
        hts(tA, tB, 2.0, None, ALU.add, None)
        B.recip(hcap[tA], hcap[tA], [t_hc[tA]], [t_hc[tA]])
        htt(tB, tB, tA, ALU.mult)
        htt(tA, tB, tB, ALU.mult)
        hts(tC, tA, 1.0 / 13.0, 1.0 / 11.0, ALU.mult, ALU.add)
        for cst in (1.0 / 9.0, 1.0 / 7.0, 1.0 / 5.0, 1.0 / 3.0, 1.0):
            htt(tC, tC, tA, ALU.mult)
            hts(tC, tC, cst, None, ALU.add, None)
        htt(tC, tC, tB, ALU.mult)
        hts(tA, src, float(sign), 0.0, ALU.mult, ALU.max)
        B.stt(hcap[dst], hcap[tC], 2.0, hcap[tA], ALU.mult, ALU.add, [t_hc[tC], t_hc[tA]],
              [t_hc[dst]])

    def mlstm_gates():
        P.add("pool", lambda e: e.memset(hcap[ZER], 0.0), [], [t_hc[ZER]])
        gate_project(0, [0, 1, 2, 3], [0, 1, 2, 3])
        for d in range(2):
            Gi, Gf = 2 * d, 2 * d + 1
            T1, SPt, LCS, CSP, U, LMX, X, COLS = 4, 5, 6, 7, 8, 12, 13, 20
            softplus_acc(SPt, Gf, -1.0, T1, LCS, CSP)
            scan(LCS, SPt, ALU.add)
            cross_chunk_sum(COLS, LCS, 127, d)
            if d == 1:
                hts(X, LCS, -1.0, hcap[LCS][:, 127:128], ALU.mult, ALU.add)
                htt(LCS, X, SPt, ALU.add)
            hts(CSP, LCS, hcap[COLS][:, 0:1], None, ALU.add, None, extra=[t_hc[COLS]])
            htt(U, Gi, CSP, ALU.add)
            if d == 0:
                scan(LMX, U, ALU.max)
                ccol_ = 127
            else:
                hts(X, U, 0.0, None, ALU.max, None)
                cur, oth = X, LMX
                for sh in (1, 2, 4, 8, 16, 32, 64):
                    B.tt("dve", hcap[oth][:, 0:128 - sh], hcap[cur][:, 0:128 - sh],
                         hcap[cur][:, sh:128], ALU.max, [t_hc[cur]], [t_hc[oth]])
                    B.cp("dve", hcap[oth][:, 128 - sh:128], hcap[cur][:, 128 - sh:128],
                         [t_hc[cur]], [t_hc[oth]])
                    cur, oth = oth, cur
                if cur != LMX:
                    B.cp("dve", hcap[LMX], hcap[cur], [t_hc[cur]], [t_hc[LMX]])
                ccol_ = 0
            cross_chunk_max(COLS, 1, LMX, ccol_, d, 17, 18, 19)
            hts(X, LMX, hcap[COLS][:, 1:2], None, ALU.max, None, extra=[t_hc[COLS]])
            hts(NEGM[d], X, -1.0, None, ALU.mult, None)
            htt(T1, CSP, X, ALU.subtract)
            hact(FLOOR[d], T1, AF.Exp)
            hc_transpose_to(UCOL[d], U, 2)
            B.dma(bc_d[d], hcap[NEGM[d]], [t_hc[NEGM[d]]], [t_bcd[d]], eng="act")
            B.dma(bc_d[2 + d], hcap[FLOOR[d]], [t_hc[FLOOR[d]]], [t_bcd[2 + d]], eng="act")

    def proj_fm(wsl, wt, evac):
        for tb in range(NTB):
            tsl = slice(tb * 512, (tb + 1) * 512)
            bi = rbank()
            for kc in range(KC):
                B.mm(banks[bi][:], wsl[:, kc * 128:(kc + 1) * 128], hT[:, kc, tsl],
                     kc == 0, kc == KC - 1, [wt, t_h[tb]], [bk[bi]])
            evac(tb, tsl, bi)

    def out_norm(h, which):
        for tb in range(NTB):
            tsl = slice(tb * 512, (tb + 1) * 512)
            sl = tb % 2
            bi = 6 + sl
            B.act(sqb[:, sl, :], hacc[:, tsl], AF.Square, [t_hacc[tb]], [t_sqs[sl]])
            B.mm(banks[bi][:], onesb[:], sqb[:, sl, :], True, True, [t_sqs[sl], t_const], [bk[bi]])
            B.act(nrmA[:, sl, :], banks[bi][:], AF.Ln, [bk[bi], t_const], [t_rs[sl]], bias=epsc[:, 0:1],
                  scale=1.0 / 128)
            B.act(rstd[:, sl, :], nrmA[:, sl, :], AF.Exp, [t_rs[sl]], [t_rs[sl]], scale=-0.5)
            B.stt(tmpB[:, sl, :], hacc[:, tsl], onorm[:, which:which + 1], rstd[:, sl, :],
                  ALU.mult, ALU.mult, [t_hacc[tb], t_c2, t_rs[sl]], [t_fin[sl]])
            B.tt("dve", hXT[:, h * S + tb * 512: h * S + (tb + 1) * 512], tmpB[:, sl, :],
                 gateT[:, tsl], ALU.mult, [t_fin[sl], t_gate], [t_hx[h][tb]])

    def selector(bank_i, src_tile, h, tb):
        for r in range(4):
            hc_ = h * 16 + tb * 4 + r
            B.mm(banks[bank_i][:, r * 128:(r + 1) * 128],
                 ident[:, hc_:hc_ + 1].to_broadcast([128, 128]), hcap[src_tile], True, True,
                 [t_const, t_hc[src_tile]], [bk[bank_i]])

    pend_on = {"f": None}

    def mlstm_head(h):
        w, wt = w_next()
        proj_fm(w[:, 0:1024], wt, lambda tb, tsl, bi: B.act(
            qT[:, tsl], banks[bi][:], AF.Identity, [bk[bi]], [t_q], scale=128.0 ** -0.5))
        proj_fm(w[:, 1024:2048], wt, lambda tb, tsl, bi: B.cp(
            "dve", kT[:, tsl], banks[bi][:], [bk[bi]], [t_k]))
        if pend_on["f"] is not None:
            pend_on["f"]()
            pend_on["f"] = None
        w, wt = w_next()
        for tc in range(16):
            q4 = tc % 4
            for kc in range(KC):
                B.mm(banks[5][:, q4 * 128:(q4 + 1) * 128], hT[:, kc, tc * 128:(tc + 1) * 128],
                     w[:, kc * 128:(kc + 1) * 128], kc == 0, kc == KC - 1, [wt, t_h[tc // 4]],
                     [bk[5]])
            if q4 == 3:
                B.cp("act", v_tok[:, (tc - 3) * 128:(tc + 1) * 128], banks[5][:], [bk[5]],
                     [t_vtok])
        proj_fm(w[:, 1024:2048], wt, lambda tb, tsl, bi: B.act(
            gateT[:, tsl], banks[bi][:], AF.Sigmoid, [bk[bi]], [t_gate]))

        iters = [(0, tb) for tb in range(NTB)] + [(1, tb) for tb in range(NTB - 1, -1, -1)]
        bc_load(0, 0, h, 0)
        bc_load(1, 2, h, 0)
        bc_load(3, 2 + iters[1][0], h, iters[1][1])
        pend = {"g": None}

        def step_pending():
            if pend["g"] is not None:
                try:
                    next(pend["g"])
                except StopIteration:
                    pend["g"] = None

        def epilogue_gen(d, tb, tsl, bn_, bd_, FL, tFL):
            B.act(nrmA[:, 0, :], banks[bd_][:], AF.Abs, [bk[bd_]], [t_rs[0]])
            yield
            B.tt("dve", nrmA[:, 0, :], FL, nrmA[:, 0, :], ALU.max, [tFL, t_rs[0]], [t_rs[0]])
            yield
            B.act(nrmA[:, 0, :], nrmA[:, 0, :], AF.Ln, [t_rs[0]], [t_rs[0]])
            yield
            B.act(rstd[:, 0, :], nrmA[:, 0, :], AF.Exp, [t_rs[0]], [t_rs[0]], scale=-1.0)
            yield
            if d == 0:
                B.tt("dve", hacc[:, tsl], banks[bn_][:], rstd[:, 0, :], ALU.mult,
                     [bk[bn_], t_rs[0]], [t_hacc[tb]])
            else:
                B.tt("dve", nrmA[:, 0, :], banks[bn_][:], rstd[:, 0, :], ALU.mult,
                     [bk[bn_], t_rs[0]], [t_rs[0]])
                yield
                B.tt("pool", hacc[:, tsl], hacc[:, tsl], nrmA[:, 0, :], ALU.add,
                     [t_hacc[tb], t_rs[0]], [t_hacc[tb]])
            yield
        for it_, (d, tb) in enumerate(iters):
            if True:
                tsl = slice(tb * 512, (tb + 1) * 512)
                if it_ + 1 < len(iters):
                    nd, ntb = iters[it_ + 1]
                    bc_load(2 * ((it_ + 1) % 2), nd, h, ntb)
                bn_, bd_ = (3, 4) if it_ % 2 == 0 else (5, 6)
                NB, tNB = bcs[2 * (it_ % 2)], t_bc[2 * (it_ % 2)]
                FL, tFL = bcs[2 * (it_ % 2) + 1], t_bc[2 * (it_ % 2) + 1]
                scs = list(range(0, 4 * tb + 4)) if d == 0 else list(range(15, 4 * tb - 1, -1))
                tiles = []
                for sc in scs:
                    r = sc - 4 * tb
                    diag = 0 <= r <= 3
                    if d == 0:
                        c0, c1 = (r * 128 if diag else 0), 512
                    else:
                        c0, c1 = 0, ((r + 1) * 128 if diag else 512)
                    tiles.append((sc, r, diag, c0, c1))

                def emit_st(tile):
                    sc, r, diag, c0, c1 = tile
                    sbk = rbank()
                    B.mm(banks[sbk][:, c0:c1], kT[:, sc * 128:(sc + 1) * 128],
                         qT[:, tb * 512 + c0: tb * 512 + c1], True, True, [t_k, t_q], [bk[sbk]])
                    return sbk

                sbanks = {}

                def stage_ab(n_):
                    sc, r, diag, c0, c1 = tiles[n_]
                    sbanks[n_] = emit_st(tiles[n_])
                    hcs = h * 16 + sc
                    ucv = hcap[UCOL[d]][:, hcs:hcs + 1]
                    ds = n_ % 2
                    Dt = tmpA[:, ds, :]
                    if diag:
                        if d == 0:
                            t0, t1, r0, r1 = c0, c0 + 128, c0 + 128, c1
                        else:
                            t0, t1, r0, r1 = c1 - 128, c1, c0, c1 - 128
                        B.stt(tmpB[:, ds, 0:128], NB[:, t0:t1], ucv, tri[d], ALU.add, ALU.min,
                              [tNB, t_hc[UCOL[d]], t_c2], [t_fin[ds]])
                        B.act(Dt[:, t0:t1], tmpB[:, ds, 0:128], AF.Exp, [t_fin[ds]], [t_sil[ds]])
                        if r1 > r0:
                            B.act(Dt[:, r0:r1], NB[:, r0:r1], AF.Exp,
                                  [tNB, t_hc[UCOL[d]]], [t_sil[ds]], bias=ucv)
                    else:
                        B.act(Dt[:, c0:c1], NB[:, c0:c1], AF.Exp, [tNB, t_hc[UCOL[d]]],
                              [t_sil[ds]], bias=ucv)

                stage_ab(0)
                for n_, (sc, r, diag, c0, c1) in enumerate(tiles):
                    if n_ + 1 < len(tiles):
                        stage_ab(n_ + 1)
                    sbk = sbanks[n_]
                    ds = n_ % 2
                    Dt = tmpA[:, ds, :]
                    Pt = cb[ds]
                    B.tt("dve", Pt[:, c0:c1], banks[sbk][:, c0:c1], Dt[:, c0:c1], ALU.mult,
                         [bk[sbk], t_sil[ds]], [t_cb[ds]])
                    first = n_ == 0
                    last = n_ == len(tiles) - 1
                    B.mm(banks[bn_][:, c0:c1], v_tok[:, sc * 128:(sc + 1) * 128], Pt[:, c0:c1],
                         first, last, [t_vtok, t_cb[ds]], [bk[bn_]])
                    B.mm(banks[bd_][:, c0:c1], onesb[:], Pt[:, c0:c1], first, last,
                         [t_const, t_cb[ds]], [bk[bd_]])
                    if n_ >= 1:
                        step_pending()
                while pend["g"] is not None:
                    step_pending()
                if it_ >= 1 and it_ + 1 < len(iters):
                    nd, ntb = iters[it_ + 1]
                    bc_load(2 * ((it_ + 1) % 2) + 1, 2 + nd, h, ntb)
                pend["g"] = epilogue_gen(d, tb, tsl, bn_, bd_, FL, tFL)
        while pend["g"] is not None:
            step_pending()
        pend_on["f"] = lambda: out_norm(h, 0)

    def gt(d, i):
        return 8 + 6 * d + i

    def gdn_gates():
        P.add("pool", lambda e: e.memset(hcap[ZER], 0.0), [], [t_hc[ZER]])
        gate_project(256, [0, 1, 2, 3], [6, None, 7, None])
        for d in range(2):
            Ga, Gb = 2 * d, 2 * d + 1
            T1, SPt, L, X, BETA = 4, 5, 6, 7, 20
            GAM = gt(d, 0)
            softplus_acc(SPt, Ga, 1.0, T1, L, X)
            hts(X, SPt, nega[:, d:d + 1], None, ALU.mult, None, extra=[t_c2])
            if d == 0:
                scan(GAM, X, ALU.add)
                gl = 127
            else:
                scan(L, X, ALU.add)
                hts(T1, L, -1.0, hcap[L][:, 127:128], ALU.mult, ALU.add)
                htt(GAM, T1, X, ALU.add)
                gl = 0
            hact(BETA, Gb, AF.Sigmoid)
            B.dma(bc_d[4 + d], hcap[GAM], [t_hc[GAM]], [t_bcd[4 + d]], eng="act")
            hc_transpose_to(gt(d, 1), GAM, 0)
            hts(T1, BETA, -1.0, None, ALU.mult, None)
            hc_transpose_to(gt(d, 2), T1, 1)
            hc_transpose_to(gt(d, 3), BETA, 2)
            hact(SPt, GAM, AF.Exp)
            htt(T1, BETA, SPt, ALU.mult)
            hc_transpose_to(gt(d, 4), T1, 3)
            B.act(hcap[X], hcap[GAM], AF.Exp, [t_hc[GAM]], [t_hc[X]],
                  bias=hcap[GAM][:, gl:gl + 1], scale=-1.0)
            hc_transpose_to(gt(d, 5), X, 0)

    def conv_proj(wsl, wt, ch, kind):
        raw = hacc
        acc = msc[:, 7168:7168 + 2048]

        def tacc(tb):
            return [t_cb[2 * tb], t_cb[2 * tb + 1]]

        def proj(tb):
            tsl = slice(tb * 512, (tb + 1) * 512)
            bi = rbank()
            for kc in range(KC):
                B.mm(banks[bi][:], wsl[:, kc * 128:(kc + 1) * 128], hT[:, kc, tsl],
                     kc == 0, kc == KC - 1, [wt, t_h[tb]], [bk[bi]])
            B.cp("act", raw[:, tsl], banks[bi][:], [bk[bi]], [t_hacc[tb]])

        def conv(tb):
            lo, hi = tb * 512, (tb + 1) * 512
            B.ts("dve", acc[:, lo:hi], raw[:, lo:hi], cw[:, ch * 5 + 2: ch * 5 + 3], None,
                 ALU.mult, None, [t_hacc[tb], t_c2], tacc(tb))
            for j in (0, 1, 3, 4):
                dd = j - 2
                o0 = max(lo, -dd)
                o1 = min(hi, S - dd)
                rd = [t_hacc[tb], t_c2] + tacc(tb)
                if dd < 0 and tb > 0:
                    rd.append(t_hacc[tb - 1])
                if dd > 0 and tb < NTB - 1:
                    rd.append(t_hacc[tb + 1])
                B.stt(acc[:, o0:o1], raw[:, o0 + dd:o1 + dd], cw[:, ch * 5 + j: ch * 5 + j + 1],
                      acc[:, o0:o1], ALU.mult, ALU.add, rd, tacc(tb))

        def post(tb):
            tsl = slice(tb * 512, (tb + 1) * 512)
            sl = tb % 2
            if kind == "v":
                B.act(cb[8 + tb][:, :], acc[:, tsl], AF.Silu, tacc(tb), [t_cb[8 + tb]])
                return
            B.act(tmpA[:, sl, :], acc[:, tsl], AF.Silu, tacc(tb), [t_sil[sl]])
            B.act(sqb[:, sl, :], tmpA[:, sl, :], AF.Square, [t_sil[sl]], [t_sqs[sl]])
            bi = 6 + sl
            B.mm(banks[bi][:], onesb[:], sqb[:, sl, :], True, True, [t_sqs[sl], t_const], [bk[bi]])
            B.act(nrmA[:, sl, :], banks[bi][:], AF.Ln, [bk[bi], t_const], [t_rs[sl]],
                  bias=epsc[:, 0:1], scale=1.0)
            B.act(rstd[:, sl, :], nrmA[:, sl, :], AF.Exp, [t_rs[sl]], [t_rs[sl]], scale=-0.5)
            if kind == "q":
                B.stt(qT[:, tsl], tmpA[:, sl, :], 128.0 ** -0.5, rstd[:, sl, :], ALU.mult, ALU.mult,
                      [t_sil[sl], t_rs[sl]], [t_q])
            else:
                B.tt("dve", kT[:, tsl], tmpA[:, sl, :], rstd[:, sl, :], ALU.mult,
                     [t_sil[sl], t_rs[sl]], [t_k])

        proj(0)
        proj(1)
        conv(0)
        post(0)
        proj(2)
        conv(1)
        post(1)
        proj(3)
        conv(2)
        post(2)
        conv(3)
        post(3)

    def to_token_major(dst, t_dst, srcf, src_toks):
        pb = banks[5][:].bitcast(BF16)
        for tc in range(16):
            q4 = tc % 4
            B.tr(pb[:, q4 * 128:(q4 + 1) * 128], srcf(tc), identb[:], src_toks + [t_const], [bk[5]])
            if q4 == 3:
                B.cp("act", dst[:, (tc - 3) * 128:(tc + 1) * 128], pb[:, 0:512], [bk[5]], [t_dst])

    def gdn_head(h):
        w, wt = w_next()
        conv_proj(w[:, 0:1024], wt, h, "q")
        conv_proj(w[:, 1024:2048], wt, 8 + h, "k")
        to_token_major(k_tok, t_ktok, lambda tc: kT[:, tc * 128:(tc + 1) * 128], [t_k])
        w, wt = w_next()
        conv_proj(w[:, 0:1024], wt, 16 + h, "v")
        to_token_major(v_tok, t_vtok,
                       lambda tc: cb[8 + tc // 4][:, (tc % 4) * 128:(tc % 4 + 1) * 128],
                       t_cb[8:12])
        proj_fm(w[:, 1024:2048], wt, lambda tb, tsl, bi: B.act(
            gateT[:, tsl], banks[bi][:], AF.Silu, [bk[bi]], [t_gate]))

        def v4(ap):
            return ap.rearrange("p (r t) -> p r t", r=4)

        for tb in range(NTB):
            B.memset("pool", hacc[:, tb * 512:(tb + 1) * 512], 0.0, [], [t_hacc[tb]])

        def chain(d, R):
            GAM, GAMc, NEGBc, BETAc, BEGc, KDc = [gt(d, i) for i in range(6)]
            P0, P1, P2, P3 = R["banks"]
            cbs, tcs = R["cb"], R["tcb"]
            N0, Noff, Zt, Db, Tb, attnT, Ru, Rw, kd_, nWt, qdT = cbs[0:11]
            (tN0, tNoff, tZt, tDb, tTb, tattn, tRu, tRw, tkd, tnW, tqd) = tcs[0:11]
            E, tE = R["E"]
            XA, tXA = R["XA"]
            XT, tXT = R["XT"]
            EG, tEG = R["EG"]
            S_, tS_ = R["S"]
            Sb_, tSb_ = R["Sb"]
            vn_, tvn_ = R["vnb"]
            eg_, teg_ = R["egl"]
            B.memset("pool", S_, 0.0, [], [tS_])
            B.memset("pool", Sb_, 0.0, [], [tSb_])
            tbs = range(NTB) if d == 0 else range(NTB - 1, -1, -1)
            i4 = identb4[:].rearrange("p a b -> p (a b)")

            def mk(dd, lev):
                return gmaskb[:, dd * 7 + lev, :].unsqueeze(1).to_broadcast([128, 4, 128])

            tbl = list(tbs)
            cix = R["banks"][0] // 4
            bc_load(2 * cix, 4 + d, h, tbl[0])
            for k_, tb in enumerate(tbl):
                tsl = slice(tb * 512, (tb + 1) * 512)
                hc0 = h * 16 + tb * 4
                if k_ + 1 < len(tbl):
                    bc_load(2 * cix + (k_ + 1) % 2, 4 + d, h, tbl[k_ + 1])
                GB, tGB = bcs[2 * cix + k_ % 2], t_bc[2 * cix + k_ % 2]

                def colb(tile):
                    return hcap[tile][:, hc0:hc0 + 4].unsqueeze(2).to_broadcast([128, 4, 128])

                for r in range(4):
                    csl = slice(tb * 512 + r * 128, tb * 512 + (r + 1) * 128)
                    B.mm(banks[P1][:, r * 128:(r + 1) * 128], kT[:, csl], kT[:, csl], True, True,
                         [t_k], [bk[P1]])
                B.tt("dve", v4(E), v4(GB), colb(GAMc), ALU.subtract,
                     [tGB, t_hc[GAMc]], [tE])
                B.tt("dve", v4(XA), v4(E), bigA[d].unsqueeze(1).to_broadcast([128, 4, 128]),
                     ALU.max, [tE, t_c2], [tXA])
                B.act(XA, XA, AF.Exp, [tXA], [tXA], scale=-1.0)
                B.tt("pool", v4(XA), v4(XA), colb(NEGBc), ALU.mult, [tXA, t_hc[NEGBc]], [tXA])
                B.tt("dve", N0, banks[P1][:], XA, ALU.mult, [bk[P1], tXA], [tN0])
                yield
                for r in range(4):
                    csl = slice(tb * 512 + r * 128, tb * 512 + (r + 1) * 128)
                    B.mm(banks[P1][:, r * 128:(r + 1) * 128], kT[:, csl], qT[:, csl], True, True,
                         [t_k, t_q], [bk[P1]])
                B.tt("dve", v4(XT), v4(E), tri[d].unsqueeze(1).to_broadcast([128, 4, 128]),
                     ALU.min, [tE, t_c2], [tXT])
                B.act(XT, XT, AF.Exp, [tXT], [tXT])
                B.tt("dve", attnT, banks[P1][:], XT, ALU.mult, [bk[P1], tXT], [tattn])
                B.act(EG, GB, AF.Exp, [tGB], [tEG])
                B.tt("pool", qdT, qT[:, tsl], EG, ALU.mult, [t_q, tEG], [tqd])
                gl = 127 if d == 0 else 0
                B.act(eg_, v4(GB)[:, :, gl], AF.Exp, [tGB], [teg_])
                yield
                pb1 = banks[P1][:].bitcast(BF16)
                for r in range(4):
                    B.tr(pb1[:, r * 128:(r + 1) * 128], N0[:, r * 128:(r + 1) * 128], identb[:],
                         [tN0, t_const], [bk[P1]])
                B.cp("act", Zt, pb1[:, 0:512], [bk[P1]], [tZt])
                B.tt("pool", v4(Zt), v4(Zt), mk(1 - d, 0), ALU.mult, [tZt, t_c2], [tZt])
                B.tt("dve", Tb, Zt, i4, ALU.add, [tZt, t_c2], [tTb])
                B.mm(banks[P2][:], identb[:], Tb, True, True, [t_const, tTb], [bk[P2]], skip=True)
                B.tt("pool", v4(Noff), v4(N0), mk(d, 0), ALU.mult, [tN0, t_c2], [tNoff])
                B.tt("dve", Db, Noff, i4, ALU.add, [tNoff, t_c2], [tDb])
                B.mm(banks[P1][:], identb[:], Db, True, True, [t_const, tDb], [bk[P1]], skip=True)
                yield
                for lev in range(1, 7):
                    for r in range(4):
                        rs_ = slice(r * 128, (r + 1) * 128)
                        B.mm(banks[P0][:, rs_], N0[:, rs_], Tb[:, rs_], True, True,
                             [tN0, tTb], [bk[P0]])
                    B.tt("dve", v4(Zt), v4(banks[P0][:]), mk(1 - d, lev), ALU.mult,
                         [bk[P0], t_c2], [tZt])
                    yield
                    for r in range(4):
                        rs_ = slice(r * 128, (r + 1) * 128)
                        B.mm(banks[P2][:, rs_], Db[:, rs_], Zt[:, rs_], False, True,
                             [tDb, tZt], [bk[P2]], skip=True)
                    if lev < 6:
                        for r in range(4):
                            rs_ = slice(r * 128, (r + 1) * 128)
                            B.mm(banks[P1][:, rs_], Zt[:, rs_], Db[:, rs_], False, True,
                                 [tDb, tZt], [bk[P1]], skip=True)
                    B.cp("act", Tb, banks[P2][:], [bk[P2]], [tTb])
                    if lev < 6:
                        B.cp("dve", Db, banks[P1][:], [bk[P1]], [tDb])
                    yield
                B.tt("pool", v4(Ru), v4(v_tok[:, tsl]), colb(BETAc), ALU.mult,
                     [t_vtok, t_hc[BETAc]], [tRu])
                B.tt("pool", v4(Rw), v4(k_tok[:, tsl]), colb(BEGc), ALU.mult,
                     [t_ktok, t_hc[BEGc]], [tRw])
                B.tt("pool", v4(kd_), v4(k_tok[:, tsl]), colb(KDc), ALU.mult,
                     [t_ktok, t_hc[KDc]], [tkd])
                for r in range(4):
                    rs_ = slice(r * 128, (r + 1) * 128)
                    B.mm(banks[P0][:, rs_], Rw[:, rs_], Tb[:, rs_], True, True, [tRw, tTb],
                         [bk[P0]])
                B.act(nWt, banks[P0][:], AF.Identity, [bk[P0]], [tnW], scale=-1.0)
                yield
                rr = range(4) if d == 0 else range(3, -1, -1)
                for r in rr:
                    rs_ = slice(r * 128, (r + 1) * 128)
                    B.mm(banks[P1][:, 0:128], Tb[:, rs_], Ru[:, rs_], True, False, [tTb, tRu],
                         [bk[P1]])
                    B.mm(banks[P1][:, 0:128], nWt[:, rs_], Sb_, False, True, [tnW, tSb_], [bk[P1]])
                    B.cp("act", vn_, banks[P1][:, 0:128], [bk[P1]], [tvn_])
                    yield
                    B.mm(banks[P3][:, rs_], Sb_, qdT[:, rs_], True, False, [tSb_, tqd], [bk[P3]])
                    B.mm(banks[P3][:, rs_], vn_, attnT[:, rs_], False, True, [tvn_, tattn],
                         [bk[P3]])
                    B.mm(banks[P1][:, 128:256], kd_[:, rs_], vn_, True, True, [tkd, tvn_],
                         [bk[P1]])
                    B.stt(S_, S_, eg_[:, r:r + 1], banks[P1][:, 128:256], ALU.mult, ALU.add,
                          [tS_, teg_, bk[P1]], [tS_])
                    B.cp("act", Sb_, S_, [tS_], [tSb_])
                    yield
                B.tt("dve", hacc[:, tsl], banks[P3][:], hacc[:, tsl], ALU.add,
                     [bk[P3], t_hacc[tb]], [t_hacc[tb]])
                yield

        gens = [chain(0, chainR[0]), chain(1, chainR[1])]
        while gens:
            for g in list(gens):
                try:
                    next(g)
                except StopIteration:
                    gens.remove(g)
        out_norm(h, 1)

    def branch_phase():
        inherit(y_flat + t_xm, msc_toks)
        for m in range(KC):
            w, wt = w_next()
            for tb in range(NTB):
                tsl = slice(tb * 512, (tb + 1) * 512)
                for kc in range(KC):
                    B.mm(banks[0][:], w[:, kc * 128:(kc + 1) * 128], hT[:, kc, tsl], kc == 0,
                         kc == KC - 1, [wt, t_h[tb]], [bk[0]])
                for kh in range(8):
                    B.mm(banks[1][:], w[:, 1024 + kh * 128:1024 + (kh + 1) * 128],
                         hXT[:, kh * S + tb * 512: kh * S + (tb + 1) * 512], kh == 0, kh == 7,
                         [wt, t_hx[kh][tb]], [bk[1]])
                sl = tb % 2
                B.act(tmpA[:, sl, :], banks[0][:], AF.Sigmoid, [bk[0]], [t_sil[sl]])
                B.tt("dve", yT[:, m * S + tb * 512: m * S + (tb + 1) * 512], banks[1][:],
                     tmpA[:, sl, :], ALU.mult, [bk[1], t_sil[sl]], [t_y[m][tb]])
        for mm_ in range(4):
            w, wt = w_next()
            for half in range(2):
                m = 2 * mm_ + half
                sl = m % 2
                B.dma(xm[:, sl, :], xsp_d[:, m, :], [t_xsp[m]], [t_xm[sl]], eng="act")
                for tb in range(NTB):
                    tsl = slice(tb * 512, (tb + 1) * 512)
                    bi = 2 + (tb % 2)
                    for kc in range(KC):
                        B.mm(banks[bi][:], w[:, half * 1024 + kc * 128: half * 1024 + (kc + 1) * 128],
                             yT[:, kc * S + tb * 512: kc * S + (tb + 1) * 512], kc == 0,
                             kc == KC - 1, [wt, t_y[kc][tb]], [bk[bi]])
                    B.stt(xm[:, sl, tsl], banks[bi][:], gcol[:, KC + m:KC + m + 1], xm[:, sl, tsl],
                          ALU.mult, ALU.add, [bk[bi], t_mod, t_xm[sl]], [t_xm[sl]])
                B.dma(xsp_d[:, m, :], xm[:, sl, :], [t_xm[sl]], [t_xsp[m]], eng="act")
        inherit(msc_toks, y_flat + t_xm)

    t_xsp = [Tok() for _ in range(KC)]

    def mixer(which):
        nonlocal arena_toks
        for tb in range(NTB):
            norm_block(1, tb)
        x_flat = [t for row in t_x for t in row]
        for kc in range(KC):
            B.dma(xsp_d[:, kc, :], xT[:, kc, :], t_x[kc], [t_xsp[kc]], eng="sp")
        inherit(msc_toks + y_flat + t_xm, x_flat)
        inherit(hx_flat + t_hc, arena_toks)
        w, wt = w_next()
        B.cp("pool", wgb[:], w[:, 0:512], [wt], [t_c2])
        if which in ("all", "mlstm"):
            mlstm_gates()
            for h in range(8):
                mlstm_head(h)
            pend_on["f"]()
            pend_on["f"] = None
        else:
            for h in range(8):
                w_next()
                w_next()
            for tk in hx_flat:
                pass
            P.add("pool", lambda e: e.memset(hXT, 0.0), [], hx_flat)
        branch_phase()
        if which in ("all", "gdn"):
            gdn_gates()
            for h in range(8):
                gdn_head(h)
        else:
            for h in range(8):
                w_next()
                w_next()
            P.add("pool", lambda e: e.memset(hXT, 0.0), hx_flat, hx_flat)
        branch_phase()
        inherit(x_flat, msc_toks + y_flat + t_xm)
        for kc in range(KC):
            B.dma(xT[:, kc, :], xsp_d[:, kc, :], [t_xsp[kc]], t_x[kc], eng="sp")
        arena_toks = hx_flat + t_hc

    if debug == "nomix":
        ffn(0)
        ffn(2)
        for tb in range(NTB):
            norm_block(3, tb, final=True)
    elif debug in ("mlstm", "gdn", "mixonly"):
        ffn(0)
        mixer({"mlstm": "mlstm", "gdn": "gdn", "mixonly": "all"}[debug])
        for kc in range(KC):
            t_o = Tok()
            out_toks.append(t_o)
            B.dma(out_d[:, kc, :], xT[:, kc, :], t_x[kc], [t_o])
    else:
        ffn(0)
        mixer("all")
        ffn(2)
        for tb in range(NTB):
            norm_block(3, tb, final=True)
    P.add("sp", None, out_toks, [])

    pool_sz = {"pe": 1, "act": 8, "dve": 1, "pool": 8, "sp": 16}
    P.finalize(pool_sz)
    engsem = {e: B.es.enter_context(nc.semaphore("sem_" + e)) for e in ENGS}
    dmasem = {e: [B.es.enter_context(nc.semaphore("dsem_%s_%d" % (e, i)))
                  for i in range(pool_sz[e])] for e in ("act", "pool", "sp")}
    with nc.Block() as block:
        @block.tensor
        def _(e):
            P.emit("pe", e, engsem, dmasem)

        @block.scalar
        def _(e):
            P.emit("act", e, engsem, dmasem)

        @block.vector
        def _(e):
            P.emit("dve", e, engsem, dmasem)

        @block.gpsimd
        def _(e):
            P.emit("pool", e, engsem, dmasem)

        @block.sync
        def _(e):
            P.emit("sp", e, engsem, dmasem)
    B.es.close()
    return nc


def _col(v):
    return np.ascontiguousarray(v.reshape(-1, 128).T)


def _chunk(w, c0):
    return w[:, c0:c0 + 128].reshape(KC, 128, 128).transpose(1, 0, 2).reshape(128, 1024)


def host_layout(inp):
    shared = {}
    w_ada = inp["w_ada"][0]
    shared["wada"] = np.ascontiguousarray(
        w_ada.reshape(KC, 128, 18, 512).transpose(2, 1, 0, 3).reshape(18, 128, KC * 512))
    shared["bada"] = np.ascontiguousarray(inp["b_ada"][0].reshape(1, 9216))
    shared["normw"] = np.ascontiguousarray(np.concatenate(
        [_col(inp["norm_ffn1"][0]), _col(inp["norm_mix"][0]), _col(inp["norm_ffn2"][0]),
         _col(inp["norm_final"])], axis=1))

    def ffn_in(w):
        return np.ascontiguousarray(
            w.reshape(KC, 128, 2, NJ, 128).transpose(3, 1, 2, 0, 4).reshape(NJ, 128, 2048))

    def ffn_out(w):
        return np.ascontiguousarray(
            w.reshape(2, GJ, 128, KC, 128).transpose(0, 3, 2, 1, 4).reshape(16, 128, GJ * 128))

    shared["wf1i"] = ffn_in(inp["w_ffn1_in"][0])
    shared["wf1o"] = ffn_out(inp["w_ffn1_out"][0])
    shared["wf2i"] = ffn_in(inp["w_ffn2_in"][0])
    shared["wf2o"] = ffn_out(inp["w_ffn2_out"][0])
    shared["ident"] = np.eye(128, dtype=np.float32)

    win = inp["w_in"][0]
    wbm = inp["w_branch_mlstm"][0]
    wbg = inp["w_branch_gdn"][0]
    wo = inp["w_out"][0]
    O_MQ, O_MK, O_MV, O_MO, O_MG = 0, 1024, 2048, 3072, 4096
    O_GQ, O_GK, O_GV, O_GZ, O_GG = 4128, 5152, 6176, 7200, 8224
    O_MM, O_MGG = 8256, 9280
    units = []
    for h in range(8):
        units.append(np.concatenate([_chunk(win, O_MQ + h * 128), _chunk(win, O_MK + h * 128)], 1))
    for h in range(8):
        units.append(np.concatenate([_chunk(win, O_MV + h * 128), _chunk(win, O_MO + h * 128)], 1))
    for m in range(8):
        units.append(np.concatenate([_chunk(win, O_MM + m * 128), _chunk(wbm, m * 128)], 1))
    for mm_ in range(4):
        units.append(np.concatenate([_chunk(wo, 2 * mm_ * 128), _chunk(wo, (2 * mm_ + 1) * 128)], 1))
    for h in range(8):
        units.append(np.concatenate([_chunk(win, O_GQ + h * 128), _chunk(win, O_GK + h * 128)], 1))
    for h in range(8):
        units.append(np.concatenate([_chunk(win, O_GV + h * 128), _chunk(win, O_GZ + h * 128)], 1))
    for m in range(8):
        units.append(np.concatenate([_chunk(win, O_MGG + m * 128), _chunk(wbg, m * 128)], 1))
    shared["winu"] = np.ascontiguousarray(np.stack(units, 0))

    def gchunk(c0):
        return win[:, c0:c0 + 32].reshape(KC, 128, 32).transpose(1, 0, 2).reshape(128, 256)

    shared["wgates"] = np.ascontiguousarray(np.concatenate([gchunk(O_MG), gchunk(O_GG)], 1))
    hidx = np.arange(128) // 16
    gbias = np.zeros((128, 8), np.float32)
    mgb = inp["mlstm_gate_bias"][0]
    for g in range(4):
        gbias[:, g] = mgb[g][hidx]
    gbias[:, 4] = inp["gdn_a_log"][0][0][hidx]
    gbias[:, 5] = inp["gdn_a_log"][0][1][hidx]
    gbias[:, 6] = inp["gdn_dt_bias"][0][0][hidx]
    gbias[:, 7] = inp["gdn_dt_bias"][0][1][hidx]
    shared["gbias"] = gbias
    cwv = inp["gdn_conv_w"][0]
    shared["convw"] = np.ascontiguousarray(
        cwv.reshape(5, 24, 128).transpose(2, 1, 0).reshape(128, 120))
    shared["onorm"] = np.ascontiguousarray(
        np.stack([inp["mlstm_out_norm"][0], inp["gdn_out_norm"][0]], 1))
    p = np.arange(128)[:, None]
    j = np.arange(128)[None, :]
    tri_f = np.where(p <= j, 0.0, -BIG)
    tri_b = np.where(p >= j, 0.0, -BIG)
    big_f = np.where(j < p, 0.0, BIG)
    big_b = np.where(j > p, 0.0, BIG)
    shared["cmask"] = np.ascontiguousarray(
        np.stack([tri_f, tri_b, big_f, big_b], 1).reshape(128, 512).astype(np.float32))
    hk, ck = p // 16, p % 16
    hm, cm_ = j // 16, j % 16
    lf = ((hk == hm) & (ck < cm_)).astype(np.float32)
    lb = ((hk == hm) & (ck > cm_)).astype(np.float32)
    shared["clm"] = np.ascontiguousarray(np.stack([lf, lb], 1).reshape(128, 256))
    gms = []
    for dd in range(2):
        for lev in range(7):
            bsz = 1 << lev
            same = (p // (2 * bsz)) == (j // (2 * bsz))
            if dd == 0:
                mk_ = same & (p % (2 * bsz) >= bsz) & (j % (2 * bsz) < bsz)
            else:
                mk_ = same & (p % (2 * bsz) < bsz) & (j % (2 * bsz) >= bsz)
            gms.append(mk_)
    shared["gmask"] = np.ascontiguousarray(
        np.stack(gms, 1).reshape(128, 14 * 128).astype(ml_dtypes.bfloat16))
    in_maps = []
    for b in range(NCORES):
        m = dict(shared)
        m["xT"] = np.ascontiguousarray(inp["x"][b].T.reshape(KC, 128, S).transpose(1, 0, 2))
        m["ccol"] = _col(inp["c"][b])
        in_maps.append(m)
    return in_maps


_NC_CACHE = {}
DEBUG = None


def kernel(**inputs):
    inp = {k: np.asarray(v, dtype=np.float32) for k, v in inputs.items()}
    in_maps = host_layout(inp)
    if "nc" not in _NC_CACHE:
        _NC_CACHE["nc"] = build(DEBUG)
    nc = _NC_CACHE["nc"]
    res = run_bass_kernel_spmd(nc, in_maps, core_ids=list(range(NCORES)))
    out = np.empty((NCORES, S, D), np.float32)
    for b in range(NCORES):
        oT = np.asarray(res.results[b]["outT"]).reshape(128, KC, S)
        out[b] = oT.transpose(1, 0, 2).reshape(D, S).T
    return out
```
